# Optimizing a Trainium2 kernel written in Bass

```python
import jax, jax.numpy as jnp
from jax import lax
import numpy as np

D_MODEL = 1024
BATCH = 16
SEQ = 4096
DEPTH = 4

N_META = 16
N_MIXERS = 2
N_CONV_LAYERS = (DEPTH + 1) // 2
N_GLA_LAYERS = DEPTH // 2
DN_ALPHA = (2 * DEPTH) ** 0.25
DN_BETA = (8 * DEPTH) ** -0.25
LN_EPS = 1e-5
RMS_EPS = 1e-6
CONV_CH = D_MODEL
CONV_TAPS = 31
CONV_IN = 3 * CONV_CH
GLA_HEADS = 4
GLA_DK = D_MODEL // 2
GLA_DV = D_MODEL
GLA_HK = GLA_DK // GLA_HEADS
GLA_HV = GLA_DV // GLA_HEADS
GLA_GATE_RANK = 16
GLA_TAU = 16.0
GLA_CHUNK = 64
GLA_IN = 2 * GLA_DK + 2 * GLA_DV + GLA_GATE_RANK

kernel_name = "hybrid_conformer_gla_deepnorm_meta"


def layer_norm(x, g, b):
    xf = x.astype(jnp.float32)
    mu = jnp.mean(xf, axis=-1, keepdims=True)
    xc = xf - mu
    var = jnp.mean(xc * xc, axis=-1, keepdims=True)
    y = xc * lax.rsqrt(var + LN_EPS) * g.astype(jnp.float32) + b.astype(jnp.float32)
    return y.astype(x.dtype)


def conformer_mixer(h, w_in, b_in, w_dw, b_dw, norm_g, norm_b, w_out, b_out):
    u = h @ w_in + b_in
    a, ga, z = jnp.split(u, 3, axis=-1)
    glu = a * jax.nn.sigmoid(ga)
    c = lax.conv_general_dilated(
        glu, w_dw[:, None, :].astype(glu.dtype), window_strides=(1,),
        padding=[(CONV_TAPS - 1, 0)], dimension_numbers=('NWC', 'WIO', 'NWC'),
        feature_group_count=CONV_CH) + b_dw
    c = jax.nn.silu(layer_norm(c, norm_g, norm_b))
    return (c * jax.nn.silu(z)) @ w_out + b_out


def _gla_chunk_step(S, inp):
    qc, kc, vc, gc = inp
    bc = jnp.cumsum(gc, axis=1)
    o_inter = jnp.einsum('bthd,bhde->bthe', qc * jnp.exp(bc), S)
    causal = jnp.tril(jnp.ones((GLA_CHUNK, GLA_CHUNK), dtype=bool))
    diff = bc[:, :, None] - bc[:, None, :]
    decay = jnp.exp(jnp.where(causal[None, :, :, None, None], diff, -jnp.inf))
    att = jnp.einsum('bthd,bshd,btshd->bhts', qc, kc, decay)
    o_intra = jnp.einsum('bhts,bshe->bthe', att, vc)
    b_last = bc[:, -1]
    S_new = jnp.exp(b_last)[..., None] * S + jnp.einsum(
        'bshd,bshe->bhde', kc * jnp.exp(b_last[:, None] - bc), vc)
    return S_new, o_inter + o_intra


def gla_mixer(h, w_in, w_a2, b_a, norm_g, w_out):
    B, L, _ = h.shape
    u = h @ w_in
    q, k, v, r, a1 = jnp.split(
        u, [GLA_DK, 2 * GLA_DK, 2 * GLA_DK + GLA_DV, 2 * GLA_DK + 2 * GLA_DV], axis=-1)
    glog = jax.nn.log_sigmoid((a1 @ w_a2 + b_a).astype(jnp.float32)) / GLA_TAU
    pad = GLA_CHUNK - N_META
    n_chunks = (L + pad) // GLA_CHUNK

    def prep(t, hd):
        t = jnp.pad(t.astype(jnp.float32), ((0, 0), (pad, 0), (0, 0)))
        return t.reshape(B, n_chunks, GLA_CHUNK, GLA_HEADS, hd).transpose(1, 0, 2, 3, 4)

    qs = prep(q, GLA_HK) * (GLA_HK ** -0.5)
    ks = prep(k, GLA_HK)
    vs = prep(v, GLA_HV)
    gs = prep(glog, GLA_HK)
    S0 = jnp.zeros((B, GLA_HEADS, GLA_HK, GLA_HV), jnp.float32)
    _, o = lax.scan(_gla_chunk_step, S0, (qs, ks, vs, gs))
    o = o.transpose(1, 0, 2, 3, 4).reshape(B, n_chunks * GLA_CHUNK, GLA_HEADS, GLA_HV)[:, pad:]
    o = o * lax.rsqrt(jnp.mean(o * o, axis=-1, keepdims=True) + RMS_EPS) * norm_g.astype(jnp.float32)
    o = o.reshape(B, L, GLA_DV).astype(h.dtype)
    return (o * jax.nn.silu(r)) @ w_out


def setup_inputs(seed: int = 0) -> dict:
    key = jax.random.key(seed)
    ks = jax.random.split(key, 20)
    nrm = jax.random.normal
    f32 = jnp.float32
    Lc, Lg = N_CONV_LAYERS, N_GLA_LAYERS
    return {
        "x": nrm(ks[0], (BATCH, SEQ, D_MODEL), f32),
        "meta": nrm(ks[1], (N_META, D_MODEL), f32),
        "conv_w_in": nrm(ks[2], (Lc, D_MODEL, CONV_IN), f32) * D_MODEL ** -0.5,
        "conv_b_in": nrm(ks[3], (Lc, CONV_IN), f32) * 0.02,
        "conv_w_dw": nrm(ks[4], (Lc, CONV_TAPS, CONV_CH), f32) * CONV_TAPS ** -0.5,
        "conv_b_dw": nrm(ks[5], (Lc, CONV_CH), f32) * 0.02,
        "conv_norm_g": 1.0 + 0.02 * nrm(ks[6], (Lc, CONV_CH), f32),
        "conv_norm_b": nrm(ks[7], (Lc, CONV_CH), f32) * 0.02,
        "conv_w_out": nrm(ks[8], (Lc, CONV_CH, D_MODEL), f32) * (CONV_CH ** -0.5 * DN_BETA),
        "conv_b_out": nrm(ks[9], (Lc, D_MODEL), f32) * 0.02,
        "gla_w_in": nrm(ks[10], (Lg, D_MODEL, GLA_IN), f32) * D_MODEL ** -0.5,
        "gla_w_a2": nrm(ks[11], (Lg, GLA_GATE_RANK, GLA_DK), f32) * GLA_GATE_RANK ** -0.5,
        "gla_b_a": nrm(ks[12], (Lg, GLA_DK), f32) * 0.02,
        "gla_norm_g": 1.0 + 0.02 * nrm(ks[13], (Lg, GLA_HV), f32),
        "gla_w_out": nrm(ks[14], (Lg, GLA_DV, D_MODEL), f32) * (GLA_DV ** -0.5 * DN_BETA),
        "post_ln_g": 1.0 + 0.02 * nrm(ks[15], (DEPTH, D_MODEL), f32),
        "post_ln_b": nrm(ks[16], (DEPTH, D_MODEL), f32) * 0.02,
    }


def reference(x, meta, conv_w_in, conv_b_in, conv_w_dw, conv_b_dw, conv_norm_g, conv_norm_b,
              conv_w_out, conv_b_out, gla_w_in, gla_w_a2, gla_b_a, gla_norm_g, gla_w_out,
              post_ln_g, post_ln_b):
    B = x.shape[0]
    h = jnp.concatenate(
        [jnp.broadcast_to(meta[None].astype(x.dtype), (B, N_META, D_MODEL)), x], axis=1)
    for i in range(DEPTH):
        j = i // N_MIXERS
        if i % N_MIXERS == 0:
            y = conformer_mixer(h, conv_w_in[j], conv_b_in[j], conv_w_dw[j], conv_b_dw[j],
                                conv_norm_g[j], conv_norm_b[j], conv_w_out[j], conv_b_out[j])
        else:
            y = gla_mixer(h, gla_w_in[j], gla_w_a2[j], gla_b_a[j], gla_norm_g[j], gla_w_out[j])
        h = layer_norm(DN_ALPHA * h + y, post_ln_g[i], post_ln_b[i])
    return h[:, N_META:]
```

```python
import numpy as np
import concourse.bass as bass
import concourse.mybir as mybir
from concourse.bass_utils import run_bass_kernel_spmd

F32 = mybir.dt.float32
BF16 = mybir.dt.bfloat16
AF = mybir.ActivationFunctionType
ALU = mybir.AluOpType

D = 1024
KT = 8
NMETA = 16
TAPS = 31
HALO = 30
DEPTH_FULL = 4
ALPHA = (2 * DEPTH_FULL) ** 0.25
LN_EPS = 1e-5
RMS_EPS = 1e-6
TT = 512
NCORES = 8
CONV_ROWS = 37
NRING = 6

COMPUTE = ("pe", "act", "dve", "pool")


class Prog:
    def __init__(self, nc):
        self.nc = nc
        self.ops = {e: [] for e in ("pe", "act", "dve", "pool", "sp")}
        self.last_w = {}
        self.readers = {}
        self.dma_cnt = {}
        self.bar = {e: set() for e in self.ops}

    def op(self, eng, fn, reads=(), writes=(), dma_key=None):
        lst = self.ops[eng]
        idx = len(lst)
        deps = {}

        def add(d, kind):
            if d is None:
                return
            if d in deps and deps[d] == "raw":
                return
            deps[d] = kind

        for r in reads:
            add(self.last_w.get(r), "raw")
        for r in writes:
            add(self.last_w.get(r), "waw")
            for rd in self.readers.get(r, ()):
                add(rd, "war")
        for d in self.bar[eng]:
            add(d, "raw")
        self.bar[eng] = set()
        rec = dict(fn=fn, deps=deps, dma_key=dma_key, dma_n=None, signal=False, ordinal=None)
        if dma_key is not None:
            n = self.dma_cnt.get(dma_key, 0) + 1
            self.dma_cnt[dma_key] = n
            rec["dma_n"] = n
        lst.append(rec)
        me = (eng, idx)
        for r in reads:
            s = self.readers.setdefault(r, set())
            if dma_key is None:
                for o in [o for o in s if o[0] == eng and self.ops[eng][o[1]]["dma_key"] is None]:
                    s.discard(o)
            s.add(me)
        for r in writes:
            self.last_w[r] = me
            self.readers[r] = set()
        return me

    def barrier(self):
        lasts = set()
        for e in COMPUTE:
            for i in range(len(self.ops[e]) - 1, -1, -1):
                if self.ops[e][i]["dma_key"] is None:
                    lasts.add((e, i))
                    break
        for e in COMPUTE:
            self.bar[e] = set(d for d in lasts if d[0] != e)

    def resolve(self):
        for eng, lst in self.ops.items():
            waited = {e: -1 for e in self.ops}
            waited_dma = {}
            for idx, rec in enumerate(lst):
                waits = []
                for (pe_, pi), kind in rec["deps"].items():
                    prod = self.ops[pe_][pi]
                    if prod["dma_key"] is not None:
                        k = prod["dma_key"]
                        if waited_dma.get(k, 0) >= prod["dma_n"]:
                            continue
                        waited_dma[k] = prod["dma_n"]
                        waits.append(("dma", k, prod["dma_n"]))
                        continue
                    if pe_ == eng:
                        if eng in ("pe", "sp"):
                            continue
                    if waited[pe_] >= pi:
                        continue
                    waited[pe_] = pi
                    prod["signal"] = True
                    waits.append(("eng", pe_, pi))
                rec["waits"] = waits
        for eng, lst in self.ops.items():
            n = 0
            for rec in lst:
                if rec["signal"]:
                    n += 1
                    rec["ordinal"] = n

    def emit(self, block_ctx_factory, sems, dma_sems, final_waits):
        self.resolve()
        nc = self.nc
        P = self

        def run(eng_name):
            def body(e):
                for rec in P.ops[eng_name]:
                    for w in rec["waits"]:
                        if w[0] == "dma":
                            e.wait_ge(dma_sems[w[1]], 16 * w[2])
                        else:
                            e.wait_ge(sems[w[1]], P.ops[w[1]][w[2]]["ordinal"])
                    ins = rec["fn"](e)
                    if rec["dma_key"] is not None:
                        ins.then_inc(dma_sems[rec["dma_key"]], 16)
                    elif rec["signal"]:
                        ins.then_inc(sems[eng_name], 1)
                if eng_name == "sp":
                    for k in final_waits:
                        if P.dma_cnt.get(k, 0):
                            e.wait_ge(dma_sems[k], 16 * P.dma_cnt[k])
            return body

        with nc.Block() as block:
            block.tensor(run("pe"))
            block.scalar(run("act"))
            block.vector(run("dve"))
            block.gpsimd(run("pool"))
            block.sync(run("sp"))


def layer_groups(depth):
    groups = []
    for l in range(depth):
        j = l // 2
        if l % 2 == 0:
            for c0 in (0, 1024, 2048, 512, 1536, 2560):
                groups.append((l, "conv_w_in", j, c0))
            for c0 in (0, 512):
                groups.append((l, "conv_w_out", j, c0))
        else:
            for c0 in (0, 512, 2048, 2560, 1024, 1536):
                groups.append((l, "gla_w_in", j, c0))
            for c0 in (0, 512):
                groups.append((l, "gla_w_out", j, c0))
    return groups


def build(nseq, seq, depth):
    assert seq % TT == 0
    nc = bass.Bass("TRN2", target_bir_lowering=False)
    n_conv = (depth + 1) // 2
    n_gla = depth // 2
    dr = {}

    def din(name, shape):
        dr[name] = nc.dram_tensor(name, list(shape), F32, kind="ExternalInput").ap()
        return dr[name]

    x = din("x", (nseq, seq, D))
    meta = din("meta", (NMETA, D))
    din("conv_w_in", (2, D, 3072)); din("conv_b_in", (2, 3072)); din("conv_w_dw", (2, TAPS, D))
    din("conv_b_dw", (2, D)); din("conv_norm_g", (2, D)); din("conv_norm_b", (2, D))
    din("conv_w_out", (2, D, D)); din("conv_b_out", (2, D))
    din("gla_w_in", (2, D, 3088)); din("gla_w_a2", (2, 16, 512)); din("gla_b_a", (2, 512))
    din("gla_norm_g", (2, 256)); din("gla_w_out", (2, D, D))
    din("post_ln_g", (4, D)); din("post_ln_b", (4, D))
    ident_d = din("c_ident", (128, 128))
    tri_d = din("c_tri", (128, 128))
    y = nc.dram_tensor("y", [nseq, seq, D], F32, kind="ExternalOutput").ap()

    groups = layer_groups(depth)
    NG = len(groups)
    wscr = nc.dram_tensor("wscr", [max(NG, 1), 128, KT * 512], BF16, kind="Internal").ap()

    P = Prog(nc)
    import contextlib
    es = contextlib.ExitStack()
    with es:
        def sb(name, shape, dt):
            return es.enter_context(nc.sbuf_tensor(name, list(shape), dt))

        XIN = sb("XIN", [128, 4, D], F32)
        YOUT = sb("YOUT", [128, 2, D], F32)
        X = sb("X", [128, KT, TT], F32)
        Xb = sb("Xb", [128, KT, TT], BF16)
        WR = sb("WR", [128, NRING, KT * 512], BF16)
        WSM = sb("WSM", [128, 2, KT, 16], BF16)
        WA2 = sb("WA2", [16, 2, 512], BF16)
        BAROW = sb("BAROW", [1, 2, 512], BF16)
        BOROW = sb("BOROW", [1, 2, D], BF16)
        HAL = sb("HAL", [128, 2, KT, HALO], BF16)
        S32 = sb("S32", [128, 2, 4, 256], F32)
        Sb = sb("Sb", [128, 2, 4, 256], BF16)
        PROW2 = sb("PROW2", [4, 128], F32)
        PT = sb("PT", [128, KT, 88], F32)
        GN = sb("GN", [128, 4], F32)
        ident32 = sb("ident32", [128, 128], F32)
        tri32 = sb("tri32", [128, 128], F32)
        identb = sb("identb", [128, 128], BF16)
        trib = sb("trib", [128, 128], BF16)
        ones32 = sb("ones32", [128, 128], F32)
        onesb = sb("onesb", [128, TT], BF16)
        MEAN = sb("MEAN", [128, TT], F32)
        EPS_LN = sb("EPS_LN", [128, 1], F32)
        EPS_RMS = sb("EPS_RMS", [128, 1], F32)
        MSQ = sb("MSQ", [128, TT], F32)
        RSTD = sb("RSTD", [128, TT], F32)
        T1 = sb("T1", [128, 2, TT], F32)
        SQ = sb("SQ", [128, 2, TT], BF16)
        M = sb("M", [128, KT, TT], BF16)
        ARENA = sb("ARENA", [128, 15 * 1024], F32)
        PSt = [es.enter_context(nc.psum_tensor(f"ps{i}", [128, 512], F32)) for i in range(8)]

        class Arena:
            def __init__(self):
                self.off = 0

            def reset(self):
                self.off = 0

            def take(self, n_elems, dt, shape_str=None, **kw):
                nb = n_elems * (4 if dt == F32 else 2)
                nw = (nb + 3) // 4
                nw = (nw + 7) // 8 * 8
                v = ARENA[:, self.off:self.off + nw]
                self.off += nw
                assert self.off <= 15 * 1024, self.off
                if dt != F32:
                    v = v.bitcast(dt)
                v = v[:, 0:n_elems]
                if shape_str:
                    v = v.rearrange(shape_str, **kw)
                return v

        AR = Arena()
        PROW = AR.take(D, F32)

        sem_names = ["pe", "act", "dve", "pool", "sp"]
        sems = {n: es.enter_context(nc.semaphore("s_" + n)) for n in sem_names}
        dma_keys = [("w", s) for s in range(NRING)] + [("cv", i) for i in range(8)] + \
                   [("xin",), ("yout", 0), ("yout", 1)] + [("par", i) for i in range(4)]
        dma_sems = {k: es.enter_context(nc.semaphore("d_" + "_".join(str(a) for a in k))) for k in dma_keys}

        class Banks:
            def __init__(self):
                self.free_list = list(range(6))

            def alloc(self):
                assert self.free_list, "PSUM pool exhausted"
                return self.free_list.pop(0)

            def free(self, b):
                self.free_list.append(b)

        PS = Banks()
        ST1, ST2 = 6, 7

        def ps(b):
            return PSt[b]

        def psr(b):
            return ("ps", b)

        import os
        DBG = os.environ.get("KDBG", "")
        P.op("sp", lambda e: e.dma_start(out=ident32[:], in_=ident_d), writes=["ident32"], dma_key=("par", 0))
        P.op("sp", lambda e: e.dma_start(out=tri32[:], in_=tri_d), writes=["tri32"], dma_key=("par", 1))
        P.op("dve", lambda e: e.tensor_copy(out=identb[:], in_=ident32[:]), reads=["ident32"], writes=["identb"])
        P.op("dve", lambda e: e.tensor_copy(out=trib[:], in_=tri32[:]), reads=["tri32"], writes=["trib"])
        P.op("dve", lambda e: e.memset(ones32[:], 1.0), writes=["ones32"])
        P.op("dve", lambda e: e.memset(EPS_LN[:], LN_EPS), writes=["EPS"])
        P.op("dve", lambda e: e.memset(EPS_RMS[:], RMS_EPS), writes=["EPS"])
        P.op("dve", lambda e: e.memset(onesb[:], 1.0), writes=["onesb"])
        P.op("dve", lambda e: e.memset(PROW[:], 0.0), writes=["PROW"])

        def prow_dma(dst_rows, src, key_i):
            P.op("sp", lambda e: e.dma_start(out=dst_rows, in_=src), reads=[], writes=["PROW"],
                 dma_key=("par", key_i))

        r = 0
        for j in range(2 if "noparam" not in DBG else 0):
            base = j * CONV_ROWS
            prow_dma(PROW[base:base + 3, :], dr["conv_b_in"][j].rearrange("(a n) -> a n", a=3), 2)
            prow_dma(PROW[base + 3:base + 34, :], dr["conv_w_dw"][j], 2)
            prow_dma(PROW[base + 34:base + 35, :], dr["conv_b_dw"][j:j + 1, :], 2)
            prow_dma(PROW[base + 35:base + 36, :], dr["conv_norm_g"][j:j + 1, :], 2)
            prow_dma(PROW[base + 36:base + 37, :], dr["conv_norm_b"][j:j + 1, :], 2)
        LNB = 2 * CONV_ROWS
        for l in range(4 if "noparam" not in DBG else 0):
            prow_dma(PROW[LNB + 2 * l:LNB + 2 * l + 1, :], dr["post_ln_g"][l:l + 1, :], 2)
            prow_dma(PROW[LNB + 2 * l + 1:LNB + 2 * l + 2, :], dr["post_ln_b"][l:l + 1, :], 2)
        NROWS = LNB + 8
        P.op("sp", lambda e: e.dma_start(out=PROW2[:], in_=dr["gla_norm_g"].rearrange("l (a n) -> (l a) n", a=2)),
             writes=["PROW2"], dma_key=("par", 3))
        for i in range(KT if "nopt" not in DBG else 0):
            b = PS.alloc()
            P.op("pe", lambda e, i=i, b=b: e.transpose(out=ps(b)[:, 0:NROWS], in_=PROW[0:NROWS, i * 128:(i + 1) * 128],
                                                       identity=ident32[0:NROWS, 0:NROWS]),
                 reads=["PROW", "ident32"], writes=[psr(b)])
            P.op("act", lambda e, i=i, b=b: e.copy(out=PT[:, i, 0:NROWS], in_=ps(b)[:, 0:NROWS]),
                 reads=[psr(b)], writes=["PT"])
            PS.free(b)
        b = PS.alloc()
        P.op("pe", lambda e, b=b: e.transpose(out=ps(b)[:, 0:4], in_=PROW2[0:4, :], identity=ident32[0:4, 0:4]),
             reads=["PROW2", "ident32"], writes=[psr(b)])
        P.op("act", lambda e, b=b: e.copy(out=GN[:], in_=ps(b)[:, 0:4]), reads=[psr(b)], writes=["GN"])
        PS.free(b)
        if "nosmall" in DBG:
            class _N:
                def op(self, *a, **k): pass
            P_ = P; P = _N()
        P.op("pool", lambda e: e.dma_start(out=WA2[:], in_=dr["gla_w_a2"].rearrange("l r n -> r l n")),
             writes=["WA2", ("cvslot", 0)], dma_key=("cv", 0))
        P.op("pool", lambda e: e.dma_start(out=BAROW[:], in_=dr["gla_b_a"].rearrange("(o l) n -> o l n", o=1)),
             writes=["BAROW", ("cvslot", 1)], dma_key=("cv", 1))
        P.op("pool", lambda e: e.dma_start(out=BOROW[:], in_=dr["conv_b_out"].rearrange("(o l) n -> o l n", o=1)),
             writes=["BOROW", ("cvslot", 2)], dma_key=("cv", 2))
        P.op("pool", lambda e: e.dma_start(
            out=WSM[:], in_=dr["gla_w_in"][:, :, 3072:3088].rearrange("l (kt p) n -> p l kt n", p=128)),
            writes=["WSM", ("cvslot", 3)], dma_key=("cv", 3))
        if "nosmall" in DBG:
            P = P_
        for g, (l, nm, j, c0) in enumerate(groups):
            src = dr[nm][j][:, c0:c0 + 512].rearrange("(kt p) n -> p kt n", p=128)
            dst = wscr[g].rearrange("p (kt n) -> p kt n", kt=KT)
            P.op("pool", lambda e, src=src, dst=dst: e.dma_start(out=dst, in_=src),
                 writes=[("wscr", g), ("cvslot", g % 8)], dma_key=("cv", g % 8))

        tiles = []
        for s in range(nseq):
            tiles.append((s, 0, NMETA, True))
            for t in range(seq // TT):
                tiles.append((s, t * TT, TT, False))
        wseq = []
        for _ in tiles:
            for g in range(NG):
                wseq.append(g)

        class Ring:
            def __init__(self):
                self.next_load = 0
                self.next_use = 0

            def issue(self):
                if self.next_load >= len(wseq):
                    return
                n = self.next_load
                g = wseq[n]
                slot = n % NRING
                self.next_load += 1
                P.op("sp", lambda e, g=g, slot=slot: e.dma_start(out=WR[:, slot, :], in_=wscr[g]),
                     reads=[("wscr", g)], writes=[("w", slot)], dma_key=("w", slot))

            def get(self):
                n = self.next_use
                self.next_use += 1
                assert n < self.next_load
                return n % NRING

            def release(self, slot):
                self.issue()

        RG = Ring()
        for _ in range(NRING):
            RG.issue()

        def wv(slot):
            return WR[:, slot, :].rearrange("p (kt n) -> p kt n", kt=KT)

        def load_x(tile):
            s, t0, T, is_meta = tile
            if is_meta:
                P.op("sp", lambda e: e.dma_start(out=XIN[0:NMETA, 0, :], in_=meta), writes=["XIN"], dma_key=("xin",))
            else:
                P.op("sp", lambda e, s=s, t0=t0: e.dma_start(
                    out=XIN[:, :, :], in_=x[s, t0:t0 + TT, :].rearrange("(nb p) d -> p nb d", p=128)),
                    writes=["XIN"], dma_key=("xin",))

        def blocks(T):
            return [(b, min(128, T - b * 128)) for b in range((T + 127) // 128)]

        def transpose_in(T):
            for i in range(KT):
                b = PS.alloc()
                for (blk, tb) in blocks(T):
                    if "onetr" in DBG and blk > 0:
                        continue
                    if "nope" in DBG:
                        continue
                    P.op("pe", lambda e, i=i, b=b, blk=blk, tb=tb: e.transpose(
                        out=ps(b)[:, blk * 128:blk * 128 + tb], in_=XIN[0:tb, blk, i * 128:(i + 1) * 128],
                        identity=ident32[0:tb, 0:tb]), reads=["XIN", "ident32"], writes=[psr(b)])
                if "noact" not in DBG:
                    P.op("act", lambda e, i=i, b=b: e.copy(out=X[:, i, 0:T], in_=ps(b)[:, 0:T]),
                         reads=[psr(b)], writes=[("X", i)])
                if "nodve" not in DBG:
                    P.op("dve", lambda e, i=i, b=b: e.tensor_copy(out=Xb[:, i, 0:T], in_=X[:, i, 0:T]),
                         reads=[("X", i)], writes=[("Xb", i)])
                PS.free(b)

        def transpose_out(tile, cnt):
            s, t0, T, _ = tile
            for (blk, tb) in blocks(T):
                rot = cnt[0] % 2
                cnt[0] += 1
                ba, bb = PS.alloc(), PS.alloc()
                for i in range(KT):
                    bk = ba if i < 4 else bb
                    P.op("pe", lambda e, i=i, bk=bk, blk=blk: e.transpose(
                        out=ps(bk)[:, (i % 4) * 128:(i % 4 + 1) * 128], in_=X[:, i, blk * 128:(blk + 1) * 128],
                        identity=ident32[:, :]), reads=[("X", i), "ident32"], writes=[psr(bk)])
                P.op("act", lambda e, rot=rot, ba=ba: e.copy(out=YOUT[:, rot, 0:512], in_=ps(ba)[:, :]),
                     reads=[psr(ba)], writes=[("YOUT", rot, 0)])
                P.op("dve", lambda e, rot=rot, bb=bb: e.tensor_copy(out=YOUT[:, rot, 512:1024], in_=ps(bb)[:, :]),
                     reads=[psr(bb)], writes=[("YOUT", rot, 1)])
                PS.free(ba)
                PS.free(bb)
                P.op("sp", lambda e, rot=rot, s=s, t0=t0, blk=blk: e.dma_start(
                    out=y[s, t0 + blk * 128:t0 + (blk + 1) * 128, :], in_=YOUT[:, rot, :]),
                    reads=[("YOUT", rot, 0), ("YOUT", rot, 1)], dma_key=("yout", rot))

        def proj_fm(bank, slot, c0, T, extra=None):
            W = wv(slot)

            def fn(e):
                ins = None
                for k in range(KT):
                    ins = e.matmul(ps(bank)[:, 0:T], W[:, k, c0:c0 + 128], Xb[:, k, 0:T],
                                   start=(k == 0), stop=(k == KT - 1 and extra is None))
                if extra is not None:
                    ins = extra(e)
                return ins
            P.op("pe", fn, reads=[("w", slot)] + [("Xb", k) for k in range(KT)], writes=[psr(bank)])

        def stats_mm(src_ap, sq_ap, first, last, reads):
            def fn(e):
                e.matmul(ps(ST1)[:, 0:src_ap.shape[-1]], ones32[:, :], src_ap, start=first, stop=last)
                return e.matmul(ps(ST2)[:, 0:src_ap.shape[-1]], onesb[:, 0:128], sq_ap, start=first, stop=last)
            P.op("pe", fn, reads=reads + ["ones32", "onesb"], writes=[psr(ST1), psr(ST2)])

        def stats_finish(T, eps):
            P.op("dve", lambda e: e.tensor_scalar(out=MEAN[:, 0:T], in0=ps(ST1)[:, 0:T], scalar1=1.0 / D, scalar2=None,
                                                  op0=ALU.mult), reads=[psr(ST1)], writes=["MEAN"])
            P.op("dve", lambda e: e.tensor_tensor(out=MSQ[:, 0:T], in0=MEAN[:, 0:T], in1=MEAN[:, 0:T], op=ALU.mult),
                 reads=["MEAN"], writes=["MSQ"])
            P.op("dve", lambda e: e.scalar_tensor_tensor(out=MSQ[:, 0:T], in0=ps(ST2)[:, 0:T], scalar=1.0 / D,
                                                         in1=MSQ[:, 0:T], op0=ALU.mult, op1=ALU.subtract),
                 reads=[psr(ST2), "MSQ"], writes=["MSQ"])
            P.op("act", lambda e: e.activation(out=MSQ[:, 0:T], in_=MSQ[:, 0:T], func=AF.Sqrt, bias=EPS_LN[:, 0:1]),
                 reads=["MSQ", "EPS"], writes=["MSQ"])
            P.op("dve", lambda e: e.reciprocal(out=RSTD[:, 0:T], in_=MSQ[:, 0:T]), reads=["MSQ"], writes=["RSTD"])

        def out_proj_and_post_ln(l, T, slots, bias_row):
            pending = []
            for i in range(KT):
                slot = slots[i // 4]
                c0 = (i % 4) * 128
                W = wv(slot)
                b = PS.alloc()

                def fn(e, W=W, c0=c0, b=b, i=i):
                    ins = None
                    for j in range(KT):
                        ins = e.matmul(ps(b)[:, 0:T], W[:, j, c0:c0 + 128], M[:, j, 0:T], start=(j == 0),
                                       stop=(j == KT - 1 and bias_row is None))
                    if bias_row is not None:
                        ins = e.matmul(ps(b)[:, 0:T], bias_row[0:1, i * 128:(i + 1) * 128], onesb[0:1, 0:T],
                                       start=False, stop=True)
                    return ins
                P.op("pe", fn, reads=[("w", slot), "BOROW", "onesb"] + [("M", j) for j in range(KT)], writes=[psr(b)])
                if i % 4 == 3:
                    RG.release(slot)
                P.op("dve", lambda e, i=i, b=b: e.scalar_tensor_tensor(
                    out=X[:, i, 0:T], in0=X[:, i, 0:T], scalar=float(ALPHA), in1=ps(b)[:, 0:T],
                    op0=ALU.mult, op1=ALU.add), reads=[psr(b), ("X", i)], writes=[("X", i)])
                PS.free(b)
                rot = i % 2
                P.op("act", lambda e, i=i, rot=rot: e.activation(out=SQ[:, rot, 0:T], in_=X[:, i, 0:T], func=AF.Square),
                     reads=[("X", i)], writes=[("SQ", rot)])
                pending.append((i, rot))
                if len(pending) > 1:
                    pi, prot = pending.pop(0)
                    stats_mm(X[:, pi, 0:T], SQ[:, prot, 0:T], pi == 0, False, [("X", pi), ("SQ", prot)])
            pi, prot = pending.pop(0)
            stats_mm(X[:, pi, 0:T], SQ[:, prot, 0:T], False, True, [("X", pi), ("SQ", prot)])
            stats_finish(T, LN_EPS)
            gcol = LNB + 2 * l
            for i in range(KT):
                rot = i % 2
                P.op("dve", lambda e, i=i, rot=rot: e.tensor_tensor(out=T1[:, rot, 0:T], in0=X[:, i, 0:T],
                                                                     in1=MEAN[:, 0:T], op=ALU.subtract),
                     reads=[("X", i), "MEAN"], writes=[("T1", rot)])
                P.op("dve", lambda e, rot=rot: e.tensor_tensor(out=T1[:, rot, 0:T], in0=T1[:, rot, 0:T],
                                                               in1=RSTD[:, 0:T], op=ALU.mult),
                     reads=[("T1", rot), "RSTD"], writes=[("T1", rot)])
                P.op("act", lambda e, i=i, rot=rot: e.activation(
                    out=X[:, i, 0:T], in_=T1[:, rot, 0:T], func=AF.Identity,
                    scale=PT[:, i, gcol:gcol + 1], bias=PT[:, i, gcol + 1:gcol + 2]),
                    reads=[("T1", rot), "PT"], writes=[("X", i)])
                P.op("act", lambda e, i=i, rot=rot: e.activation(
                    out=Xb[:, i, 0:T], in_=T1[:, rot, 0:T], func=AF.Identity,
                    scale=PT[:, i, gcol:gcol + 1], bias=PT[:, i, gcol + 1:gcol + 2]),
                    reads=[("T1", rot), "PT"], writes=[("Xb", i)])

        def conv_layer(l, T, first):
            lc = l // 2
            base = lc * CONV_ROWS
            AR.reset()
            GLU = AR.take(KT * (TT + HALO), BF16, "p (k t) -> p k t", k=KT)
            ZS = AR.take(KT * TT, BF16, "p (k t) -> p k t", k=KT)
            C = AR.take(KT * TT, F32, "p (k t) -> p k t", k=KT)
            SIG = AR.take(2 * TT, BF16, "p (k t) -> p k t", k=2)
            CS = AR.take(2 * TT, BF16, "p (k t) -> p k t", k=2)
            DG = AR.take(2 * TAPS * 128, BF16, "p (r t c) -> p r t c", r=2, t=TAPS)
            if first:
                P.op("pool", lambda e: e.memset(HAL[:, lc, :, :], 0.0), writes=[("HAL", lc)])
            P.op("pool", lambda e: e.tensor_copy(out=GLU[:, :, 0:HALO], in_=HAL[:, lc, :, :]),
                 reads=[("HAL", lc)], writes=[("GLU", j) for j in range(KT)])
            slots = {}

            def proj(j):
                if j % 4 == 0:
                    slots["a"], slots["g"], slots["z"] = RG.get(), RG.get(), RG.get()
                c0 = (j % 4) * 128
                pa, pg, pz = PS.alloc(), PS.alloc(), PS.alloc()
                proj_fm(pg, slots["g"], c0, T)
                proj_fm(pa, slots["a"], c0, T)
                proj_fm(pz, slots["z"], c0, T)
                if j % 4 == 3:
                    RG.release(slots["a"]); RG.release(slots["g"]); RG.release(slots["z"])
                rot = j % 2
                P.op("act", lambda e: e.activation(out=SIG[:, rot, 0:T], in_=ps(pg)[:, 0:T], func=AF.Sigmoid,
                                                   bias=PT[:, j, base + 1:base + 2]),
                     reads=[psr(pg), "PT"], writes=[("SIG", rot)])
                P.op("dve", lambda e: e.scalar_tensor_tensor(
                    out=GLU[:, j, HALO:HALO + T], in0=ps(pa)[:, 0:T], scalar=PT[:, j, base:base + 1],
                    in1=SIG[:, rot, 0:T], op0=ALU.add, op1=ALU.mult),
                    reads=[psr(pa), ("SIG", rot), "PT"], writes=[("GLU", j)])
                P.op("act", lambda e: e.activation(out=ZS[:, j, 0:T], in_=ps(pz)[:, 0:T], func=AF.Silu,
                                                   bias=PT[:, j, base + 2:base + 3]),
                     reads=[psr(pz), "PT"], writes=[("ZS", j)])
                PS.free(pg); PS.free(pa); PS.free(pz)

                def dg(e):
                    ins = None
                    for tap in range(TAPS):
                        ins = e.tensor_scalar(out=DG[:, rot, tap, :], in0=identb[:, :],
                                              scalar1=PT[:, j, base + 3 + tap:base + 4 + tap], scalar2=None,
                                              op0=ALU.mult)
                    return ins
                P.op("pool", dg, reads=["identb", "PT"], writes=[("DG", rot)])

            def conv(j):
                rot = j % 2
                pc = PS.alloc()

                def fn(e):
                    ins = None
                    for tap in range(TAPS):
                        ins = e.matmul(ps(pc)[:, 0:T], DG[:, rot, tap, :], GLU[:, j, tap:tap + T],
                                       start=(tap == 0), stop=(tap == TAPS - 1))
                    return ins
                P.op("pe", fn, reads=[("DG", rot), ("GLU", j)], writes=[psr(pc)])
                P.op("act", lambda e: e.activation(out=C[:, j, 0:T], in_=ps(pc)[:, 0:T], func=AF.Identity,
                                                   bias=PT[:, j, base + 34:base + 35]),
                     reads=[psr(pc), "PT"], writes=[("C", j)])
                P.op("act", lambda e: e.activation(out=SQ[:, rot, 0:T], in_=ps(pc)[:, 0:T], func=AF.Square,
                                                   bias=PT[:, j, base + 34:base + 35]),
                     reads=[psr(pc), "PT"], writes=[("SQ", rot)])
                PS.free(pc)

            def stats(j):
                stats_mm(C[:, j, 0:T], SQ[:, j % 2, 0:T], j == 0, j == KT - 1, [("C", j), ("SQ", j % 2)])

            for j in range(KT + 2):
                if j < KT:
                    proj(j)
                if 1 <= j <= KT:
                    conv(j - 1)
                if j >= 2:
                    stats(j - 2)
            P.op("pool", lambda e: e.tensor_copy(out=HAL[:, lc, :, :], in_=GLU[:, :, T:T + HALO]),
                 reads=[("GLU", j) for j in range(KT)], writes=[("HAL", lc)])
            stats_finish(T, LN_EPS)
            for j in range(KT):
                rot = j % 2
                P.op("dve", lambda e, j=j, rot=rot: e.tensor_tensor(out=T1[:, rot, 0:T], in0=C[:, j, 0:T],
                                                                     in1=MEAN[:, 0:T], op=ALU.subtract),
                     reads=[("C", j), "MEAN"], writes=[("T1", rot)])
                P.op("dve", lambda e, rot=rot: e.tensor_tensor(out=T1[:, rot, 0:T], in0=T1[:, rot, 0:T],
                                                               in1=RSTD[:, 0:T], op=ALU.mult),
                     reads=[("T1", rot), "RSTD"], writes=[("T1", rot)])
                P.op("act", lambda e, j=j, rot=rot: e.activation(
                    out=CS[:, rot, 0:T], in_=T1[:, rot, 0:T], func=AF.Silu,
                    scale=PT[:, j, base + 35:base + 36], bias=PT[:, j, base + 36:base + 37]),
                    reads=[("T1", rot), "PT"], writes=[("CS", rot)])
                P.op("dve", lambda e, j=j, rot=rot: e.tensor_tensor(out=M[:, j, 0:T], in0=CS[:, rot, 0:T],
                                                                     in1=ZS[:, j, 0:T], op=ALU.mult),
                     reads=[("CS", rot), ("ZS", j)], writes=[("M", j)])
            so = [RG.get(), RG.get()]
            out_proj_and_post_ln(l, T, so, BOROW[:, lc, :])

        def gla_layer(l, T, first):
            lg = l // 2
            AR.reset()
            A1 = AR.take(TT, BF16)
            G = AR.take(4 * 512, F32, "p (b n) -> p b n", b=4)
            E = AR.take(1 * 512, F32, "p (r n) -> p r n", r=1)
            EB = AR.take(4 * TT, F32, "p (h t) -> p h t", h=4)
            ENB = AR.take(1 * TT, F32, "p (r t) -> p r t", r=1)
            Q = AR.take(4 * TT, BF16, "p (h t) -> p h t", h=4)
            Kt = AR.take(4 * TT, BF16, "p (h t) -> p h t", h=4)
            KTm = AR.take(4 * 512, BF16, "p (b n) -> p b n", b=4)
            V = AR.take(4 * D, BF16, "p (b n) -> p b n", b=4)
            RS = AR.take(KT * TT, BF16, "p (k t) -> p k t", k=KT)
            AT = AR.take(4 * 128, BF16, "p (r t) -> p r t", r=4)
            TMP = AR.take(4 * 256, F32, "p (r t) -> p r t", r=4)
            ON = AR.take(2 * D, BF16, "p (r t) -> p r t", r=2)
            SS = AR.take(16, F32)
            RH = AR.take(16, F32)
            JUNK = AR.take(256, BF16)
            blks = blocks(T)
            if first:
                P.op("pool", lambda e: e.memset(S32[:, lg, :, :], 0.0), writes=[("S32", lg, h) for h in range(4)])
                P.op("pool", lambda e: e.memset(Sb[:, lg, :, :], 0.0), writes=[("Sb", lg, h) for h in range(4)])
            sq, sk = RG.get(), RG.get()
            b = PS.alloc()

            def fa1(e):
                ins = None
                for k in range(KT):
                    ins = e.matmul(ps(b)[0:16, 0:T], WSM[:, lg, k, :], Xb[:, k, 0:T], start=(k == 0), stop=(k == KT - 1))
                return ins
            P.op("pe", fa1, reads=["WSM"] + [("Xb", k) for k in range(KT)], writes=[psr(b)])
            P.op("dve", lambda e, b=b: e.tensor_copy(out=A1[0:16, 0:T], in_=ps(b)[0:16, 0:T]), reads=[psr(b)], writes=["A1"])
            PS.free(b)
            for (blk, tb) in blks:
                b = PS.alloc()
                rot = 0

                def fg(e, b=b, blk=blk, tb=tb):
                    e.matmul(ps(b)[0:tb, :], A1[0:16, blk * 128:blk * 128 + tb], WA2[0:16, lg, :], start=True, stop=False)
                    return e.matmul(ps(b)[0:tb, :], onesb[0:1, 0:tb], BAROW[0:1, lg, :], start=False, stop=True)
                P.op("pe", fg, reads=["A1", "WA2", "BAROW", "onesb"], writes=[psr(b)])
                P.op("act", lambda e, b=b, tb=tb, rot=rot: e.activation(out=E[0:tb, rot, :], in_=ps(b)[0:tb, :],
                                                                        func=AF.Exp, scale=-1.0),
                     reads=[psr(b)], writes=[("E", rot)])
                PS.free(b)
                P.op("act", lambda e, blk=blk, tb=tb, rot=rot: e.activation(out=G[0:tb, blk, :], in_=E[0:tb, rot, :],
                                                                            func=AF.Ln, bias=1.0),
                     reads=[("E", rot)], writes=[("G", blk)])
            for h in range(4):
                bbc, bq, bk = PS.alloc(), PS.alloc(), PS.alloc()

                def fbc(e, h=h, bbc=bbc):
                    ins = None
                    for (blk, tb) in blks:
                        ins = e.matmul(ps(bbc)[:, blk * 128:blk * 128 + tb], G[0:tb, blk, h * 128:(h + 1) * 128],
                                       tri32[0:tb, 0:tb], start=True, stop=True)
                    return ins
                P.op("pe", fbc, reads=[("G", blk) for (blk, _) in blks] + ["tri32"], writes=[psr(bbc)])
                proj_fm(bq, sq, h * 128, T)
                proj_fm(bk, sk, h * 128, T)
                rot = 0
                P.op("act", lambda e, h=h, bbc=bbc: e.activation(out=EB[:, h, 0:T], in_=ps(bbc)[:, 0:T], func=AF.Exp,
                                                                 scale=-1.0 / 16.0),
                     reads=[psr(bbc)], writes=[("EB", h)])
                P.op("act", lambda e, rot=rot, bbc=bbc: e.activation(out=ENB[:, rot, 0:T], in_=ps(bbc)[:, 0:T],
                                                                     func=AF.Exp, scale=1.0 / 16.0),
                     reads=[psr(bbc)], writes=[("ENB", rot)])
                P.op("dve", lambda e, h=h, bq=bq: e.scalar_tensor_tensor(
                    out=Q[:, h, 0:T], in0=ps(bq)[:, 0:T], scalar=float(128 ** -0.5), in1=EB[:, h, 0:T],
                    op0=ALU.mult, op1=ALU.mult), reads=[psr(bq), ("EB", h)], writes=[("Q", h)])
                P.op("dve", lambda e, h=h, bk=bk, rot=rot: e.tensor_tensor(
                    out=Kt[:, h, 0:T], in0=ps(bk)[:, 0:T], in1=ENB[:, rot, 0:T], op=ALU.mult),
                    reads=[psr(bk), ("ENB", rot)], writes=[("Kt", h)])
                PS.free(bbc); PS.free(bq); PS.free(bk)
            RG.release(sq); RG.release(sk)
            sr = [RG.get(), RG.get()]
            for j in range(KT):
                b = PS.alloc()
                proj_fm(b, sr[j // 4], (j % 4) * 128, T)
                if j % 4 == 3:
                    RG.release(sr[j // 4])
                P.op("act", lambda e, j=j, b=b: e.activation(out=RS[:, j, 0:T], in_=ps(b)[:, 0:T], func=AF.Silu),
                     reads=[psr(b)], writes=[("RS", j)])
                PS.free(b)
                gc = lg * 2 + (j % 2)
                P.op("pool", lambda e, j=j, gc=gc: e.tensor_scalar(out=RS[:, j, 0:T], in0=RS[:, j, 0:T],
                                                                   scalar1=GN[:, gc:gc + 1], scalar2=None, op0=ALU.mult),
                     reads=[("RS", j), "GN"], writes=[("RS", j)])
            sv = [RG.get(), RG.get()]
            for half in range(2):
                W = wv(sv[half])
                for (blk, tb) in blks:
                    b = PS.alloc()

                    def fv(e, W=W, b=b, blk=blk, tb=tb):
                        ins = None
                        for k in range(KT):
                            ins = e.matmul(ps(b)[0:tb, :], Xb[:, k, blk * 128:blk * 128 + tb], W[:, k, :],
                                           start=(k == 0), stop=(k == KT - 1))
                        return ins
                    P.op("pe", fv, reads=[("w", sv[half])] + [("Xb", k) for k in range(KT)], writes=[psr(b)])
                    eng = "act" if (blk + half) % 2 == 0 else "dve"
                    if eng == "act":
                        P.op("act", lambda e, b=b, blk=blk, tb=tb, half=half: e.copy(
                            out=V[0:tb, blk, half * 512:(half + 1) * 512], in_=ps(b)[0:tb, :]),
                            reads=[psr(b)], writes=[("V", blk, half)])
                    else:
                        P.op("dve", lambda e, b=b, blk=blk, tb=tb, half=half: e.tensor_copy(
                            out=V[0:tb, blk, half * 512:(half + 1) * 512], in_=ps(b)[0:tb, :]),
                            reads=[psr(b)], writes=[("V", blk, half)])
                    PS.free(b)
                RG.release(sv[half])
            for (blk, tb) in blks:
                b = PS.alloc()
                pb = ps(b)[:, :].bitcast(BF16)

                def fkt(e, pb=pb, blk=blk, tb=tb):
                    ins = None
                    for h in range(4):
                        ins = e.transpose(out=pb[0:tb, h * 128:(h + 1) * 128], in_=Kt[:, h, blk * 128:blk * 128 + tb],
                                          identity=identb[:, :])
                    return ins
                P.op("pe", fkt, reads=[("Kt", h) for h in range(4)] + ["identb"], writes=[psr(b)])
                P.op("dve", lambda e, pb=pb, blk=blk, tb=tb: e.tensor_copy(out=KTm[0:tb, blk, :], in_=pb[0:tb, 0:512]),
                     reads=[psr(b)], writes=[("KTm", blk)])
                PS.free(b)
            def chunk(blk, tb):
                t0 = blk * 128
                orot = blk % 2
                pat = PS.alloc()

                def fat(e):
                    ins = None
                    for h in range(4):
                        ins = e.matmul(ps(pat)[0:tb, h * 128:h * 128 + tb], Kt[:, h, t0:t0 + tb], Q[:, h, t0:t0 + tb],
                                       start=True, stop=True)
                    return ins
                P.op("pe", fat, reads=[("Kt", h) for h in range(4)] + [("Q", h) for h in range(4)], writes=[psr(pat)])
                for h in range(4):
                    P.op("dve", lambda e, h=h: e.tensor_tensor(out=AT[0:tb, h, 0:tb], in0=ps(pat)[0:tb, h * 128:h * 128 + tb],
                                                                in1=trib[0:tb, 0:tb], op=ALU.mult),
                         reads=[psr(pat), "trib"], writes=[("AT", h)])
                PS.free(pat)
                pkv = [PS.alloc(), PS.alloc()]
                for h in range(4):
                    P.op("pe", lambda e, h=h: e.matmul(ps(pkv[h // 2])[:, (h % 2) * 256:(h % 2) * 256 + 256],
                                                       KTm[0:tb, blk, h * 128:(h + 1) * 128],
                                                       V[0:tb, blk, h * 256:(h + 1) * 256], start=True, stop=True),
                         reads=[("KTm", blk), ("V", blk, h // 2)], writes=[psr(pkv[h // 2])])
                po = [PS.alloc(), PS.alloc()]
                for h in range(4):
                    pob = po[h // 2]
                    oc = (h % 2) * 256
                    P.op("pe", lambda e, h=h, pob=pob, oc=oc: e.matmul(
                        ps(pob)[0:tb, oc:oc + 256], AT[0:tb, h, 0:tb], V[0:tb, blk, h * 256:(h + 1) * 256],
                        start=True, stop=False), reads=[("AT", h), ("V", blk, h // 2)], writes=[psr(pob)])
                    P.op("pe", lambda e, h=h, pob=pob, oc=oc: e.matmul(
                        ps(pob)[0:tb, oc:oc + 256], Q[:, h, t0:t0 + tb], Sb[:, lg, h, :],
                        start=False, stop=True), reads=[("Q", h), ("Sb", lg, h)], writes=[psr(pob)])
                for h in range(4):
                    el = EB[:, h, t0 + tb - 1:t0 + tb]
                    kc = (h % 2) * 256
                    P.op("act", lambda e, h=h, el=el, kc=kc: e.activation(out=TMP[:, h, :], in_=ps(pkv[h // 2])[:, kc:kc + 256],
                                                                          func=AF.Copy, scale=el),
                         reads=[psr(pkv[h // 2]), ("EB", h)], writes=[("TMP", h)])
                    P.op("dve", lambda e, h=h, el=el: e.scalar_tensor_tensor(
                        out=S32[:, lg, h, :], in0=S32[:, lg, h, :], scalar=el, in1=TMP[:, h, :],
                        op0=ALU.mult, op1=ALU.add), reads=[("S32", lg, h), ("TMP", h), ("EB", h)],
                        writes=[("S32", lg, h)])
                    P.op("pool", lambda e, h=h: e.tensor_copy(out=Sb[:, lg, h, :], in_=S32[:, lg, h, :]),
                         reads=[("S32", lg, h)], writes=[("Sb", lg, h)])
                PS.free(pkv[0]); PS.free(pkv[1])
                for h in range(4):
                    pob = po[h // 2]
                    oc = (h % 2) * 256
                    P.op("act", lambda e, h=h, pob=pob, oc=oc: e.activation(
                        out=JUNK[0:tb, :], in_=ps(pob)[0:tb, oc:oc + 256], func=AF.Square,
                        accum_out=SS[0:tb, blk * 4 + h:blk * 4 + h + 1]),
                        reads=[psr(pob)], writes=[("SS", blk), "JUNK"])
                P.op("act", lambda e: e.activation(out=RH[0:tb, blk * 4:blk * 4 + 4], in_=SS[0:tb, blk * 4:blk * 4 + 4],
                                                   func=AF.Sqrt, scale=1.0 / 256.0, bias=EPS_RMS[0:tb, 0:1]),
                     reads=[("SS", blk), "EPS"], writes=[("RH", blk)])
                P.op("dve", lambda e: e.reciprocal(out=RH[0:tb, blk * 4:blk * 4 + 4], in_=RH[0:tb, blk * 4:blk * 4 + 4]),
                     reads=[("RH", blk)], writes=[("RH", blk)])
                for h in range(4):
                    pob = po[h // 2]
                    oc = (h % 2) * 256
                    P.op("act", lambda e, h=h, pob=pob, oc=oc: e.activation(
                        out=ON[0:tb, orot, h * 256:(h + 1) * 256], in_=ps(pob)[0:tb, oc:oc + 256], func=AF.Copy,
                        scale=RH[0:tb, blk * 4 + h:blk * 4 + h + 1]),
                        reads=[psr(pob), ("RH", blk)], writes=[("ON", orot)])
                PS.free(po[0]); PS.free(po[1])
                pt_ = PS.alloc()
                ptb = ps(pt_)[:, :].bitcast(BF16).rearrange("p (j t) -> p j t", j=KT)

                def ftr(e):
                    ins = None
                    for j in range(KT):
                        ins = e.transpose(out=ptb[:, j, 0:tb], in_=ON[0:tb, orot, j * 128:(j + 1) * 128],
                                          identity=identb[0:tb, 0:tb])
                    return ins
                P.op("pe", ftr, reads=[("ON", orot), "identb"], writes=[psr(pt_)])
                P.op("dve", lambda e: e.tensor_tensor(out=M[:, :, t0:t0 + tb], in0=ptb[:, :, 0:tb],
                                                      in1=RS[:, :, t0:t0 + tb], op=ALU.mult),
                     reads=[psr(pt_)] + [("RS", j) for j in range(KT)], writes=[("M", j) for j in range(KT)])
                PS.free(pt_)

            for (blk, tb) in blks:
                chunk(blk, tb)
            so = [RG.get(), RG.get()]
            out_proj_and_post_ln(l, T, so, None)

        ycnt = [0]
        load_x(tiles[0])
        for ti, tile in enumerate(tiles):
            s, t0, T, is_meta = tile
            if "notile" in DBG:
                break
            if "nometa" in DBG and is_meta:
                if ti + 1 < len(tiles):
                    load_x(tiles[ti + 1])
                continue
            transpose_in(T)
            if ti + 1 < len(tiles):
                load_x(tiles[ti + 1])
            for l in range(depth):
                P.barrier()
                if l % 2 == 0:
                    conv_layer(l, T, is_meta)
                else:
                    gla_layer(l, T, is_meta)
            P.barrier()
            if not is_meta and "noout" not in DBG:
                transpose_out(tile, ycnt)
        P.emit(None, sems, dma_sems, [("yout", 0), ("yout", 1)])
    return nc


_CACHE = {}


def _consts():
    ident = np.eye(128, dtype=np.float32)
    tri = np.triu(np.ones((128, 128), dtype=np.float32))
    return ident, tri


def run(inputs, nseq_per_core, ncores, depth=DEPTH_FULL, trace=False):
    x = np.ascontiguousarray(inputs["x"], dtype=np.float32)
    seq = x.shape[1]
    key = (nseq_per_core, seq, depth)
    if key not in _CACHE:
        _CACHE[key] = build(nseq_per_core, seq, depth)
    nc = _CACHE[key]
    ident, tri = _consts()
    in_maps = []
    for c in range(ncores):
        m = {k: np.ascontiguousarray(v, dtype=np.float32) for k, v in inputs.items() if k != "x"}
        m["x"] = x[c * nseq_per_core:(c + 1) * nseq_per_core]
        m["c_ident"] = ident
        m["c_tri"] = tri
        in_maps.append(m)
    res = run_bass_kernel_spmd(nc, in_maps, core_ids=list(range(ncores)), trace=trace)
    out = np.concatenate([r["y"] for r in res.results], axis=0)
    return out, res


def kernel(**inputs):
    out, _ = run(inputs, inputs["x"].shape[0] // NCORES, NCORES)
    return out
```

```python
import numpy as np
import concourse.bass as bass
import concourse.mybir as mybir
from concourse.bass_utils import run_bass_kernel_spmd

F32 = mybir.dt.float32
BF16 = mybir.dt.bfloat16
AF = mybir.ActivationFunctionType
ALU = mybir.AluOpType

D = 1024
KT = 8
NMETA = 16
TAPS = 31
HALO = 30
DEPTH_FULL = 4
ALPHA = (2 * DEPTH_FULL) ** 0.25
LN_EPS = 1e-5
RMS_EPS = 1e-6
TT = 512
NCORES = 8
CONV_ROWS = 37
NRING = 6

COMPUTE = ("pe", "act", "dve", "pool")


class Prog:
    def __init__(self, nc):
        self.nc = nc
        self.ops = {e: [] for e in ("pe", "act", "dve", "pool", "sp")}
        self.last_w = {}
        self.readers = {}
        self.dma_cnt = {}
        self.bar = {e: set() for e in self.ops}

    def op(self, eng, fn, reads=(), writes=(), dma_key=None):
        lst = self.ops[eng]
        idx = len(lst)
        deps = {}

        def add(d, kind):
            if d is None:
                return
            if d in deps and deps[d] == "raw":
                return
            deps[d] = kind

        for r in reads:
            add(self.last_w.get(r), "raw")
        for r in writes:
            add(self.last_w.get(r), "waw")
            for rd in self.readers.get(r, ()):
                add(rd, "war")
        for d in self.bar[eng]:
            add(d, "raw")
        self.bar[eng] = set()
        rec = dict(fn=fn, deps=deps, dma_key=dma_key, dma_n=None, signal=False, ordinal=None)
        if dma_key is not None:
            n = self.dma_cnt.get(dma_key, 0) + 1
            self.dma_cnt[dma_key] = n
            rec["dma_n"] = n
        lst.append(rec)
        me = (eng, idx)
        for r in reads:
            s = self.readers.setdefault(r, set())
            if dma_key is None:
                for o in [o for o in s if o[0] == eng and self.ops[eng][o[1]]["dma_key"] is None]:
                    s.discard(o)
            s.add(me)
        for r in writes:
            self.last_w[r] = me
            self.readers[r] = set()
        return me

    def barrier(self):
        lasts = set()
        for e in COMPUTE:
            for i in range(len(self.ops[e]) - 1, -1, -1):
                if self.ops[e][i]["dma_key"] is None:
                    lasts.add((e, i))
                    break
        for e in COMPUTE:
            self.bar[e] = set(d for d in lasts if d[0] != e)

    def resolve(self):
        for eng, lst in self.ops.items():
            waited = {e: -1 for e in self.ops}
            waited_dma = {}
            for idx, rec in enumerate(lst):
                waits = []
                for (pe_, pi), kind in rec["deps"].items():
                    prod = self.ops[pe_][pi]
                    if prod["dma_key"] is not None:
                        k = prod["dma_key"]
                        if waited_dma.get(k, 0) >= prod["dma_n"]:
                            continue
                        waited_dma[k] = prod["dma_n"]
                        waits.append(("dma", k, prod["dma_n"]))
                        continue
                    if pe_ == eng:
                        if eng in ("pe", "sp"):
                            continue
                    if waited[pe_] >= pi:
                        continue
                    waited[pe_] = pi
                    prod["signal"] = True
                    waits.append(("eng", pe_, pi))
                rec["waits"] = waits
        for eng, lst in self.ops.items():
            n = 0
            for rec in lst:
                if rec["signal"]:
                    n += 1
                    rec["ordinal"] = n

    def emit(self, block_ctx_factory, sems, dma_sems, final_waits):
        self.resolve()
        nc = self.nc
        P = self

        def run(eng_name):
            def body(e):
                for rec in P.ops[eng_name]:
                    for w in rec["waits"]:
                        if w[0] == "dma":
                            e.wait_ge(dma_sems[w[1]], 16 * w[2])
                        else:
                            e.wait_ge(sems[w[1]], P.ops[w[1]][w[2]]["ordinal"])
                    ins = rec["fn"](e)
                    if rec["dma_key"] is not None:
                        ins.then_inc(dma_sems[rec["dma_key"]], 16)
                    elif rec["signal"]:
                        ins.then_inc(sems[eng_name], 1)
                if eng_name == "sp":
                    for k in final_waits:
                        if P.dma_cnt.get(k, 0):
                            e.wait_ge(dma_sems[k], 16 * P.dma_cnt[k])
            return body

        with nc.Block() as block:
            block.tensor(run("pe"))
            block.scalar(run("act"))
            block.vector(run("dve"))
            block.gpsimd(run("pool"))
            block.sync(run("sp"))


def layer_groups(depth):
    groups = []
    for l in range(depth):
        j = l // 2
        if l % 2 == 0:
            for c0 in (0, 1024, 2048, 512, 1536, 2560):
                groups.append((l, "conv_w_in", j, c0))
            for c0 in (0, 512):
                groups.append((l, "conv_w_out", j, c0))
        else:
            for c0 in (0, 512, 2048, 2560, 1024, 1536):
                groups.append((l, "gla_w_in", j, c0))
            for c0 in (0, 512):
                groups.append((l, "gla_w_out", j, c0))
    return groups


def build(nseq, seq, depth):
    assert seq % TT == 0
    nc = bass.Bass("TRN2", target_bir_lowering=False)
    n_conv = (depth + 1) // 2
    n_gla = depth // 2
    dr = {}

    def din(name, shape):
        dr[name] = nc.dram_tensor(name, list(shape), F32, kind="ExternalInput").ap()
        return dr[name]

    x = din("x", (nseq, seq, D))
    meta = din("meta", (NMETA, D))
    din("conv_w_in", (2, D, 3072)); din("conv_b_in", (2, 3072)); din("conv_w_dw", (2, TAPS, D))
    din("conv_b_dw", (2, D)); din("conv_norm_g", (2, D)); din("conv_norm_b", (2, D))
    din("conv_w_out", (2, D, D)); din("conv_b_out", (2, D))
    din("gla_w_in", (2, D, 3088)); din("gla_w_a2", (2, 16, 512)); din("gla_b_a", (2, 512))
    din("gla_norm_g", (2, 256)); din("gla_w_out", (2, D, D))
    din("post_ln_g", (4, D)); din("post_ln_b", (4, D))
    ident_d = din("c_ident", (128, 128))
    tri_d = din("c_tri", (128, 128))
    y = nc.dram_tensor("y", [nseq, seq, D], F32, kind="ExternalOutput").ap()

    groups = layer_groups(depth)
    NG = len(groups)
    wscr = nc.dram_tensor("wscr", [max(NG, 1), 128, KT * 512], BF16, kind="Internal").ap()

    P = Prog(nc)
    import contextlib
    es = contextlib.ExitStack()
    with es:
        def sb(name, shape, dt):
            return es.enter_context(nc.sbuf_tensor(name, list(shape), dt))

        XIN = sb("XIN", [128, 4, D], F32)
        YOUT = sb("YOUT", [128, 2, D], F32)
        X = sb("X", [128, KT, TT], F32)
        Xb = sb("Xb", [128, KT, TT], BF16)
        WR = sb("WR", [128, NRING, KT * 512], BF16)
        WSM = sb("WSM", [128, 2, KT, 16], BF16)
        WA2 = sb("WA2", [16, 2, 512], BF16)
        BAROW = sb("BAROW", [1, 2, 512], BF16)
        BOROW = sb("BOROW", [1, 2, D], BF16)
        HAL = sb("HAL", [128, 2, KT, HALO], BF16)
        S32 = sb("S32", [128, 2, 4, 256], F32)
        Sb = sb("Sb", [128, 2, 4, 256], BF16)
        PROW2 = sb("PROW2", [4, 128], F32)
        PT = sb("PT", [128, KT, 88], F32)
        GN = sb("GN", [128, 4], F32)
        ident32 = sb("ident32", [128, 128], F32)
        tri32 = sb("tri32", [128, 128], F32)
        identb = sb("identb", [128, 128], BF16)
        trib = sb("trib", [128, 128], BF16)
        ones32 = sb("ones32", [128, 128], F32)
        onesb = sb("onesb", [128, TT], BF16)
        MEAN = sb("MEAN", [128, TT], F32)
        EPS_LN = sb("EPS_LN", [128, 1], F32)
        EPS_RMS = sb("EPS_RMS", [128, 1], F32)
        MSQ = sb("MSQ", [128, TT], F32)
        RSTD = sb("RSTD", [128, TT], F32)
        T1 = sb("T1", [128, 2, TT], F32)
        SQ = sb("SQ", [128, 2, TT], BF16)
        M = sb("M", [128, KT, TT], BF16)
        ARENA = sb("ARENA", [128, 15 * 1024], F32)
        PSt = [es.enter_context(nc.psum_tensor(f"ps{i}", [128, 512], F32)) for i in range(8)]

        class Arena:
            def __init__(self):
                self.off = 0

            def reset(self):
                self.off = 0

            def take(self, n_elems, dt, shape_str=None, **kw):
                nb = n_elems * (4 if dt == F32 else 2)
                nw = (nb + 3) // 4
                nw = (nw + 7) // 8 * 8
                v = ARENA[:, self.off:self.off + nw]
                self.off += nw
                assert self.off <= 15 * 1024, self.off
                if dt != F32:
                    v = v.bitcast(dt)
                v = v[:, 0:n_elems]
                if shape_str:
                    v = v.rearrange(shape_str, **kw)
                return v

        AR = Arena()
        PROW = AR.take(D, F32)

        sem_names = ["pe", "act", "dve", "pool", "sp"]
        sems = {n: es.enter_context(nc.semaphore("s_" + n)) for n in sem_names}
        dma_keys = [("w", s) for s in range(NRING)] + [("cv", i) for i in range(8)] + \
                   [("xin",), ("yout", 0), ("yout", 1)] + [("par", i) for i in range(4)]
        dma_sems = {k: es.enter_context(nc.semaphore("d_" + "_".join(str(a) for a in k))) for k in dma_keys}

        class Banks:
            def __init__(self):
                self.free_list = list(range(6))

            def alloc(self):
                assert self.free_list, "PSUM pool exhausted"
                return self.free_list.pop(0)

            def free(self, b):
                self.free_list.append(b)

        PS = Banks()
        ST1, ST2 = 6, 7

        def ps(b):
            return PSt[b]

        def psr(b):
            return ("ps", b)

        import os
        DBG = os.environ.get("KDBG", "")
        P.op("sp", lambda e: e.dma_start(out=ident32[:], in_=ident_d), writes=["ident32"], dma_key=("par", 0))
        P.op("sp", lambda e: e.dma_start(out=tri32[:], in_=tri_d), writes=["tri32"], dma_key=("par", 1))
        P.op("dve", lambda e: e.tensor_copy(out=identb[:], in_=ident32[:]), reads=["ident32"], writes=["identb"])
        P.op("dve", lambda e: e.tensor_copy(out=trib[:], in_=tri32[:]), reads=["tri32"], writes=["trib"])
        P.op("dve", lambda e: e.memset(ones32[:], 1.0), writes=["ones32"])
        P.op("dve", lambda e: e.memset(EPS_LN[:], LN_EPS), writes=["EPS"])
        P.op("dve", lambda e: e.memset(EPS_RMS[:], RMS_EPS), writes=["EPS"])
        P.op("dve", lambda e: e.memset(onesb[:], 1.0), writes=["onesb"])
        P.op("dve", lambda e: e.memset(PROW[:], 0.0), writes=["PROW"])

        def prow_dma(dst_rows, src, key_i):
            P.op("sp", lambda e: e.dma_start(out=dst_rows, in_=src), reads=[], writes=["PROW"],
                 dma_key=("par", key_i))

        r = 0
        for j in range(2 if "noparam" not in DBG else 0):
            base = j * CONV_ROWS
            prow_dma(PROW[base:base + 3, :], dr["conv_b_in"][j].rearrange("(a n) -> a n", a=3), 2)
            prow_dma(PROW[base + 3:base + 34, :], dr["conv_w_dw"][j], 2)
            prow_dma(PROW[base + 34:base + 35, :], dr["conv_b_dw"][j:j + 1, :], 2)
            prow_dma(PROW[base + 35:base + 36, :], dr["conv_norm_g"][j:j + 1, :], 2)
            prow_dma(PROW[base + 36:base + 37, :], dr["conv_norm_b"][j:j + 1, :], 2)
        LNB = 2 * CONV_ROWS
        for l in range(4 if "noparam" not in DBG else 0):
            prow_dma(PROW[LNB + 2 * l:LNB + 2 * l + 1, :], dr["post_ln_g"][l:l + 1, :], 2)
            prow_dma(PROW[LNB + 2 * l + 1:LNB + 2 * l + 2, :], dr["post_ln_b"][l:l + 1, :], 2)
        NROWS = LNB + 8
        P.op("sp", lambda e: e.dma_start(out=PROW2[:], in_=dr["gla_norm_g"].rearrange("l (a n) -> (l a) n", a=2)),
             writes=["PROW2"], dma_key=("par", 3))
        for i in range(KT if "nopt" not in DBG else 0):
            b = PS.alloc()
            P.op("pe", lambda e, i=i, b=b: e.transpose(out=ps(b)[:, 0:NROWS], in_=PROW[0:NROWS, i * 128:(i + 1) * 128],
                                                       identity=ident32[0:NROWS, 0:NROWS]),
                 reads=["PROW", "ident32"], writes=[psr(b)])
            P.op("act", lambda e, i=i, b=b: e.copy(out=PT[:, i, 0:NROWS], in_=ps(b)[:, 0:NROWS]),
                 reads=[psr(b)], writes=["PT"])
            PS.free(b)
        b = PS.alloc()
        P.op("pe", lambda e, b=b: e.transpose(out=ps(b)[:, 0:4], in_=PROW2[0:4, :], identity=ident32[0:4, 0:4]),
             reads=["PROW2", "ident32"], writes=[psr(b)])
        P.op("act", lambda e, b=b: e.copy(out=GN[:], in_=ps(b)[:, 0:4]), reads=[psr(b)], writes=["GN"])
        PS.free(b)
        if "nosmall" in DBG:
            class _N:
                def op(self, *a, **k): pass
            P_ = P; P = _N()
        P.op("pool", lambda e: e.dma_start(out=WA2[:], in_=dr["gla_w_a2"].rearrange("l r n -> r l n")),
             writes=["WA2", ("cvslot", 0)], dma_key=("cv", 0))
        P.op("pool", lambda e: e.dma_start(out=BAROW[:], in_=dr["gla_b_a"].rearrange("(o l) n -> o l n", o=1)),
             writes=["BAROW", ("cvslot", 1)], dma_key=("cv", 1))
        P.op("pool", lambda e: e.dma_start(out=BOROW[:], in_=dr["conv_b_out"].rearrange("(o l) n -> o l n", o=1)),
             writes=["BOROW", ("cvslot", 2)], dma_key=("cv", 2))
        P.op("pool", lambda e: e.dma_start(
            out=WSM[:], in_=dr["gla_w_in"][:, :, 3072:3088].rearrange("l (kt p) n -> p l kt n", p=128)),
            writes=["WSM", ("cvslot", 3)], dma_key=("cv", 3))
        if "nosmall" in DBG:
            P = P_
        for g, (l, nm, j, c0) in enumerate(groups):
            src = dr[nm][j][:, c0:c0 + 512].rearrange("(kt p) n -> p kt n", p=128)
            dst = wscr[g].rearrange("p (kt n) -> p kt n", kt=KT)
            P.op("pool", lambda e, src=src, dst=dst: e.dma_start(out=dst, in_=src),
                 writes=[("wscr", g), ("cvslot", g % 8)], dma_key=("cv", g % 8))

        tiles = []
        for s in range(nseq):
            tiles.append((s, 0, NMETA, True))
            for t in range(seq // TT):
                tiles.append((s, t * TT, TT, False))
        wseq = []
        for _ in tiles:
            for g in range(NG):
                wseq.append(g)

        class Ring:
            def __init__(self):
                self.next_load = 0
                self.next_use = 0

            def issue(self):
                if self.next_load >= len(wseq):
                    return
                n = self.next_load
                g = wseq[n]
                slot = n % NRING
                self.next_load += 1
                P.op("sp", lambda e, g=g, slot=slot: e.dma_start(out=WR[:, slot, :], in_=wscr[g]),
                     reads=[("wscr", g)], writes=[("w", slot)], dma_key=("w", slot))

            def get(self):
                n = self.next_use
                self.next_use += 1
                assert n < self.next_load
                return n % NRING

            def release(self, slot):
                self.issue()

        RG = Ring()
        for _ in range(NRING):
            RG.issue()

        def wv(slot):
            return WR[:, slot, :].rearrange("p (kt n) -> p kt n", kt=KT)

        def load_x(tile):
            s, t0, T, is_meta = tile
            if is_meta:
                P.op("sp", lambda e: e.dma_start(out=XIN[0:NMETA, 0, :], in_=meta), writes=["XIN"], dma_key=("xin",))
            else:
                P.op("sp", lambda e, s=s, t0=t0: e.dma_start(
                    out=XIN[:, :, :], in_=x[s, t0:t0 + TT, :].rearrange("(nb p) d -> p nb d", p=128)),
                    writes=["XIN"], dma_key=("xin",))

        def blocks(T):
            return [(b, min(128, T - b * 128)) for b in range((T + 127) // 128)]

        def transpose_in(T):
            for i in range(KT):
                b = PS.alloc()
                for (blk, tb) in blocks(T):
                    if "onetr" in DBG and blk > 0:
                        continue
                    if "nope" in DBG:
                        continue
                    P.op("pe", lambda e, i=i, b=b, blk=blk, tb=tb: e.transpose(
                        out=ps(b)[:, blk * 128:blk * 128 + tb], in_=XIN[0:tb, blk, i * 128:(i + 1) * 128],
                        identity=ident32[0:tb, 0:tb]), reads=["XIN", "ident32"], writes=[psr(b)])
                if "noact" not in DBG:
                    P.op("act", lambda e, i=i, b=b: e.copy(out=X[:, i, 0:T], in_=ps(b)[:, 0:T]),
                         reads=[psr(b)], writes=[("X", i)])
                if "nodve" not in DBG:
                    P.op("dve", lambda e, i=i, b=b: e.tensor_copy(out=Xb[:, i, 0:T], in_=X[:, i, 0:T]),
                         reads=[("X", i)], writes=[("Xb", i)])
                PS.free(b)

        def transpose_out(tile, cnt):
            s, t0, T, _ = tile
            for (blk, tb) in blocks(T):
                rot = cnt[0] % 2
                cnt[0] += 1
                ba, bb = PS.alloc(), PS.alloc()
                for i in range(KT):
                    bk = ba if i < 4 else bb
                    P.op("pe", lambda e, i=i, bk=bk, blk=blk: e.transpose(
                        out=ps(bk)[:, (i % 4) * 128:(i % 4 + 1) * 128], in_=X[:, i, blk * 128:(blk + 1) * 128],
                        identity=ident32[:, :]), reads=[("X", i), "ident32"], writes=[psr(bk)])
                P.op("act", lambda e, rot=rot, ba=ba: e.copy(out=YOUT[:, rot, 0:512], in_=ps(ba)[:, :]),
                     reads=[psr(ba)], writes=[("YOUT", rot, 0)])
                P.op("dve", lambda e, rot=rot, bb=bb: e.tensor_copy(out=YOUT[:, rot, 512:1024], in_=ps(bb)[:, :]),
                     reads=[psr(bb)], writes=[("YOUT", rot, 1)])
                PS.free(ba)
                PS.free(bb)
                P.op("sp", lambda e, rot=rot, s=s, t0=t0, blk=blk: e.dma_start(
                    out=y[s, t0 + blk * 128:t0 + (blk + 1) * 128, :], in_=YOUT[:, rot, :]),
                    reads=[("YOUT", rot, 0), ("YOUT", rot, 1)], dma_key=("yout", rot))

        def proj_fm(bank, slot, c0, T, extra=None):
            W = wv(slot)

            def fn(e):
                ins = None
                for k in range(KT):
                    ins = e.matmul(ps(bank)[:, 0:T], W[:, k, c0:c0 + 128], Xb[:, k, 0:T],
                                   start=(k == 0), stop=(k == KT - 1 and extra is None))
                if extra is not None:
                    ins = extra(e)
                return ins
            P.op("pe", fn, reads=[("w", slot)] + [("Xb", k) for k in range(KT)], writes=[psr(bank)])

        def stats_mm(src_ap, sq_ap, first, last, reads):
            def fn(e):
                e.matmul(ps(ST1)[:, 0:src_ap.shape[-1]], ones32[:, :], src_ap, start=first, stop=last)
                return e.matmul(ps(ST2)[:, 0:src_ap.shape[-1]], onesb[:, 0:128], sq_ap, start=first, stop=last)
            P.op("pe", fn, reads=reads + ["ones32", "onesb"], writes=[psr(ST1), psr(ST2)])

        def stats_finish(T, eps):
            P.op("dve", lambda e: e.tensor_scalar(out=MEAN[:, 0:T], in0=ps(ST1)[:, 0:T], scalar1=1.0 / D, scalar2=None,
                                                  op0=ALU.mult), reads=[psr(ST1)], writes=["MEAN"])
            P.op("dve", lambda e: e.tensor_tensor(out=MSQ[:, 0:T], in0=MEAN[:, 0:T], in1=MEAN[:, 0:T], op=ALU.mult),
                 reads=["MEAN"], writes=["MSQ"])
            P.op("dve", lambda e: e.scalar_tensor_tensor(out=MSQ[:, 0:T], in0=ps(ST2)[:, 0:T], scalar=1.0 / D,
                                                         in1=MSQ[:, 0:T], op0=ALU.mult, op1=ALU.subtract),
                 reads=[psr(ST2), "MSQ"], writes=["MSQ"])
            P.op("act", lambda e: e.activation(out=MSQ[:, 0:T], in_=MSQ[:, 0:T], func=AF.Sqrt, bias=EPS_LN[:, 0:1]),
                 reads=["MSQ", "EPS"], writes=["MSQ"])
            P.op("dve", lambda e: e.reciprocal(out=RSTD[:, 0:T], in_=MSQ[:, 0:T]), reads=["MSQ"], writes=["RSTD"])

        def out_proj_and_post_ln(l, T, slots, bias_row):
            pending = []
            for i in range(KT):
                slot = slots[i // 4]
                c0 = (i % 4) * 128
                W = wv(slot)
                b = PS.alloc()

                def fn(e, W=W, c0=c0, b=b, i=i):
                    ins = None
                    for j in range(KT):
                        ins = e.matmul(ps(b)[:, 0:T], W[:, j, c0:c0 + 128], M[:, j, 0:T], start=(j == 0),
                                       stop=(j == KT - 1 and bias_row is None))
                    if bias_row is not None:
                        ins = e.matmul(ps(b)[:, 0:T], bias_row[0:1, i * 128:(i + 1) * 128], onesb[0:1, 0:T],
                                       start=False, stop=True)
                    return ins
                P.op("pe", fn, reads=[("w", slot), "BOROW", "onesb"] + [("M", j) for j in range(KT)], writes=[psr(b)])
                if i % 4 == 3:
                    RG.release(slot)
                P.op("dve", lambda e, i=i, b=b: e.scalar_tensor_tensor(
                    out=X[:, i, 0:T], in0=X[:, i, 0:T], scalar=float(ALPHA), in1=ps(b)[:, 0:T],
                    op0=ALU.mult, op1=ALU.add), reads=[psr(b), ("X", i)], writes=[("X", i)])
                PS.free(b)
                rot = i % 2
                P.op("act", lambda e, i=i, rot=rot: e.activation(out=SQ[:, rot, 0:T], in_=X[:, i, 0:T], func=AF.Square),
                     reads=[("X", i)], writes=[("SQ", rot)])
                pending.append((i, rot))
                if len(pending) > 1:
                    pi, prot = pending.pop(0)
                    stats_mm(X[:, pi, 0:T], SQ[:, prot, 0:T], pi == 0, False, [("X", pi), ("SQ", prot)])
            pi, prot = pending.pop(0)
            stats_mm(X[:, pi, 0:T], SQ[:, prot, 0:T], False, True, [("X", pi), ("SQ", prot)])
            stats_finish(T, LN_EPS)
            gcol = LNB + 2 * l
            for i in range(KT):
                rot = i % 2
                P.op("dve", lambda e, i=i, rot=rot: e.tensor_tensor(out=T1[:, rot, 0:T], in0=X[:, i, 0:T],
                                                                     in1=MEAN[:, 0:T], op=ALU.subtract),
                     reads=[("X", i), "MEAN"], writes=[("T1", rot)])
                P.op("dve", lambda e, rot=rot: e.tensor_tensor(out=T1[:, rot, 0:T], in0=T1[:, rot, 0:T],
                                                               in1=RSTD[:, 0:T], op=ALU.mult),
                     reads=[("T1", rot), "RSTD"], writes=[("T1", rot)])
                P.op("act", lambda e, i=i, rot=rot: e.activation(
                    out=X[:, i, 0:T], in_=T1[:, rot, 0:T], func=AF.Identity,
                    scale=PT[:, i, gcol:gcol + 1], bias=PT[:, i, gcol + 1:gcol + 2]),
                    reads=[("T1", rot), "PT"], writes=[("X", i)])
                P.op("act", lambda e, i=i, rot=rot: e.activation(
                    out=Xb[:, i, 0:T], in_=T1[:, rot, 0:T], func=AF.Identity,
                    scale=PT[:, i, gcol:gcol + 1], bias=PT[:, i, gcol + 1:gcol + 2]),
                    reads=[("T1", rot), "PT"], writes=[("Xb", i)])

        def conv_layer(l, T, first):
            lc = l // 2
            base = lc * CONV_ROWS
            AR.reset()
            GLU = AR.take(KT * (TT + HALO), BF16, "p (k t) -> p k t", k=KT)
            ZS = AR.take(KT * TT, BF16, "p (k t) -> p k t", k=KT)
            C = AR.take(KT * TT, F32, "p (k t) -> p k t", k=KT)
            SIG = AR.take(2 * TT, BF16, "p (k t) -> p k t", k=2)
            CS = AR.take(2 * TT, BF16, "p (k t) -> p k t", k=2)
            DG = AR.take(2 * TAPS * 128, BF16, "p (r t c) -> p r t c", r=2, t=TAPS)
            if first:
                P.op("pool", lambda e: e.memset(HAL[:, lc, :, :], 0.0), writes=[("HAL", lc)])
            P.op("pool", lambda e: e.tensor_copy(out=GLU[:, :, 0:HALO], in_=HAL[:, lc, :, :]),
                 reads=[("HAL", lc)], writes=[("GLU", j) for j in range(KT)])
            slots = {}

            def proj(j):
                if j % 4 == 0:
                    slots["a"], slots["g"], slots["z"] = RG.get(), RG.get(), RG.get()
                c0 = (j % 4) * 128
                pa, pg, pz = PS.alloc(), PS.alloc(), PS.alloc()
                proj_fm(pg, slots["g"], c0, T)
                proj_fm(pa, slots["a"], c0, T)
                proj_fm(pz, slots["z"], c0, T)
                if j % 4 == 3:
                    RG.release(slots["a"]); RG.release(slots["g"]); RG.release(slots["z"])
                rot = j % 2
                P.op("act", lambda e: e.activation(out=SIG[:, rot, 0:T], in_=ps(pg)[:, 0:T], func=AF.Sigmoid,
                                                   bias=PT[:, j, base + 1:base + 2]),
                     reads=[psr(pg), "PT"], writes=[("SIG", rot)])
                P.op("dve", lambda e: e.scalar_tensor_tensor(
                    out=GLU[:, j, HALO:HALO + T], in0=ps(pa)[:, 0:T], scalar=PT[:, j, base:base + 1],
                    in1=SIG[:, rot, 0:T], op0=ALU.add, op1=ALU.mult),
                    reads=[psr(pa), ("SIG", rot), "PT"], writes=[("GLU", j)])
                P.op("act", lambda e: e.activation(out=ZS[:, j, 0:T], in_=ps(pz)[:, 0:T], func=AF.Silu,
                                                   bias=PT[:, j, base + 2:base + 3]),
                     reads=[psr(pz), "PT"], writes=[("ZS", j)])
                PS.free(pg); PS.free(pa); PS.free(pz)

                def dg_dve(e):
                    ins = None
                    for tap in range(TAPS):
                        ins = e.tensor_scalar(out=DG[:, rot, tap, :], in0=identb[:, :],
                                              scalar1=PT[:, j, base + 3 + tap:base + 4 + tap], scalar2=None,
                                              op0=ALU.mult)
                    return ins

                def dg_act(e):
                    ins = None
                    for tap in range(TAPS):
                        ins = e.activation(out=DG[:, rot, tap, :], in_=identb[:, :], func=AF.Copy,
                                           scale=PT[:, j, base + 3 + tap:base + 4 + tap])
                    return ins
                if j % 2 == 0:
                    P.op("dve", dg_dve, reads=["identb", "PT"], writes=[("DG", rot)])
                else:
                    P.op("act", dg_act, reads=["identb", "PT"], writes=[("DG", rot)])

            def conv(j):
                rot = j % 2
                pc = PS.alloc()

                def fn(e):
                    ins = None
                    for tap in range(TAPS):
                        ins = e.matmul(ps(pc)[:, 0:T], DG[:, rot, tap, :], GLU[:, j, tap:tap + T],
                                       start=(tap == 0), stop=(tap == TAPS - 1))
                    return ins
                P.op("pe", fn, reads=[("DG", rot), ("GLU", j)], writes=[psr(pc)])
                P.op("act", lambda e: e.activation(out=C[:, j, 0:T], in_=ps(pc)[:, 0:T], func=AF.Identity,
                                                   bias=PT[:, j, base + 34:base + 35]),
                     reads=[psr(pc), "PT"], writes=[("C", j)])
                P.op("act", lambda e: e.activation(out=SQ[:, rot, 0:T], in_=ps(pc)[:, 0:T], func=AF.Square,
                                                   bias=PT[:, j, base + 34:base + 35]),
                     reads=[psr(pc), "PT"], writes=[("SQ", rot)])
                PS.free(pc)

            def stats(j):
                stats_mm(C[:, j, 0:T], SQ[:, j % 2, 0:T], j == 0, j == KT - 1, [("C", j), ("SQ", j % 2)])

            for j in range(KT + 2):
                if j < KT:
                    proj(j)
                if 1 <= j <= KT:
                    conv(j - 1)
                if j >= 2:
                    stats(j - 2)
            P.op("pool", lambda e: e.tensor_copy(out=HAL[:, lc, :, :], in_=GLU[:, :, T:T + HALO]),
                 reads=[("GLU", j) for j in range(KT)], writes=[("HAL", lc)])
            stats_finish(T, LN_EPS)
            for j in range(KT):
                rot = j % 2
                P.op("dve", lambda e, j=j, rot=rot: e.tensor_tensor(out=T1[:, rot, 0:T], in0=C[:, j, 0:T],
                                                                     in1=MEAN[:, 0:T], op=ALU.subtract),
                     reads=[("C", j), "MEAN"], writes=[("T1", rot)])
                P.op("dve", lambda e, rot=rot: e.tensor_tensor(out=T1[:, rot, 0:T], in0=T1[:, rot, 0:T],
                                                               in1=RSTD[:, 0:T], op=ALU.mult),
                     reads=[("T1", rot), "RSTD"], writes=[("T1", rot)])
                P.op("act", lambda e, j=j, rot=rot: e.activation(
                    out=CS[:, rot, 0:T], in_=T1[:, rot, 0:T], func=AF.Silu,
                    scale=PT[:, j, base + 35:base + 36], bias=PT[:, j, base + 36:base + 37]),
                    reads=[("T1", rot), "PT"], writes=[("CS", rot)])
                P.op("dve", lambda e, j=j, rot=rot: e.tensor_tensor(out=M[:, j, 0:T], in0=CS[:, rot, 0:T],
                                                                     in1=ZS[:, j, 0:T], op=ALU.mult),
                     reads=[("CS", rot), ("ZS", j)], writes=[("M", j)])
            so = [RG.get(), RG.get()]
            out_proj_and_post_ln(l, T, so, BOROW[:, lc, :])

        def gla_layer(l, T, first):
            lg = l // 2
            AR.reset()
            A1 = AR.take(TT, BF16)
            G = AR.take(4 * 512, F32, "p (b n) -> p b n", b=4)
            E = AR.take(1 * 512, F32, "p (r n) -> p r n", r=1)
            EB = AR.take(4 * TT, F32, "p (h t) -> p h t", h=4)
            ENB = AR.take(1 * TT, F32, "p (r t) -> p r t", r=1)
            Q = AR.take(4 * TT, BF16, "p (h t) -> p h t", h=4)
            Kt = AR.take(4 * TT, BF16, "p (h t) -> p h t", h=4)
            KTm = AR.take(4 * 512, BF16, "p (b n) -> p b n", b=4)
            V = AR.take(4 * D, BF16, "p (b n) -> p b n", b=4)
            RS = AR.take(KT * TT, BF16, "p (k t) -> p k t", k=KT)
            AT = AR.take(4 * 128, BF16, "p (r t) -> p r t", r=4)
            TMP = AR.take(4 * 256, F32, "p (r t) -> p r t", r=4)
            ON = AR.take(2 * D, BF16, "p (r t) -> p r t", r=2)
            SS = AR.take(16, F32)
            RH = AR.take(16, F32)
            JUNK = AR.take(256, BF16)
            blks = blocks(T)
            if first:
                P.op("pool", lambda e: e.memset(S32[:, lg, :, :], 0.0), writes=[("S32", lg, h) for h in range(4)])
                P.op("pool", lambda e: e.memset(Sb[:, lg, :, :], 0.0), writes=[("Sb", lg, h) for h in range(4)])
            sq, sk = RG.get(), RG.get()
            b = PS.alloc()

            def fa1(e):
                ins = None
                for k in range(KT):
                    ins = e.matmul(ps(b)[0:16, 0:T], WSM[:, lg, k, :], Xb[:, k, 0:T], start=(k == 0), stop=(k == KT - 1))
                return ins
            P.op("pe", fa1, reads=["WSM"] + [("Xb", k) for k in range(KT)], writes=[psr(b)])
            P.op("dve", lambda e, b=b: e.tensor_copy(out=A1[0:16, 0:T], in_=ps(b)[0:16, 0:T]), reads=[psr(b)], writes=["A1"])
            PS.free(b)
            for (blk, tb) in blks:
                b = PS.alloc()
                rot = 0

                def fg(e, b=b, blk=blk, tb=tb):
                    e.matmul(ps(b)[0:tb, :], A1[0:16, blk * 128:blk * 128 + tb], WA2[0:16, lg, :], start=True, stop=False)
                    return e.matmul(ps(b)[0:tb, :], onesb[0:1, 0:tb], BAROW[0:1, lg, :], start=False, stop=True)
                P.op("pe", fg, reads=["A1", "WA2", "BAROW", "onesb"], writes=[psr(b)])
                P.op("act", lambda e, b=b, tb=tb, rot=rot: e.activation(out=E[0:tb, rot, :], in_=ps(b)[0:tb, :],
                                                                        func=AF.Exp, scale=-1.0),
                     reads=[psr(b)], writes=[("E", rot)])
                PS.free(b)
                P.op("act", lambda e, blk=blk, tb=tb, rot=rot: e.activation(out=G[0:tb, blk, :], in_=E[0:tb, rot, :],
                                                                            func=AF.Ln, bias=1.0),
                     reads=[("E", rot)], writes=[("G", blk)])
            for h in range(4):
                bbc, bq, bk = PS.alloc(), PS.alloc(), PS.alloc()

                def fbc(e, h=h, bbc=bbc):
                    ins = None
                    for (blk, tb) in blks:
                        ins = e.matmul(ps(bbc)[:, blk * 128:blk * 128 + tb], G[0:tb, blk, h * 128:(h + 1) * 128],
                                       tri32[0:tb, 0:tb], start=True, stop=True)
                    return ins
                P.op("pe", fbc, reads=[("G", blk) for (blk, _) in blks] + ["tri32"], writes=[psr(bbc)])
                proj_fm(bq, sq, h * 128, T)
                proj_fm(bk, sk, h * 128, T)
                rot = 0
                P.op("act", lambda e, h=h, bbc=bbc: e.activation(out=EB[:, h, 0:T], in_=ps(bbc)[:, 0:T], func=AF.Exp,
                                                                 scale=-1.0 / 16.0),
                     reads=[psr(bbc)], writes=[("EB", h)])
                P.op("act", lambda e, rot=rot, bbc=bbc: e.activation(out=ENB[:, rot, 0:T], in_=ps(bbc)[:, 0:T],
                                                                     func=AF.Exp, scale=1.0 / 16.0),
                     reads=[psr(bbc)], writes=[("ENB", rot)])
                P.op("dve", lambda e, h=h, bq=bq: e.scalar_tensor_tensor(
                    out=Q[:, h, 0:T], in0=ps(bq)[:, 0:T], scalar=float(128 ** -0.5), in1=EB[:, h, 0:T],
                    op0=ALU.mult, op1=ALU.mult), reads=[psr(bq), ("EB", h)], writes=[("Q", h)])
                P.op("dve", lambda e, h=h, bk=bk, rot=rot: e.tensor_tensor(
                    out=Kt[:, h, 0:T], in0=ps(bk)[:, 0:T], in1=ENB[:, rot, 0:T], op=ALU.mult),
                    reads=[psr(bk), ("ENB", rot)], writes=[("Kt", h)])
                PS.free(bbc); PS.free(bq); PS.free(bk)
            RG.release(sq); RG.release(sk)
            sr = [RG.get(), RG.get()]
            for j in range(KT):
                b = PS.alloc()
                proj_fm(b, sr[j // 4], (j % 4) * 128, T)
                if j % 4 == 3:
                    RG.release(sr[j // 4])
                P.op("act", lambda e, j=j, b=b: e.activation(out=RS[:, j, 0:T], in_=ps(b)[:, 0:T], func=AF.Silu),
                     reads=[psr(b)], writes=[("RS", j)])
                PS.free(b)
                gc = lg * 2 + (j % 2)
                P.op("dve", lambda e, j=j, gc=gc: e.tensor_scalar(out=RS[:, j, 0:T], in0=RS[:, j, 0:T],
                                                                   scalar1=GN[:, gc:gc + 1], scalar2=None, op0=ALU.mult),
                     reads=[("RS", j), "GN"], writes=[("RS", j)])
            sv = [RG.get(), RG.get()]
            for half in range(2):
                W = wv(sv[half])
                for (blk, tb) in blks:
                    b = PS.alloc()

                    def fv(e, W=W, b=b, blk=blk, tb=tb):
                        ins = None
                        for k in range(KT):
                            ins = e.matmul(ps(b)[0:tb, :], Xb[:, k, blk * 128:blk * 128 + tb], W[:, k, :],
                                           start=(k == 0), stop=(k == KT - 1))
                        return ins
                    P.op("pe", fv, reads=[("w", sv[half])] + [("Xb", k) for k in range(KT)], writes=[psr(b)])
                    eng = "act" if (blk + half) % 2 == 0 else "dve"
                    if eng == "act":
                        P.op("act", lambda e, b=b, blk=blk, tb=tb, half=half: e.copy(
                            out=V[0:tb, blk, half * 512:(half + 1) * 512], in_=ps(b)[0:tb, :]),
                            reads=[psr(b)], writes=[("V", blk, half)])
                    else:
                        P.op("dve", lambda e, b=b, blk=blk, tb=tb, half=half: e.tensor_copy(
                            out=V[0:tb, blk, half * 512:(half + 1) * 512], in_=ps(b)[0:tb, :]),
                            reads=[psr(b)], writes=[("V", blk, half)])
                    PS.free(b)
                RG.release(sv[half])
            for (blk, tb) in blks:
                b = PS.alloc()
                pb = ps(b)[:, :].bitcast(BF16)

                def fkt(e, pb=pb, blk=blk, tb=tb):
                    ins = None
                    for h in range(4):
                        ins = e.transpose(out=pb[0:tb, h * 128:(h + 1) * 128], in_=Kt[:, h, blk * 128:blk * 128 + tb],
                                          identity=identb[:, :])
                    return ins
                P.op("pe", fkt, reads=[("Kt", h) for h in range(4)] + ["identb"], writes=[psr(b)])
                P.op("dve", lambda e, pb=pb, blk=blk, tb=tb: e.tensor_copy(out=KTm[0:tb, blk, :], in_=pb[0:tb, 0:512]),
                     reads=[psr(b)], writes=[("KTm", blk)])
                PS.free(b)
            def chunk(blk, tb):
                t0 = blk * 128
                orot = blk % 2
                pat = PS.alloc()

                def fat(e):
                    ins = None
                    for h in range(4):
                        ins = e.matmul(ps(pat)[0:tb, h * 128:h * 128 + tb], Kt[:, h, t0:t0 + tb], Q[:, h, t0:t0 + tb],
                                       start=True, stop=True)
                    return ins
                P.op("pe", fat, reads=[("Kt", h) for h in range(4)] + [("Q", h) for h in range(4)], writes=[psr(pat)])
                for h in range(4):
                    P.op("dve", lambda e, h=h: e.tensor_tensor(out=AT[0:tb, h, 0:tb], in0=ps(pat)[0:tb, h * 128:h * 128 + tb],
                                                                in1=trib[0:tb, 0:tb], op=ALU.mult),
                         reads=[psr(pat), "trib"], writes=[("AT", h)])
                PS.free(pat)
                pkv = [PS.alloc(), PS.alloc()]
                for h in range(4):
                    P.op("pe", lambda e, h=h: e.matmul(ps(pkv[h // 2])[:, (h % 2) * 256:(h % 2) * 256 + 256],
                                                       KTm[0:tb, blk, h * 128:(h + 1) * 128],
                                                       V[0:tb, blk, h * 256:(h + 1) * 256], start=True, stop=True),
                         reads=[("KTm", blk), ("V", blk, h // 2)], writes=[psr(pkv[h // 2])])
                po = [PS.alloc(), PS.alloc()]
                for h in range(4):
                    pob = po[h // 2]
                    oc = (h % 2) * 256
                    P.op("pe", lambda e, h=h, pob=pob, oc=oc: e.matmul(
                        ps(pob)[0:tb, oc:oc + 256], AT[0:tb, h, 0:tb], V[0:tb, blk, h * 256:(h + 1) * 256],
                        start=True, stop=False), reads=[("AT", h), ("V", blk, h // 2)], writes=[psr(pob)])
                    P.op("pe", lambda e, h=h, pob=pob, oc=oc: e.matmul(
                        ps(pob)[0:tb, oc:oc + 256], Q[:, h, t0:t0 + tb], Sb[:, lg, h, :],
                        start=False, stop=True), reads=[("Q", h), ("Sb", lg, h)], writes=[psr(pob)])
                for h in range(4):
                    el = EB[:, h, t0 + tb - 1:t0 + tb]
                    kc = (h % 2) * 256
                    P.op("act", lambda e, h=h, el=el, kc=kc: e.activation(out=TMP[:, h, :], in_=ps(pkv[h // 2])[:, kc:kc + 256],
                                                                          func=AF.Copy, scale=el),
                         reads=[psr(pkv[h // 2]), ("EB", h)], writes=[("TMP", h)])
                    P.op("dve", lambda e, h=h, el=el: e.scalar_tensor_tensor(
                        out=S32[:, lg, h, :], in0=S32[:, lg, h, :], scalar=el, in1=TMP[:, h, :],
                        op0=ALU.mult, op1=ALU.add), reads=[("S32", lg, h), ("TMP", h), ("EB", h)],
                        writes=[("S32", lg, h)])
                    P.op("dve", lambda e, h=h: e.tensor_copy(out=Sb[:, lg, h, :], in_=S32[:, lg, h, :]),
                         reads=[("S32", lg, h)], writes=[("Sb", lg, h)])
                PS.free(pkv[0]); PS.free(pkv[1])
                for h in range(4):
                    pob = po[h // 2]
                    oc = (h % 2) * 256
                    P.op("act", lambda e, h=h, pob=pob, oc=oc: e.activation(
                        out=JUNK[0:tb, :], in_=ps(pob)[0:tb, oc:oc + 256], func=AF.Square,
                        accum_out=SS[0:tb, blk * 4 + h:blk * 4 + h + 1]),
                        reads=[psr(pob)], writes=[("SS", blk), "JUNK"])
                P.op("act", lambda e: e.activation(out=RH[0:tb, blk * 4:blk * 4 + 4], in_=SS[0:tb, blk * 4:blk * 4 + 4],
                                                   func=AF.Sqrt, scale=1.0 / 256.0, bias=EPS_RMS[0:tb, 0:1]),
                     reads=[("SS", blk), "EPS"], writes=[("RH", blk)])
                P.op("dve", lambda e: e.reciprocal(out=RH[0:tb, blk * 4:blk * 4 + 4], in_=RH[0:tb, blk * 4:blk * 4 + 4]),
                     reads=[("RH", blk)], writes=[("RH", blk)])
                for h in range(4):
                    pob = po[h // 2]
                    oc = (h % 2) * 256
                    P.op("act", lambda e, h=h, pob=pob, oc=oc: e.activation(
                        out=ON[0:tb, orot, h * 256:(h + 1) * 256], in_=ps(pob)[0:tb, oc:oc + 256], func=AF.Copy,
                        scale=RH[0:tb, blk * 4 + h:blk * 4 + h + 1]),
                        reads=[psr(pob), ("RH", blk)], writes=[("ON", orot)])
                PS.free(po[0]); PS.free(po[1])
                pt_ = PS.alloc()
                ptb = ps(pt_)[:, :].bitcast(BF16).rearrange("p (j t) -> p j t", j=KT)

                def ftr(e):
                    ins = None
                    for j in range(KT):
                        ins = e.transpose(out=ptb[:, j, 0:tb], in_=ON[0:tb, orot, j * 128:(j + 1) * 128],
                                          identity=identb[0:tb, 0:tb])
                    return ins
                P.op("pe", ftr, reads=[("ON", orot), "identb"], writes=[psr(pt_)])
                P.op("dve", lambda e: e.tensor_tensor(out=M[:, :, t0:t0 + tb], in0=ptb[:, :, 0:tb],
                                                      in1=RS[:, :, t0:t0 + tb], op=ALU.mult),
                     reads=[psr(pt_)] + [("RS", j) for j in range(KT)], writes=[("M", j) for j in range(KT)])
                PS.free(pt_)

            for (blk, tb) in blks:
                chunk(blk, tb)
            so = [RG.get(), RG.get()]
            out_proj_and_post_ln(l, T, so, None)

        ycnt = [0]
        load_x(tiles[0])
        for ti, tile in enumerate(tiles):
            s, t0, T, is_meta = tile
            if "notile" in DBG:
                break
            if "nometa" in DBG and is_meta:
                if ti + 1 < len(tiles):
                    load_x(tiles[ti + 1])
                continue
            transpose_in(T)
            if ti + 1 < len(tiles):
                load_x(tiles[ti + 1])
            for l in range(depth):
                P.barrier()
                if l % 2 == 0:
                    conv_layer(l, T, is_meta)
                else:
                    gla_layer(l, T, is_meta)
            P.barrier()
            if not is_meta and "noout" not in DBG:
                transpose_out(tile, ycnt)
        P.emit(None, sems, dma_sems, [("yout", 0), ("yout", 1)])
    return nc


_CACHE = {}


def _consts():
    ident = np.eye(128, dtype=np.float32)
    tri = np.triu(np.ones((128, 128), dtype=np.float32))
    return ident, tri


def run(inputs, nseq_per_core, ncores, depth=DEPTH_FULL, trace=False):
    x = np.ascontiguousarray(inputs["x"], dtype=np.float32)
    seq = x.shape[1]
    key = (nseq_per_core, seq, depth)
    if key not in _CACHE:
        _CACHE[key] = build(nseq_per_core, seq, depth)
    nc = _CACHE[key]
    ident, tri = _consts()
    in_maps = []
    for c in range(ncores):
        m = {k: np.ascontiguousarray(v, dtype=np.float32) for k, v in inputs.items() if k != "x"}
        m["x"] = x[c * nseq_per_core:(c + 1) * nseq_per_core]
        m["c_ident"] = ident
        m["c_tri"] = tri
        in_maps.append(m)
    res = run_bass_kernel_spmd(nc, in_maps, core_ids=list(range(ncores)), trace=trace)
    out = np.concatenate([r["y"] for r in res.results], axis=0)
    return out, res


def kernel(**inputs):
    out, _ = run(inputs, inputs["x"].shape[0] // NCORES, NCORES)
    return out
```

```python
import numpy as np
import concourse.bass as bass
import concourse.mybir as mybir
from concourse.bass_utils import run_bass_kernel_spmd

F32 = mybir.dt.float32
BF16 = mybir.dt.bfloat16
AF = mybir.ActivationFunctionType
ALU = mybir.AluOpType

D = 1024
KT = 8
NMETA = 16
TAPS = 31
HALO = 30
DEPTH_FULL = 4
ALPHA = (2 * DEPTH_FULL) ** 0.25
LN_EPS = 1e-5
RMS_EPS = 1e-6
TT = 512
NCORES = 8
CONV_ROWS = 37
NRING = 6

COMPUTE = ("pe", "act", "dve", "pool")


class Prog:
    def __init__(self, nc):
        self.nc = nc
        self.ops = {e: [] for e in ("pe", "act", "dve", "pool", "sp")}
        self.last_w = {}
        self.readers = {}
        self.dma_cnt = {}
        self.bar = {e: set() for e in self.ops}

    def op(self, eng, fn, reads=(), writes=(), dma_key=None):
        lst = self.ops[eng]
        idx = len(lst)
        deps = {}

        def add(d, kind):
            if d is None:
                return
            if d in deps and deps[d] == "raw":
                return
            deps[d] = kind

        for r in reads:
            add(self.last_w.get(r), "raw")
        for r in writes:
            add(self.last_w.get(r), "waw")
            for rd in self.readers.get(r, ()):
                add(rd, "war")
        for d in self.bar[eng]:
            add(d, "raw")
        self.bar[eng] = set()
        rec = dict(fn=fn, deps=deps, dma_key=dma_key, dma_n=None, signal=False, ordinal=None)
        if dma_key is not None:
            n = self.dma_cnt.get(dma_key, 0) + 1
            self.dma_cnt[dma_key] = n
            rec["dma_n"] = n
        lst.append(rec)
        me = (eng, idx)
        for r in reads:
            s = self.readers.setdefault(r, set())
            if dma_key is None:
                for o in [o for o in s if o[0] == eng and self.ops[eng][o[1]]["dma_key"] is None]:
                    s.discard(o)
            s.add(me)
        for r in writes:
            self.last_w[r] = me
            self.readers[r] = set()
        return me

    def barrier(self):
        lasts = set()
        for e in COMPUTE:
            for i in range(len(self.ops[e]) - 1, -1, -1):
                if self.ops[e][i]["dma_key"] is None:
                    lasts.add((e, i))
                    break
        for e in COMPUTE:
            self.bar[e] = set(d for d in lasts if d[0] != e)

    def resolve(self):
        for eng, lst in self.ops.items():
            waited = {e: -1 for e in self.ops}
            waited_dma = {}
            for idx, rec in enumerate(lst):
                waits = []
                for (pe_, pi), kind in rec["deps"].items():
                    prod = self.ops[pe_][pi]
                    if prod["dma_key"] is not None:
                        k = prod["dma_key"]
                        if waited_dma.get(k, 0) >= prod["dma_n"]:
                            continue
                        waited_dma[k] = prod["dma_n"]
                        waits.append(("dma", k, prod["dma_n"]))
                        continue
                    if pe_ == eng:
                        if eng in ("pe", "sp"):
                            continue
                    if waited[pe_] >= pi:
                        continue
                    waited[pe_] = pi
                    prod["signal"] = True
                    waits.append(("eng", pe_, pi))
                rec["waits"] = waits
        for eng, lst in self.ops.items():
            n = 0
            for rec in lst:
                if rec["signal"]:
                    n += 1
                    rec["ordinal"] = n

    def emit(self, block_ctx_factory, sems, dma_sems, final_waits):
        self.resolve()
        nc = self.nc
        P = self

        def run(eng_name):
            def body(e):
                for rec in P.ops[eng_name]:
                    for w in rec["waits"]:
                        if w[0] == "dma":
                            e.wait_ge(dma_sems[w[1]], 16 * w[2])
                        else:
                            e.wait_ge(sems[w[1]], P.ops[w[1]][w[2]]["ordinal"])
                    ins = rec["fn"](e)
                    if rec["dma_key"] is not None:
                        ins.then_inc(dma_sems[rec["dma_key"]], 16)
                    elif rec["signal"]:
                        ins.then_inc(sems[eng_name], 1)
                if eng_name == "sp":
                    for k in final_waits:
                        if P.dma_cnt.get(k, 0):
                            e.wait_ge(dma_sems[k], 16 * P.dma_cnt[k])
            return body

        with nc.Block() as block:
            block.tensor(run("pe"))
            block.scalar(run("act"))
            block.vector(run("dve"))
            block.gpsimd(run("pool"))
            block.sync(run("sp"))


def layer_groups(depth):
    groups = []
    for l in range(depth):
        j = l // 2
        if l % 2 == 0:
            for c0 in (0, 1024, 512, 1536, 2048, 2560):
                groups.append((l, "conv_w_in", j, c0))
            for c0 in (0, 512):
                groups.append((l, "conv_w_out", j, c0))
        else:
            for c0 in (0, 512, 2048, 2560, 1024, 1536):
                groups.append((l, "gla_w_in", j, c0))
            for c0 in (0, 512):
                groups.append((l, "gla_w_out", j, c0))
    return groups


def build(nseq, seq, depth):
    assert seq % TT == 0
    nc = bass.Bass("TRN2", target_bir_lowering=False)
    n_conv = (depth + 1) // 2
    n_gla = depth // 2
    dr = {}

    def din(name, shape):
        dr[name] = nc.dram_tensor(name, list(shape), F32, kind="ExternalInput").ap()
        return dr[name]

    x = din("x", (nseq, seq, D))
    meta = din("meta", (NMETA, D))
    din("conv_w_in", (2, D, 3072)); din("conv_b_in", (2, 3072)); din("conv_w_dw", (2, TAPS, D))
    din("conv_b_dw", (2, D)); din("conv_norm_g", (2, D)); din("conv_norm_b", (2, D))
    din("conv_w_out", (2, D, D)); din("conv_b_out", (2, D))
    din("gla_w_in", (2, D, 3088)); din("gla_w_a2", (2, 16, 512)); din("gla_b_a", (2, 512))
    din("gla_norm_g", (2, 256)); din("gla_w_out", (2, D, D))
    din("post_ln_g", (4, D)); din("post_ln_b", (4, D))
    ident_d = din("c_ident", (128, 128))
    tri_d = din("c_tri", (128, 128))
    y = nc.dram_tensor("y", [nseq, seq, D], F32, kind="ExternalOutput").ap()

    groups = layer_groups(depth)
    NG = len(groups)
    wscr = nc.dram_tensor("wscr", [max(NG, 1), 128, KT * 512], BF16, kind="Internal").ap()

    P = Prog(nc)
    import contextlib
    es = contextlib.ExitStack()
    with es:
        def sb(name, shape, dt):
            return es.enter_context(nc.sbuf_tensor(name, list(shape), dt))

        XIN = sb("XIN", [128, 4, D], F32)
        YOUT = sb("YOUT", [128, 2, D], F32)
        X = sb("X", [128, KT, TT], F32)
        Xb = sb("Xb", [128, KT, TT], BF16)
        WR = sb("WR", [128, NRING, KT * 512], BF16)
        WSM = sb("WSM", [128, 2, KT, 16], BF16)
        WA2 = sb("WA2", [16, 2, 512], BF16)
        BAROW = sb("BAROW", [1, 2, 512], BF16)
        BOROW = sb("BOROW", [1, 2, D], BF16)
        HAL = sb("HAL", [128, 2, KT, HALO], BF16)
        S32 = sb("S32", [128, 2, 4, 256], F32)
        Sb = sb("Sb", [128, 2, 4, 256], BF16)
        PROW2 = sb("PROW2", [4, 128], F32)
        PT = sb("PT", [128, KT, 88], F32)
        GN = sb("GN", [128, 4], F32)
        ident32 = sb("ident32", [128, 128], F32)
        tri32 = sb("tri32", [128, 128], F32)
        identb = sb("identb", [128, 128], BF16)
        trib = sb("trib", [128, 128], BF16)
        ones32 = sb("ones32", [128, 128], F32)
        onesb = sb("onesb", [128, TT], BF16)
        MEAN = sb("MEAN", [128, TT], F32)
        EPS_LN = sb("EPS_LN", [128, 1], F32)
        EPS_RMS = sb("EPS_RMS", [128, 1], F32)
        MSQ = sb("MSQ", [128, TT], F32)
        RSTD = sb("RSTD", [128, TT], F32)
        T1 = sb("T1", [128, 2, TT], F32)
        SQ = sb("SQ", [128, 2, TT], BF16)
        M = sb("M", [128, KT, TT], BF16)
        ARENA = sb("ARENA", [128, 15 * 1024], F32)
        PSt = [es.enter_context(nc.psum_tensor(f"ps{i}", [128, 512], F32)) for i in range(8)]

        class Arena:
            def __init__(self):
                self.off = 0

            def reset(self):
                self.off = 0

            def take(self, n_elems, dt, shape_str=None, **kw):
                nb = n_elems * (4 if dt == F32 else 2)
                nw = (nb + 3) // 4
                nw = (nw + 7) // 8 * 8
                v = ARENA[:, self.off:self.off + nw]
                self.off += nw
                assert self.off <= 15 * 1024, self.off
                if dt != F32:
                    v = v.bitcast(dt)
                v = v[:, 0:n_elems]
                if shape_str:
                    v = v.rearrange(shape_str, **kw)
                return v

        AR = Arena()
        PROW = AR.take(D, F32)

        sem_names = ["pe", "act", "dve", "pool", "sp"]
        sems = {n: es.enter_context(nc.semaphore("s_" + n)) for n in sem_names}
        dma_keys = [("w", s) for s in range(NRING)] + [("cv", i) for i in range(8)] + \
                   [("xin",), ("yout", 0), ("yout", 1)] + [("par", i) for i in range(4)]
        dma_sems = {k: es.enter_context(nc.semaphore("d_" + "_".join(str(a) for a in k))) for k in dma_keys}

        class Banks:
            def __init__(self):
                self.free_list = list(range(6))

            def alloc(self):
                assert self.free_list, "PSUM pool exhausted"
                return self.free_list.pop(0)

            def free(self, b):
                self.free_list.append(b)

        PS = Banks()
        ST1, ST2 = 6, 7

        def ps(b):
            return PSt[b]

        def psr(b):
            return ("ps", b)

        import os
        DBG = os.environ.get("KDBG", "")
        P.op("sp", lambda e: e.dma_start(out=ident32[:], in_=ident_d), writes=["ident32"], dma_key=("par", 0))
        P.op("sp", lambda e: e.dma_start(out=tri32[:], in_=tri_d), writes=["tri32"], dma_key=("par", 1))
        P.op("dve", lambda e: e.tensor_copy(out=identb[:], in_=ident32[:]), reads=["ident32"], writes=["identb"])
        P.op("dve", lambda e: e.tensor_copy(out=trib[:], in_=tri32[:]), reads=["tri32"], writes=["trib"])
        P.op("dve", lambda e: e.memset(ones32[:], 1.0), writes=["ones32"])
        P.op("dve", lambda e: e.memset(EPS_LN[:], LN_EPS), writes=["EPS"])
        P.op("dve", lambda e: e.memset(EPS_RMS[:], RMS_EPS), writes=["EPS"])
        P.op("dve", lambda e: e.memset(onesb[:], 1.0), writes=["onesb"])
        P.op("dve", lambda e: e.memset(PROW[:], 0.0), writes=["PROW"])

        def prow_dma(dst_rows, src, key_i):
            P.op("sp", lambda e: e.dma_start(out=dst_rows, in_=src), reads=[], writes=["PROW"],
                 dma_key=("par", key_i))

        r = 0
        for j in range(2 if "noparam" not in DBG else 0):
            base = j * CONV_ROWS
            prow_dma(PROW[base:base + 3, :], dr["conv_b_in"][j].rearrange("(a n) -> a n", a=3), 2)
            prow_dma(PROW[base + 3:base + 34, :], dr["conv_w_dw"][j], 2)
            prow_dma(PROW[base + 34:base + 35, :], dr["conv_b_dw"][j:j + 1, :], 2)
            prow_dma(PROW[base + 35:base + 36, :], dr["conv_norm_g"][j:j + 1, :], 2)
            prow_dma(PROW[base + 36:base + 37, :], dr["conv_norm_b"][j:j + 1, :], 2)
        LNB = 2 * CONV_ROWS
        for l in range(4 if "noparam" not in DBG else 0):
            prow_dma(PROW[LNB + 2 * l:LNB + 2 * l + 1, :], dr["post_ln_g"][l:l + 1, :], 2)
            prow_dma(PROW[LNB + 2 * l + 1:LNB + 2 * l + 2, :], dr["post_ln_b"][l:l + 1, :], 2)
        NROWS = LNB + 8
        P.op("sp", lambda e: e.dma_start(out=PROW2[:], in_=dr["gla_norm_g"].rearrange("l (a n) -> (l a) n", a=2)),
             writes=["PROW2"], dma_key=("par", 3))
        for i in range(KT if "nopt" not in DBG else 0):
            b = PS.alloc()
            P.op("pe", lambda e, i=i, b=b: e.transpose(out=ps(b)[:, 0:NROWS], in_=PROW[0:NROWS, i * 128:(i + 1) * 128],
                                                       identity=ident32[0:NROWS, 0:NROWS]),
                 reads=["PROW", "ident32"], writes=[psr(b)])
            P.op("act", lambda e, i=i, b=b: e.copy(out=PT[:, i, 0:NROWS], in_=ps(b)[:, 0:NROWS]),
                 reads=[psr(b)], writes=["PT"])
            PS.free(b)
        b = PS.alloc()
        P.op("pe", lambda e, b=b: e.transpose(out=ps(b)[:, 0:4], in_=PROW2[0:4, :], identity=ident32[0:4, 0:4]),
             reads=["PROW2", "ident32"], writes=[psr(b)])
        P.op("act", lambda e, b=b: e.copy(out=GN[:], in_=ps(b)[:, 0:4]), reads=[psr(b)], writes=["GN"])
        PS.free(b)
        if "nosmall" in DBG:
            class _N:
                def op(self, *a, **k): pass
            P_ = P; P = _N()
        P.op("pool", lambda e: e.dma_start(out=WA2[:], in_=dr["gla_w_a2"].rearrange("l r n -> r l n")),
             writes=["WA2", ("cvslot", 0)], dma_key=("cv", 0))
        P.op("pool", lambda e: e.dma_start(out=BAROW[:], in_=dr["gla_b_a"].rearrange("(o l) n -> o l n", o=1)),
             writes=["BAROW", ("cvslot", 1)], dma_key=("cv", 1))
        P.op("pool", lambda e: e.dma_start(out=BOROW[:], in_=dr["conv_b_out"].rearrange("(o l) n -> o l n", o=1)),
             writes=["BOROW", ("cvslot", 2)], dma_key=("cv", 2))
        P.op("pool", lambda e: e.dma_start(
            out=WSM[:], in_=dr["gla_w_in"][:, :, 3072:3088].rearrange("l (kt p) n -> p l kt n", p=128)),
            writes=["WSM", ("cvslot", 3)], dma_key=("cv", 3))
        if "nosmall" in DBG:
            P = P_
        for g, (l, nm, j, c0) in enumerate(groups):
            src = dr[nm][j][:, c0:c0 + 512].rearrange("(kt p) n -> p kt n", p=128)
            dst = wscr[g].rearrange("p (kt n) -> p kt n", kt=KT)
            P.op("pool", lambda e, src=src, dst=dst: e.dma_start(out=dst, in_=src),
                 writes=[("wscr", g), ("cvslot", g % 8)], dma_key=("cv", g % 8))

        tiles = []
        for s in range(nseq):
            tiles.append((s, 0, NMETA, True))
            for t in range(seq // TT):
                tiles.append((s, t * TT, TT, False))
        wseq = []
        for _ in tiles:
            for g in range(NG):
                wseq.append(g)

        class Ring:
            def __init__(self):
                self.next_load = 0
                self.next_use = 0

            def issue(self):
                if self.next_load >= len(wseq):
                    return
                n = self.next_load
                g = wseq[n]
                slot = n % NRING
                self.next_load += 1
                P.op("sp", lambda e, g=g, slot=slot: e.dma_start(out=WR[:, slot, :], in_=wscr[g]),
                     reads=[("wscr", g)], writes=[("w", slot)], dma_key=("w", slot))

            def get(self):
                n = self.next_use
                self.next_use += 1
                assert n < self.next_load
                return n % NRING

            def release(self, slot):
                self.issue()

        RG = Ring()
        for _ in range(NRING):
            RG.issue()

        def wv(slot):
            return WR[:, slot, :].rearrange("p (kt n) -> p kt n", kt=KT)

        DEFER = []

        def run_deferred():
            while DEFER:
                DEFER.pop(0)()

        def load_x(tile):
            s, t0, T, is_meta = tile
            if is_meta:
                P.op("sp", lambda e: e.dma_start(out=XIN[0:NMETA, 0, :], in_=meta), writes=["XIN"], dma_key=("xin",))
            else:
                P.op("sp", lambda e, s=s, t0=t0: e.dma_start(
                    out=XIN[:, :, :], in_=x[s, t0:t0 + TT, :].rearrange("(nb p) d -> p nb d", p=128)),
                    writes=["XIN"], dma_key=("xin",))

        def blocks(T):
            return [(b, min(128, T - b * 128)) for b in range((T + 127) // 128)]

        def transpose_in(T):
            for i in range(KT):
                b = PS.alloc()
                for (blk, tb) in blocks(T):
                    if "onetr" in DBG and blk > 0:
                        continue
                    if "nope" in DBG:
                        continue
                    P.op("pe", lambda e, i=i, b=b, blk=blk, tb=tb: e.transpose(
                        out=ps(b)[:, blk * 128:blk * 128 + tb], in_=XIN[0:tb, blk, i * 128:(i + 1) * 128],
                        identity=ident32[0:tb, 0:tb]), reads=["XIN", "ident32"], writes=[psr(b)])
                if "noact" not in DBG:
                    P.op("act", lambda e, i=i, b=b: e.copy(out=X[:, i, 0:T], in_=ps(b)[:, 0:T]),
                         reads=[psr(b)], writes=[("X", i)])
                if "nodve" not in DBG:
                    P.op("dve", lambda e, i=i, b=b: e.tensor_copy(out=Xb[:, i, 0:T], in_=X[:, i, 0:T]),
                         reads=[("X", i)], writes=[("Xb", i)])
                PS.free(b)

        def transpose_out(tile, cnt):
            s, t0, T, _ = tile
            for (blk, tb) in blocks(T):
                rot = cnt[0] % 2
                cnt[0] += 1
                ba, bb = PS.alloc(), PS.alloc()
                for i in range(KT):
                    bk = ba if i < 4 else bb
                    P.op("pe", lambda e, i=i, bk=bk, blk=blk: e.transpose(
                        out=ps(bk)[:, (i % 4) * 128:(i % 4 + 1) * 128], in_=X[:, i, blk * 128:(blk + 1) * 128],
                        identity=ident32[:, :]), reads=[("X", i), "ident32"], writes=[psr(bk)])
                P.op("act", lambda e, rot=rot, ba=ba: e.copy(out=YOUT[:, rot, 0:512], in_=ps(ba)[:, :]),
                     reads=[psr(ba)], writes=[("YOUT", rot, 0)])
                P.op("dve", lambda e, rot=rot, bb=bb: e.tensor_copy(out=YOUT[:, rot, 512:1024], in_=ps(bb)[:, :]),
                     reads=[psr(bb)], writes=[("YOUT", rot, 1)])
                PS.free(ba)
                PS.free(bb)
                P.op("sp", lambda e, rot=rot, s=s, t0=t0, blk=blk: e.dma_start(
                    out=y[s, t0 + blk * 128:t0 + (blk + 1) * 128, :], in_=YOUT[:, rot, :]),
                    reads=[("YOUT", rot, 0), ("YOUT", rot, 1)], dma_key=("yout", rot))

        def proj_fm(bank, slot, c0, T, extra=None):
            W = wv(slot)

            def fn(e):
                ins = None
                for k in range(KT):
                    ins = e.matmul(ps(bank)[:, 0:T], W[:, k, c0:c0 + 128], Xb[:, k, 0:T],
                                   start=(k == 0), stop=(k == KT - 1 and extra is None))
                if extra is not None:
                    ins = extra(e)
                return ins
            P.op("pe", fn, reads=[("w", slot)] + [("Xb", k) for k in range(KT)], writes=[psr(bank)])

        def stats_mm(src_ap, sq_ap, first, last, reads):
            def fn(e):
                e.matmul(ps(ST1)[:, 0:src_ap.shape[-1]], ones32[:, :], src_ap, start=first, stop=last)
                return e.matmul(ps(ST2)[:, 0:src_ap.shape[-1]], onesb[:, 0:128], sq_ap, start=first, stop=last)
            P.op("pe", fn, reads=reads + ["ones32", "onesb"], writes=[psr(ST1), psr(ST2)])

        def stats_finish(T, eps):
            P.op("dve", lambda e: e.tensor_scalar(out=MEAN[:, 0:T], in0=ps(ST1)[:, 0:T], scalar1=1.0 / D, scalar2=None,
                                                  op0=ALU.mult), reads=[psr(ST1)], writes=["MEAN"])
            P.op("dve", lambda e: e.tensor_tensor(out=MSQ[:, 0:T], in0=MEAN[:, 0:T], in1=MEAN[:, 0:T], op=ALU.mult),
                 reads=["MEAN"], writes=["MSQ"])
            P.op("dve", lambda e: e.scalar_tensor_tensor(out=MSQ[:, 0:T], in0=ps(ST2)[:, 0:T], scalar=1.0 / D,
                                                         in1=MSQ[:, 0:T], op0=ALU.mult, op1=ALU.subtract),
                 reads=[psr(ST2), "MSQ"], writes=["MSQ"])
            P.op("act", lambda e: e.activation(out=MSQ[:, 0:T], in_=MSQ[:, 0:T], func=AF.Sqrt, bias=EPS_LN[:, 0:1]),
                 reads=["MSQ", "EPS"], writes=["MSQ"])
            P.op("dve", lambda e: e.reciprocal(out=RSTD[:, 0:T], in_=MSQ[:, 0:T]), reads=["MSQ"], writes=["RSTD"])

        def out_proj_and_post_ln(l, T, slots, bias_row):
            pending = []
            for i in range(KT):
                slot = slots[i // 4]
                c0 = (i % 4) * 128
                W = wv(slot)
                b = PS.alloc()

                def fn(e, W=W, c0=c0, b=b, i=i):
                    ins = None
                    for j in range(KT):
                        ins = e.matmul(ps(b)[:, 0:T], W[:, j, c0:c0 + 128], M[:, j, 0:T], start=(j == 0),
                                       stop=(j == KT - 1 and bias_row is None))
                    if bias_row is not None:
                        ins = e.matmul(ps(b)[:, 0:T], bias_row[0:1, i * 128:(i + 1) * 128], onesb[0:1, 0:T],
                                       start=False, stop=True)
                    return ins
                P.op("pe", fn, reads=[("w", slot), "BOROW", "onesb"] + [("M", j) for j in range(KT)], writes=[psr(b)])
                if i % 4 == 3:
                    RG.release(slot)
                P.op("dve", lambda e, i=i, b=b: e.scalar_tensor_tensor(
                    out=X[:, i, 0:T], in0=X[:, i, 0:T], scalar=float(ALPHA), in1=ps(b)[:, 0:T],
                    op0=ALU.mult, op1=ALU.add), reads=[psr(b), ("X", i)], writes=[("X", i)])
                PS.free(b)
                rot = i % 2
                P.op("act", lambda e, i=i, rot=rot: e.activation(out=SQ[:, rot, 0:T], in_=X[:, i, 0:T], func=AF.Square),
                     reads=[("X", i)], writes=[("SQ", rot)])
                pending.append((i, rot))
                if len(pending) > 1:
                    pi, prot = pending.pop(0)
                    stats_mm(X[:, pi, 0:T], SQ[:, prot, 0:T], pi == 0, False, [("X", pi), ("SQ", prot)])
            pi, prot = pending.pop(0)
            stats_mm(X[:, pi, 0:T], SQ[:, prot, 0:T], False, True, [("X", pi), ("SQ", prot)])
            stats_finish(T, LN_EPS)
            gcol = LNB + 2 * l
            last = (l == depth - 1)

            def xfin(i):
                if last:
                    P.op("act", lambda e: e.activation(
                        out=X[:, i, 0:T], in_=X[:, i, 0:T], func=AF.Identity,
                        scale=PT[:, i, gcol:gcol + 1], bias=PT[:, i, gcol + 1:gcol + 2]),
                        reads=[("X", i), "PT"], writes=[("X", i)])
                else:
                    P.op("pool", lambda e: e.tensor_scalar(
                        out=X[:, i, 0:T], in0=X[:, i, 0:T], scalar1=PT[:, i, gcol:gcol + 1],
                        scalar2=PT[:, i, gcol + 1:gcol + 2], op0=ALU.mult, op1=ALU.add),
                        reads=[("X", i), "PT"], writes=[("X", i)])

            for i in range(KT + 1):
                if i < KT:
                    P.op("dve", lambda e, i=i: e.tensor_tensor(out=X[:, i, 0:T], in0=X[:, i, 0:T],
                                                                in1=MEAN[:, 0:T], op=ALU.subtract),
                         reads=[("X", i), "MEAN"], writes=[("X", i)])
                if i >= 1:
                    k = i - 1
                    P.op("dve", lambda e, k=k: e.tensor_tensor(out=X[:, k, 0:T], in0=X[:, k, 0:T],
                                                                in1=RSTD[:, 0:T], op=ALU.mult),
                         reads=[("X", k), "RSTD"], writes=[("X", k)])
                    P.op("act", lambda e, k=k: e.activation(
                        out=Xb[:, k, 0:T], in_=X[:, k, 0:T], func=AF.Identity,
                        scale=PT[:, k, gcol:gcol + 1], bias=PT[:, k, gcol + 1:gcol + 2]),
                        reads=[("X", k), "PT"], writes=[("Xb", k)])
                    if last:
                        xfin(k)
                    else:
                        DEFER.append(lambda k=k: xfin(k))

        def conv_layer(l, T, first):
            lc = l // 2
            base = lc * CONV_ROWS
            AR.reset()
            GLU = AR.take(KT * (TT + HALO), BF16, "p (k t) -> p k t", k=KT)
            ZS = AR.take(KT * TT, BF16, "p (k t) -> p k t", k=KT)
            C = AR.take(KT * TT, F32, "p (k t) -> p k t", k=KT)
            SIG = AR.take(2 * TT, BF16, "p (k t) -> p k t", k=2)
            CS = AR.take(2 * TT, BF16, "p (k t) -> p k t", k=2)
            DG = AR.take(2 * TAPS * 128, BF16, "p (r t c) -> p r t c", r=2, t=TAPS)
            if first:
                P.op("dve", lambda e: e.memset(HAL[:, lc, :, :], 0.0), writes=[("HAL", lc)])
            P.op("dve", lambda e: e.tensor_copy(out=GLU[:, :, 0:HALO], in_=HAL[:, lc, :, :]),
                 reads=[("HAL", lc)], writes=[("GLU", j) for j in range(KT)])
            slots = {}

            def proj(j):
                if j % 4 == 0:
                    slots["a"], slots["g"] = RG.get(), RG.get()
                c0 = (j % 4) * 128
                pa, pg = PS.alloc(), PS.alloc()
                proj_fm(pg, slots["g"], c0, T)
                proj_fm(pa, slots["a"], c0, T)
                if j % 4 == 3:
                    RG.release(slots["a"]); RG.release(slots["g"])
                rot = j % 2
                P.op("act", lambda e: e.activation(out=SIG[:, rot, 0:T], in_=ps(pg)[:, 0:T], func=AF.Sigmoid,
                                                   bias=PT[:, j, base + 1:base + 2]),
                     reads=[psr(pg), "PT"], writes=[("SIG", rot)])
                P.op("dve", lambda e: e.scalar_tensor_tensor(
                    out=GLU[:, j, HALO:HALO + T], in0=ps(pa)[:, 0:T], scalar=PT[:, j, base:base + 1],
                    in1=SIG[:, rot, 0:T], op0=ALU.add, op1=ALU.mult),
                    reads=[psr(pa), ("SIG", rot), "PT"], writes=[("GLU", j)])
                PS.free(pg); PS.free(pa)

                def dg_dve(e):
                    ins = None
                    for tap in range(TAPS):
                        ins = e.tensor_scalar(out=DG[:, rot, tap, :], in0=identb[:, :],
                                              scalar1=PT[:, j, base + 3 + tap:base + 4 + tap], scalar2=None,
                                              op0=ALU.mult)
                    return ins

                def dg_act(e):
                    ins = None
                    for tap in range(TAPS):
                        ins = e.activation(out=DG[:, rot, tap, :], in_=identb[:, :], func=AF.Copy,
                                           scale=PT[:, j, base + 3 + tap:base + 4 + tap])
                    return ins
                if j % 2 == 0:
                    P.op("dve", dg_dve, reads=["identb", "PT"], writes=[("DG", rot)])
                else:
                    P.op("act", dg_act, reads=["identb", "PT"], writes=[("DG", rot)])

            def conv(j):
                rot = j % 2
                pc = PS.alloc()

                def fn(e):
                    ins = None
                    for tap in range(TAPS):
                        ins = e.matmul(ps(pc)[:, 0:T], DG[:, rot, tap, :], GLU[:, j, tap:tap + T],
                                       start=(tap == 0), stop=(tap == TAPS - 1))
                    return ins
                P.op("pe", fn, reads=[("DG", rot), ("GLU", j)], writes=[psr(pc)])
                P.op("act", lambda e: e.activation(out=C[:, j, 0:T], in_=ps(pc)[:, 0:T], func=AF.Identity,
                                                   bias=PT[:, j, base + 34:base + 35]),
                     reads=[psr(pc), "PT"], writes=[("C", j)])
                P.op("act", lambda e: e.activation(out=SQ[:, rot, 0:T], in_=ps(pc)[:, 0:T], func=AF.Square,
                                                   bias=PT[:, j, base + 34:base + 35]),
                     reads=[psr(pc), "PT"], writes=[("SQ", rot)])
                PS.free(pc)

            def stats(j):
                stats_mm(C[:, j, 0:T], SQ[:, j % 2, 0:T], j == 0, j == KT - 1, [("C", j), ("SQ", j % 2)])

            for j in range(KT + 2):
                if j < KT:
                    proj(j)
                if 1 <= j <= KT:
                    conv(j - 1)
                if j >= 2:
                    stats(j - 2)
            P.op("dve", lambda e: e.tensor_copy(out=HAL[:, lc, :, :], in_=GLU[:, :, T:T + HALO]),
                 reads=[("GLU", j) for j in range(KT)], writes=[("HAL", lc)])
            def zproj(j):
                if j % 4 == 0:
                    slots["z"] = RG.get()
                pz = PS.alloc()
                proj_fm(pz, slots["z"], (j % 4) * 128, T)
                if j % 4 == 3:
                    RG.release(slots["z"])
                P.op("act", lambda e: e.activation(out=ZS[:, j, 0:T], in_=ps(pz)[:, 0:T], func=AF.Silu,
                                                   bias=PT[:, j, base + 2:base + 3]),
                     reads=[psr(pz), "PT"], writes=[("ZS", j)])
                PS.free(pz)

            for j in range(KT):
                zproj(j)
                if j == 1:
                    stats_finish(T, LN_EPS)
            for j in range(KT):
                rot = j % 2
                P.op("dve", lambda e, j=j, rot=rot: e.tensor_tensor(out=T1[:, rot, 0:T], in0=C[:, j, 0:T],
                                                                     in1=MEAN[:, 0:T], op=ALU.subtract),
                     reads=[("C", j), "MEAN"], writes=[("T1", rot)])
                P.op("dve", lambda e, rot=rot: e.tensor_tensor(out=T1[:, rot, 0:T], in0=T1[:, rot, 0:T],
                                                               in1=RSTD[:, 0:T], op=ALU.mult),
                     reads=[("T1", rot), "RSTD"], writes=[("T1", rot)])
                P.op("act", lambda e, j=j, rot=rot: e.activation(
                    out=CS[:, rot, 0:T], in_=T1[:, rot, 0:T], func=AF.Silu,
                    scale=PT[:, j, base + 35:base + 36], bias=PT[:, j, base + 36:base + 37]),
                    reads=[("T1", rot), "PT"], writes=[("CS", rot)])
                P.op("dve", lambda e, j=j, rot=rot: e.tensor_tensor(out=M[:, j, 0:T], in0=CS[:, rot, 0:T],
                                                                     in1=ZS[:, j, 0:T], op=ALU.mult),
                     reads=[("CS", rot), ("ZS", j)], writes=[("M", j)])
            so = [RG.get(), RG.get()]
            out_proj_and_post_ln(l, T, so, BOROW[:, lc, :])

        def gla_layer(l, T, first):
            lg = l // 2
            AR.reset()
            A1 = AR.take(TT, BF16)
            G = AR.take(4 * 512, F32, "p (b n) -> p b n", b=4)
            E = AR.take(1 * 512, F32, "p (r n) -> p r n", r=1)
            EB = AR.take(4 * TT, F32, "p (h t) -> p h t", h=4)
            ENB = AR.take(1 * TT, F32, "p (r t) -> p r t", r=1)
            Q = AR.take(4 * TT, BF16, "p (h t) -> p h t", h=4)
            Kt = AR.take(4 * TT, BF16, "p (h t) -> p h t", h=4)
            KTm = AR.take(4 * 512, BF16, "p (b n) -> p b n", b=4)
            V = AR.take(4 * D, BF16, "p (b n) -> p b n", b=4)
            RS = AR.take(KT * TT, BF16, "p (k t) -> p k t", k=KT)
            AT = AR.take(4 * 128, BF16, "p (r t) -> p r t", r=4)
            TMP = AR.take(4 * 256, F32, "p (r t) -> p r t", r=4)
            ON = AR.take(2 * D, BF16, "p (r t) -> p r t", r=2)
            SS = AR.take(16, F32)
            RH = AR.take(16, F32)
            JUNK = AR.take(256, BF16)
            blks = blocks(T)
            if first:
                P.op("dve", lambda e: e.memset(S32[:, lg, :, :], 0.0), writes=[("S32", lg, h) for h in range(4)])
                P.op("dve", lambda e: e.memset(Sb[:, lg, :, :], 0.0), writes=[("Sb", lg, h) for h in range(4)])
            sq, sk = RG.get(), RG.get()
            b = PS.alloc()

            def fa1(e):
                ins = None
                for k in range(KT):
                    ins = e.matmul(ps(b)[0:16, 0:T], WSM[:, lg, k, :], Xb[:, k, 0:T], start=(k == 0), stop=(k == KT - 1))
                return ins
            P.op("pe", fa1, reads=["WSM"] + [("Xb", k) for k in range(KT)], writes=[psr(b)])
            P.op("dve", lambda e, b=b: e.tensor_copy(out=A1[0:16, 0:T], in_=ps(b)[0:16, 0:T]), reads=[psr(b)], writes=["A1"])
            PS.free(b)
            for (blk, tb) in blks:
                b = PS.alloc()
                rot = 0

                def fg(e, b=b, blk=blk, tb=tb):
                    e.matmul(ps(b)[0:tb, :], A1[0:16, blk * 128:blk * 128 + tb], WA2[0:16, lg, :], start=True, stop=False)
                    return e.matmul(ps(b)[0:tb, :], onesb[0:1, 0:tb], BAROW[0:1, lg, :], start=False, stop=True)
                P.op("pe", fg, reads=["A1", "WA2", "BAROW", "onesb"], writes=[psr(b)])
                P.op("act", lambda e, b=b, tb=tb, rot=rot: e.activation(out=E[0:tb, rot, :], in_=ps(b)[0:tb, :],
                                                                        func=AF.Exp, scale=-1.0),
                     reads=[psr(b)], writes=[("E", rot)])
                PS.free(b)
                P.op("act", lambda e, blk=blk, tb=tb, rot=rot: e.activation(out=G[0:tb, blk, :], in_=E[0:tb, rot, :],
                                                                            func=AF.Ln, bias=1.0),
                     reads=[("E", rot)], writes=[("G", blk)])
            for h in range(4):
                bbc, bq, bk = PS.alloc(), PS.alloc(), PS.alloc()

                def fbc(e, h=h, bbc=bbc):
                    ins = None
                    for (blk, tb) in blks:
                        ins = e.matmul(ps(bbc)[:, blk * 128:blk * 128 + tb], G[0:tb, blk, h * 128:(h + 1) * 128],
                                       tri32[0:tb, 0:tb], start=True, stop=True)
                    return ins
                P.op("pe", fbc, reads=[("G", blk) for (blk, _) in blks] + ["tri32"], writes=[psr(bbc)])
                proj_fm(bq, sq, h * 128, T)
                proj_fm(bk, sk, h * 128, T)
                rot = 0
                P.op("act", lambda e, h=h, bbc=bbc: e.activation(out=EB[:, h, 0:T], in_=ps(bbc)[:, 0:T], func=AF.Exp,
                                                                 scale=-1.0 / 16.0),
                     reads=[psr(bbc)], writes=[("EB", h)])
                P.op("act", lambda e, rot=rot, bbc=bbc: e.activation(out=ENB[:, rot, 0:T], in_=ps(bbc)[:, 0:T],
                                                                     func=AF.Exp, scale=1.0 / 16.0),
                     reads=[psr(bbc)], writes=[("ENB", rot)])
                P.op("dve", lambda e, h=h, bq=bq: e.scalar_tensor_tensor(
                    out=Q[:, h, 0:T], in0=ps(bq)[:, 0:T], scalar=float(128 ** -0.5), in1=EB[:, h, 0:T],
                    op0=ALU.mult, op1=ALU.mult), reads=[psr(bq), ("EB", h)], writes=[("Q", h)])
                P.op("dve", lambda e, h=h, bk=bk, rot=rot: e.tensor_tensor(
                    out=Kt[:, h, 0:T], in0=ps(bk)[:, 0:T], in1=ENB[:, rot, 0:T], op=ALU.mult),
                    reads=[psr(bk), ("ENB", rot)], writes=[("Kt", h)])
                PS.free(bbc); PS.free(bq); PS.free(bk)
            RG.release(sq); RG.release(sk)
            sr = [RG.get(), RG.get()]
            for j in range(KT):
                b = PS.alloc()
                proj_fm(b, sr[j // 4], (j % 4) * 128, T)
                if j % 4 == 3:
                    RG.release(sr[j // 4])
                P.op("act", lambda e, j=j, b=b: e.activation(out=RS[:, j, 0:T], in_=ps(b)[:, 0:T], func=AF.Silu),
                     reads=[psr(b)], writes=[("RS", j)])
                PS.free(b)
                gc = lg * 2 + (j % 2)
                P.op("dve", lambda e, j=j, gc=gc: e.tensor_scalar(out=RS[:, j, 0:T], in0=RS[:, j, 0:T],
                                                                   scalar1=GN[:, gc:gc + 1], scalar2=None, op0=ALU.mult),
                     reads=[("RS", j), "GN"], writes=[("RS", j)])
            sv = [RG.get(), RG.get()]
            for half in range(2):
                W = wv(sv[half])
                for (blk, tb) in blks:
                    b = PS.alloc()

                    def fv(e, W=W, b=b, blk=blk, tb=tb):
                        ins = None
                        for k in range(KT):
                            ins = e.matmul(ps(b)[0:tb, :], Xb[:, k, blk * 128:blk * 128 + tb], W[:, k, :],
                                           start=(k == 0), stop=(k == KT - 1))
                        return ins
                    P.op("pe", fv, reads=[("w", sv[half])] + [("Xb", k) for k in range(KT)], writes=[psr(b)])
                    eng = "act" if (blk + half) % 2 == 0 else "dve"
                    if eng == "act":
                        P.op("act", lambda e, b=b, blk=blk, tb=tb, half=half: e.copy(
                            out=V[0:tb, blk, half * 512:(half + 1) * 512], in_=ps(b)[0:tb, :]),
                            reads=[psr(b)], writes=[("V", blk, half)])
                    else:
                        P.op("dve", lambda e, b=b, blk=blk, tb=tb, half=half: e.tensor_copy(
                            out=V[0:tb, blk, half * 512:(half + 1) * 512], in_=ps(b)[0:tb, :]),
                            reads=[psr(b)], writes=[("V", blk, half)])
                    PS.free(b)
                RG.release(sv[half])
            for (blk, tb) in blks:
                b = PS.alloc()
                pb = ps(b)[:, :].bitcast(BF16)

                def fkt(e, pb=pb, blk=blk, tb=tb):
                    ins = None
                    for h in range(4):
                        ins = e.transpose(out=pb[0:tb, h * 128:(h + 1) * 128], in_=Kt[:, h, blk * 128:blk * 128 + tb],
                                          identity=identb[:, :])
                    return ins
                P.op("pe", fkt, reads=[("Kt", h) for h in range(4)] + ["identb"], writes=[psr(b)])
                P.op("dve", lambda e, pb=pb, blk=blk, tb=tb: e.tensor_copy(out=KTm[0:tb, blk, :], in_=pb[0:tb, 0:512]),
                     reads=[psr(b)], writes=[("KTm", blk)])
                PS.free(b)
            def chunk(blk, tb):
                t0 = blk * 128
                orot = blk % 2
                pat = PS.alloc()

                def fat(e):
                    ins = None
                    for h in range(4):
                        ins = e.matmul(ps(pat)[0:tb, h * 128:h * 128 + tb], Kt[:, h, t0:t0 + tb], Q[:, h, t0:t0 + tb],
                                       start=True, stop=True)
                    return ins
                P.op("pe", fat, reads=[("Kt", h) for h in range(4)] + [("Q", h) for h in range(4)], writes=[psr(pat)])
                for h in range(4):
                    P.op("dve", lambda e, h=h: e.tensor_tensor(out=AT[0:tb, h, 0:tb], in0=ps(pat)[0:tb, h * 128:h * 128 + tb],
                                                                in1=trib[0:tb, 0:tb], op=ALU.mult),
                         reads=[psr(pat), "trib"], writes=[("AT", h)])
                PS.free(pat)
                pkv = [PS.alloc(), PS.alloc()]
                for h in range(4):
                    P.op("pe", lambda e, h=h: e.matmul(ps(pkv[h // 2])[:, (h % 2) * 256:(h % 2) * 256 + 256],
                                                       KTm[0:tb, blk, h * 128:(h + 1) * 128],
                                                       V[0:tb, blk, h * 256:(h + 1) * 256], start=True, stop=True),
                         reads=[("KTm", blk), ("V", blk, h // 2)], writes=[psr(pkv[h // 2])])
                po = [PS.alloc(), PS.alloc()]
                for h in range(4):
                    pob = po[h // 2]
                    oc = (h % 2) * 256
                    P.op("pe", lambda e, h=h, pob=pob, oc=oc: e.matmul(
                        ps(pob)[0:tb, oc:oc + 256], AT[0:tb, h, 0:tb], V[0:tb, blk, h * 256:(h + 1) * 256],
                        start=True, stop=False), reads=[("AT", h), ("V", blk, h // 2)], writes=[psr(pob)])
                    P.op("pe", lambda e, h=h, pob=pob, oc=oc: e.matmul(
                        ps(pob)[0:tb, oc:oc + 256], Q[:, h, t0:t0 + tb], Sb[:, lg, h, :],
                        start=False, stop=True), reads=[("Q", h), ("Sb", lg, h)], writes=[psr(pob)])
                for h in range(4):
                    el = EB[:, h, t0 + tb - 1:t0 + tb]
                    kc = (h % 2) * 256
                    P.op("act", lambda e, h=h, el=el, kc=kc: e.activation(out=TMP[:, h, :], in_=ps(pkv[h // 2])[:, kc:kc + 256],
                                                                          func=AF.Copy, scale=el),
                         reads=[psr(pkv[h // 2]), ("EB", h)], writes=[("TMP", h)])
                    P.op("dve", lambda e, h=h, el=el: e.scalar_tensor_tensor(
                        out=S32[:, lg, h, :], in0=S32[:, lg, h, :], scalar=el, in1=TMP[:, h, :],
                        op0=ALU.mult, op1=ALU.add), reads=[("S32", lg, h), ("TMP", h), ("EB", h)],
                        writes=[("S32", lg, h)])
                    P.op("dve", lambda e, h=h: e.tensor_copy(out=Sb[:, lg, h, :], in_=S32[:, lg, h, :]),
                         reads=[("S32", lg, h)], writes=[("Sb", lg, h)])
                PS.free(pkv[0]); PS.free(pkv[1])
                for h in range(4):
                    pob = po[h // 2]
                    oc = (h % 2) * 256
                    P.op("act", lambda e, h=h, pob=pob, oc=oc: e.activation(
                        out=JUNK[0:tb, :], in_=ps(pob)[0:tb, oc:oc + 256], func=AF.Square,
                        accum_out=SS[0:tb, blk * 4 + h:blk * 4 + h + 1]),
                        reads=[psr(pob)], writes=[("SS", blk), "JUNK"])
                P.op("act", lambda e: e.activation(out=RH[0:tb, blk * 4:blk * 4 + 4], in_=SS[0:tb, blk * 4:blk * 4 + 4],
                                                   func=AF.Sqrt, scale=1.0 / 256.0, bias=EPS_RMS[0:tb, 0:1]),
                     reads=[("SS", blk), "EPS"], writes=[("RH", blk)])
                P.op("dve", lambda e: e.reciprocal(out=RH[0:tb, blk * 4:blk * 4 + 4], in_=RH[0:tb, blk * 4:blk * 4 + 4]),
                     reads=[("RH", blk)], writes=[("RH", blk)])
                for h in range(4):
                    pob = po[h // 2]
                    oc = (h % 2) * 256
                    P.op("act", lambda e, h=h, pob=pob, oc=oc: e.activation(
                        out=ON[0:tb, orot, h * 256:(h + 1) * 256], in_=ps(pob)[0:tb, oc:oc + 256], func=AF.Copy,
                        scale=RH[0:tb, blk * 4 + h:blk * 4 + h + 1]),
                        reads=[psr(pob), ("RH", blk)], writes=[("ON", orot)])
                PS.free(po[0]); PS.free(po[1])
                pt_ = PS.alloc()
                ptb = ps(pt_)[:, :].bitcast(BF16).rearrange("p (j t) -> p j t", j=KT)

                def ftr(e):
                    ins = None
                    for j in range(KT):
                        ins = e.transpose(out=ptb[:, j, 0:tb], in_=ON[0:tb, orot, j * 128:(j + 1) * 128],
                                          identity=identb[0:tb, 0:tb])
                    return ins
                P.op("pe", ftr, reads=[("ON", orot), "identb"], writes=[psr(pt_)])
                P.op("dve", lambda e: e.tensor_tensor(out=M[:, :, t0:t0 + tb], in0=ptb[:, :, 0:tb],
                                                      in1=RS[:, :, t0:t0 + tb], op=ALU.mult),
                     reads=[psr(pt_)] + [("RS", j) for j in range(KT)], writes=[("M", j) for j in range(KT)])
                PS.free(pt_)

            for (blk, tb) in blks:
                chunk(blk, tb)
            so = [RG.get(), RG.get()]
            out_proj_and_post_ln(l, T, so, None)

        ycnt = [0]
        load_x(tiles[0])
        for ti, tile in enumerate(tiles):
            s, t0, T, is_meta = tile
            if "notile" in DBG:
                break
            if "nometa" in DBG and is_meta:
                if ti + 1 < len(tiles):
                    load_x(tiles[ti + 1])
                continue
            transpose_in(T)
            if ti + 1 < len(tiles):
                load_x(tiles[ti + 1])
            for l in range(depth):
                P.barrier()
                run_deferred()
                if l % 2 == 0:
                    conv_layer(l, T, is_meta)
                else:
                    gla_layer(l, T, is_meta)
            P.barrier()
            run_deferred()
            if not is_meta and "noout" not in DBG:
                transpose_out(tile, ycnt)
        P.emit(None, sems, dma_sems, [("yout", 0), ("yout", 1)])
    return nc


_CACHE = {}


def _consts():
    ident = np.eye(128, dtype=np.float32)
    tri = np.triu(np.ones((128, 128), dtype=np.float32))
    return ident, tri


def run(inputs, nseq_per_core, ncores, depth=DEPTH_FULL, trace=False):
    x = np.ascontiguousarray(inputs["x"], dtype=np.float32)
    seq = x.shape[1]
    key = (nseq_per_core, seq, depth)
    if key not in _CACHE:
        _CACHE[key] = build(nseq_per_core, seq, depth)
    nc = _CACHE[key]
    ident, tri = _consts()
    in_maps = []
    for c in range(ncores):
        m = {k: np.ascontiguousarray(v, dtype=np.float32) for k, v in inputs.items() if k != "x"}
        m["x"] = x[c * nseq_per_core:(c + 1) * nseq_per_core]
        m["c_ident"] = ident
        m["c_tri"] = tri
        in_maps.append(m)
    res = run_bass_kernel_spmd(nc, in_maps, core_ids=list(range(ncores)), trace=trace)
    out = np.concatenate([r["y"] for r in res.results], axis=0)
    return out, res


def kernel(**inputs):
    out, _ = run(inputs, inputs["x"].shape[0] // NCORES, NCORES)
    return out
```

```python
import numpy as np
import concourse.bass as bass
import concourse.mybir as mybir
from concourse.bass_utils import run_bass_kernel_spmd

F32 = mybir.dt.float32
BF16 = mybir.dt.bfloat16
AF = mybir.ActivationFunctionType
ALU = mybir.AluOpType

D = 1024
KT = 8
NMETA = 16
TAPS = 31
HALO = 30
DEPTH_FULL = 4
ALPHA = (2 * DEPTH_FULL) ** 0.25
LN_EPS = 1e-5
RMS_EPS = 1e-6
TT = 512
NCORES = 8
CONV_ROWS = 37
NRING = 6

COMPUTE = ("pe", "act", "dve", "pool")


class Prog:
    def __init__(self, nc):
        self.nc = nc
        self.ops = {e: [] for e in ("pe", "act", "dve", "pool", "sp")}
        self.last_w = {}
        self.readers = {}
        self.dma_cnt = {}
        self.bar = {e: set() for e in self.ops}

    def op(self, eng, fn, reads=(), writes=(), dma_key=None):
        lst = self.ops[eng]
        idx = len(lst)
        deps = {}

        def add(d, kind):
            if d is None:
                return
            if d in deps and deps[d] == "raw":
                return
            deps[d] = kind

        for r in reads:
            add(self.last_w.get(r), "raw")
        for r in writes:
            add(self.last_w.get(r), "waw")
            for rd in self.readers.get(r, ()):
                add(rd, "war")
        for d in self.bar[eng]:
            add(d, "raw")
        self.bar[eng] = set()
        rec = dict(fn=fn, deps=deps, dma_key=dma_key, dma_n=None, signal=False, ordinal=None)
        if dma_key is not None:
            n = self.dma_cnt.get(dma_key, 0) + 1
            self.dma_cnt[dma_key] = n
            rec["dma_n"] = n
        lst.append(rec)
        me = (eng, idx)
        for r in reads:
            s = self.readers.setdefault(r, set())
            if dma_key is None:
                for o in [o for o in s if o[0] == eng and self.ops[eng][o[1]]["dma_key"] is None]:
                    s.discard(o)
            s.add(me)
        for r in writes:
            self.last_w[r] = me
            self.readers[r] = set()
        return me

    def barrier(self):
        lasts = set()
        for e in COMPUTE:
            for i in range(len(self.ops[e]) - 1, -1, -1):
                if self.ops[e][i]["dma_key"] is None:
                    lasts.add((e, i))
                    break
        for e in COMPUTE:
            self.bar[e] = set(d for d in lasts if d[0] != e)

    def resolve(self):
        for eng, lst in self.ops.items():
            waited = {e: -1 for e in self.ops}
            waited_dma = {}
            for idx, rec in enumerate(lst):
                waits = []
                for (pe_, pi), kind in rec["deps"].items():
                    prod = self.ops[pe_][pi]
                    if prod["dma_key"] is not None:
                        k = prod["dma_key"]
                        if waited_dma.get(k, 0) >= prod["dma_n"]:
                            continue
                        waited_dma[k] = prod["dma_n"]
                        waits.append(("dma", k, prod["dma_n"]))
                        continue
                    if pe_ == eng:
                        if eng in ("pe", "sp"):
                            continue
                    if waited[pe_] >= pi:
                        continue
                    waited[pe_] = pi
                    prod["signal"] = True
                    waits.append(("eng", pe_, pi))
                rec["waits"] = waits
        for eng, lst in self.ops.items():
            n = 0
            for rec in lst:
                if rec["signal"]:
                    n += 1
                    rec["ordinal"] = n

    def emit(self, block_ctx_factory, sems, dma_sems, final_waits):
        self.resolve()
        nc = self.nc
        P = self

        def run(eng_name):
            def body(e):
                for rec in P.ops[eng_name]:
                    for w in rec["waits"]:
                        if w[0] == "dma":
                            e.wait_ge(dma_sems[w[1]], 16 * w[2])
                        else:
                            e.wait_ge(sems[w[1]], P.ops[w[1]][w[2]]["ordinal"])
                    ins = rec["fn"](e)
                    if rec["dma_key"] is not None:
                        ins.then_inc(dma_sems[rec["dma_key"]], 16)
                    elif rec["signal"]:
                        ins.then_inc(sems[eng_name], 1)
                if eng_name == "sp":
                    for k in final_waits:
                        if P.dma_cnt.get(k, 0):
                            e.wait_ge(dma_sems[k], 16 * P.dma_cnt[k])
            return body

        with nc.Block() as block:
            block.tensor(run("pe"))
            block.scalar(run("act"))
            block.vector(run("dve"))
            block.gpsimd(run("pool"))
            block.sync(run("sp"))


def layer_groups(depth):
    groups = []
    for l in range(depth):
        j = l // 2
        if l % 2 == 0:
            for c0 in (0, 1024, 512, 1536, 2048, 2560):
                groups.append((l, "conv_w_in", j, c0))
            for c0 in (0, 512):
                groups.append((l, "conv_w_out", j, c0))
        else:
            for c0 in (0, 512, 1024, 1536, 2048, 2560):
                groups.append((l, "gla_w_in", j, c0))
            for c0 in (0, 512):
                groups.append((l, "gla_w_out", j, c0))
    return groups


def build(nseq, seq, depth):
    assert seq % TT == 0
    nc = bass.Bass("TRN2", target_bir_lowering=False)
    n_conv = (depth + 1) // 2
    n_gla = depth // 2
    dr = {}

    def din(name, shape):
        dr[name] = nc.dram_tensor(name, list(shape), F32, kind="ExternalInput").ap()
        return dr[name]

    x = din("x", (nseq, seq, D))
    meta = din("meta", (NMETA, D))
    din("conv_w_in", (2, D, 3072)); din("conv_b_in", (2, 3072)); din("conv_w_dw", (2, TAPS, D))
    din("conv_b_dw", (2, D)); din("conv_norm_g", (2, D)); din("conv_norm_b", (2, D))
    din("conv_w_out", (2, D, D)); din("conv_b_out", (2, D))
    din("gla_w_in", (2, D, 3088)); din("gla_w_a2", (2, 16, 512)); din("gla_b_a", (2, 512))
    din("gla_norm_g", (2, 256)); din("gla_w_out", (2, D, D))
    din("post_ln_g", (4, D)); din("post_ln_b", (4, D))
    ident_d = din("c_ident", (128, 128))
    tri_d = din("c_tri", (128, 128))
    y = nc.dram_tensor("y", [nseq, seq, D], F32, kind="ExternalOutput").ap()

    groups = layer_groups(depth)
    NG = len(groups)
    wscr = nc.dram_tensor("wscr", [max(NG, 1), 128, KT * 512], BF16, kind="Internal").ap()

    P = Prog(nc)
    import contextlib
    es = contextlib.ExitStack()
    with es:
        def sb(name, shape, dt):
            return es.enter_context(nc.sbuf_tensor(name, list(shape), dt))

        XIN = sb("XIN", [128, 4, D], F32)
        YOUT = sb("YOUT", [128, 2, D], F32)
        X = sb("X", [128, KT, TT], F32)
        Xb = sb("Xb", [128, KT, TT], BF16)
        WR = sb("WR", [128, NRING, KT * 512], BF16)
        WSM = sb("WSM", [128, 2, KT, 16], BF16)
        WA2 = sb("WA2", [16, 2, 512], BF16)
        BAROW = sb("BAROW", [1, 2, 512], BF16)
        BOROW = sb("BOROW", [1, 2, D], BF16)
        HAL = sb("HAL", [128, 2, KT, HALO], BF16)
        S32 = sb("S32", [128, 2, 4, 256], F32)
        Sb = sb("Sb", [128, 2, 4, 256], BF16)
        PROW2 = sb("PROW2", [4, 128], F32)
        PT = sb("PT", [128, KT, 88], F32)
        GN = sb("GN", [128, 4], F32)
        ident32 = sb("ident32", [128, 128], F32)
        tri32 = sb("tri32", [128, 128], F32)
        identb = sb("identb", [128, 128], BF16)
        trib = sb("trib", [128, 128], BF16)
        ones32 = sb("ones32", [128, 128], F32)
        onesb = sb("onesb", [128, TT], BF16)
        MEAN = sb("MEAN", [128, TT], F32)
        EPS_LN = sb("EPS_LN", [128, 1], F32)
        EPS_RMS = sb("EPS_RMS", [128, 1], F32)
        MSQ = sb("MSQ", [128, TT], F32)
        RSTD = sb("RSTD", [128, TT], F32)
        T1 = sb("T1", [128, 2, TT], F32)
        SQ = sb("SQ", [128, 2, TT], BF16)
        M = sb("M", [128, KT, TT], BF16)
        ARENA = sb("ARENA", [128, 15 * 1024], F32)
        PSt = [es.enter_context(nc.psum_tensor(f"ps{i}", [128, 512], F32)) for i in range(8)]

        class Arena:
            def __init__(self):
                self.off = 0

            def reset(self):
                self.off = 0

            def take(self, n_elems, dt, shape_str=None, **kw):
                nb = n_elems * (4 if dt == F32 else 2)
                nw = (nb + 3) // 4
                nw = (nw + 7) // 8 * 8
                v = ARENA[:, self.off:self.off + nw]
                self.off += nw
                assert self.off <= 15 * 1024, self.off
                if dt != F32:
                    v = v.bitcast(dt)
                v = v[:, 0:n_elems]
                if shape_str:
                    v = v.rearrange(shape_str, **kw)
                return v

        AR = Arena()
        PROW = AR.take(D, F32)

        sem_names = ["pe", "act", "dve", "pool", "sp"]
        sems = {n: es.enter_context(nc.semaphore("s_" + n)) for n in sem_names}
        dma_keys = [("w", s) for s in range(NRING)] + [("cv", i) for i in range(8)] + \
                   [("xin",), ("yout", 0), ("yout", 1)] + [("par", i) for i in range(4)]
        dma_sems = {k: es.enter_context(nc.semaphore("d_" + "_".join(str(a) for a in k))) for k in dma_keys}

        class Banks:
            def __init__(self):
                self.free_list = list(range(6))

            def alloc(self):
                assert self.free_list, "PSUM pool exhausted"
                return self.free_list.pop(0)

            def free(self, b):
                self.free_list.append(b)

        PS = Banks()
        ST1, ST2 = 6, 7

        def ps(b):
            return PSt[b]

        def psr(b):
            return ("ps", b)

        import os
        DBG = os.environ.get("KDBG", "")
        P.op("sp", lambda e: e.dma_start(out=ident32[:], in_=ident_d), writes=["ident32"], dma_key=("par", 0))
        P.op("sp", lambda e: e.dma_start(out=tri32[:], in_=tri_d), writes=["tri32"], dma_key=("par", 1))
        P.op("dve", lambda e: e.tensor_copy(out=identb[:], in_=ident32[:]), reads=["ident32"], writes=["identb"])
        P.op("dve", lambda e: e.tensor_copy(out=trib[:], in_=tri32[:]), reads=["tri32"], writes=["trib"])
        P.op("dve", lambda e: e.memset(ones32[:], 1.0), writes=["ones32"])
        P.op("dve", lambda e: e.memset(EPS_LN[:], LN_EPS), writes=["EPS"])
        P.op("dve", lambda e: e.memset(EPS_RMS[:], RMS_EPS), writes=["EPS"])
        P.op("dve", lambda e: e.memset(onesb[:], 1.0), writes=["onesb"])
        P.op("dve", lambda e: e.memset(PROW[:], 0.0), writes=["PROW"])

        def prow_dma(dst_rows, src, key_i):
            P.op("sp", lambda e: e.dma_start(out=dst_rows, in_=src), reads=[], writes=["PROW"],
                 dma_key=("par", key_i))

        r = 0
        for j in range(2 if "noparam" not in DBG else 0):
            base = j * CONV_ROWS
            prow_dma(PROW[base:base + 3, :], dr["conv_b_in"][j].rearrange("(a n) -> a n", a=3), 2)
            prow_dma(PROW[base + 3:base + 34, :], dr["conv_w_dw"][j], 2)
            prow_dma(PROW[base + 34:base + 35, :], dr["conv_b_dw"][j:j + 1, :], 2)
            prow_dma(PROW[base + 35:base + 36, :], dr["conv_norm_g"][j:j + 1, :], 2)
            prow_dma(PROW[base + 36:base + 37, :], dr["conv_norm_b"][j:j + 1, :], 2)
        LNB = 2 * CONV_ROWS
        for l in range(4 if "noparam" not in DBG else 0):
            prow_dma(PROW[LNB + 2 * l:LNB + 2 * l + 1, :], dr["post_ln_g"][l:l + 1, :], 2)
            prow_dma(PROW[LNB + 2 * l + 1:LNB + 2 * l + 2, :], dr["post_ln_b"][l:l + 1, :], 2)
        NROWS = LNB + 8
        P.op("sp", lambda e: e.dma_start(out=PROW2[:], in_=dr["gla_norm_g"].rearrange("l (a n) -> (l a) n", a=2)),
             writes=["PROW2"], dma_key=("par", 3))
        for i in range(KT if "nopt" not in DBG else 0):
            b = PS.alloc()
            P.op("pe", lambda e, i=i, b=b: e.transpose(out=ps(b)[:, 0:NROWS], in_=PROW[0:NROWS, i * 128:(i + 1) * 128],
                                                       identity=ident32[0:NROWS, 0:NROWS]),
                 reads=["PROW", "ident32"], writes=[psr(b)])
            P.op("act", lambda e, i=i, b=b: e.copy(out=PT[:, i, 0:NROWS], in_=ps(b)[:, 0:NROWS]),
                 reads=[psr(b)], writes=["PT"])
            PS.free(b)
        b = PS.alloc()
        P.op("pe", lambda e, b=b: e.transpose(out=ps(b)[:, 0:4], in_=PROW2[0:4, :], identity=ident32[0:4, 0:4]),
             reads=["PROW2", "ident32"], writes=[psr(b)])
        P.op("act", lambda e, b=b: e.copy(out=GN[:], in_=ps(b)[:, 0:4]), reads=[psr(b)], writes=["GN"])
        PS.free(b)
        if "nosmall" in DBG:
            class _N:
                def op(self, *a, **k): pass
            P_ = P; P = _N()
        P.op("pool", lambda e: e.dma_start(out=WA2[:], in_=dr["gla_w_a2"].rearrange("l r n -> r l n")),
             writes=["WA2", ("cvslot", 0)], dma_key=("cv", 0))
        P.op("pool", lambda e: e.dma_start(out=BAROW[:], in_=dr["gla_b_a"].rearrange("(o l) n -> o l n", o=1)),
             writes=["BAROW", ("cvslot", 1)], dma_key=("cv", 1))
        P.op("pool", lambda e: e.dma_start(out=BOROW[:], in_=dr["conv_b_out"].rearrange("(o l) n -> o l n", o=1)),
             writes=["BOROW", ("cvslot", 2)], dma_key=("cv", 2))
        P.op("pool", lambda e: e.dma_start(
            out=WSM[:], in_=dr["gla_w_in"][:, :, 3072:3088].rearrange("l (kt p) n -> p l kt n", p=128)),
            writes=["WSM", ("cvslot", 3)], dma_key=("cv", 3))
        if "nosmall" in DBG:
            P = P_
        for g, (l, nm, j, c0) in enumerate(groups):
            src = dr[nm][j][:, c0:c0 + 512].rearrange("(kt p) n -> p kt n", p=128)
            dst = wscr[g].rearrange("p (kt n) -> p kt n", kt=KT)
            P.op("pool", lambda e, src=src, dst=dst: e.dma_start(out=dst, in_=src),
                 writes=[("wscr", g), ("cvslot", g % 8)], dma_key=("cv", g % 8))

        tiles = []
        for s in range(nseq):
            tiles.append((s, 0, NMETA, True))
            for t in range(seq // TT):
                tiles.append((s, t * TT, TT, False))
        wseq = []
        for _ in tiles:
            for g in range(NG):
                wseq.append(g)

        class Ring:
            def __init__(self):
                self.next_load = 0
                self.next_use = 0

            def issue(self):
                if self.next_load >= len(wseq):
                    return
                n = self.next_load
                g = wseq[n]
                slot = n % NRING
                self.next_load += 1
                P.op("sp", lambda e, g=g, slot=slot: e.dma_start(out=WR[:, slot, :], in_=wscr[g]),
                     reads=[("wscr", g)], writes=[("w", slot)], dma_key=("w", slot))

            def get(self):
                n = self.next_use
                self.next_use += 1
                assert n < self.next_load
                return n % NRING

            def release(self, slot):
                self.issue()

        RG = Ring()
        for _ in range(NRING):
            RG.issue()

        def wv(slot):
            return WR[:, slot, :].rearrange("p (kt n) -> p kt n", kt=KT)

        DEFER = []

        def run_deferred():
            while DEFER:
                DEFER.pop(0)()

        def load_x(tile):
            s, t0, T, is_meta = tile
            if is_meta:
                P.op("sp", lambda e: e.dma_start(out=XIN[0:NMETA, 0, :], in_=meta), writes=["XIN"], dma_key=("xin",))
            else:
                P.op("sp", lambda e, s=s, t0=t0: e.dma_start(
                    out=XIN[:, :, :], in_=x[s, t0:t0 + TT, :].rearrange("(nb p) d -> p nb d", p=128)),
                    writes=["XIN"], dma_key=("xin",))

        def blocks(T):
            return [(b, min(128, T - b * 128)) for b in range((T + 127) // 128)]

        def transpose_in(T):
            for i in range(KT):
                b = PS.alloc()
                for (blk, tb) in blocks(T):
                    if "onetr" in DBG and blk > 0:
                        continue
                    if "nope" in DBG:
                        continue
                    P.op("pe", lambda e, i=i, b=b, blk=blk, tb=tb: e.transpose(
                        out=ps(b)[:, blk * 128:blk * 128 + tb], in_=XIN[0:tb, blk, i * 128:(i + 1) * 128],
                        identity=ident32[0:tb, 0:tb]), reads=["XIN", "ident32"], writes=[psr(b)])
                if "noact" not in DBG:
                    P.op("act", lambda e, i=i, b=b: e.copy(out=X[:, i, 0:T], in_=ps(b)[:, 0:T]),
                         reads=[psr(b)], writes=[("X", i)])
                if "nodve" not in DBG:
                    P.op("dve", lambda e, i=i, b=b: e.tensor_copy(out=Xb[:, i, 0:T], in_=X[:, i, 0:T]),
                         reads=[("X", i)], writes=[("Xb", i)])
                PS.free(b)

        def transpose_out(tile, cnt):
            s, t0, T, _ = tile
            for (blk, tb) in blocks(T):
                rot = cnt[0] % 2
                cnt[0] += 1
                ba, bb = PS.alloc(), PS.alloc()
                for i in range(KT):
                    bk = ba if i < 4 else bb
                    P.op("pe", lambda e, i=i, bk=bk, blk=blk: e.transpose(
                        out=ps(bk)[:, (i % 4) * 128:(i % 4 + 1) * 128], in_=X[:, i, blk * 128:(blk + 1) * 128],
                        identity=ident32[:, :]), reads=[("X", i), "ident32"], writes=[psr(bk)])
                P.op("act", lambda e, rot=rot, ba=ba: e.copy(out=YOUT[:, rot, 0:512], in_=ps(ba)[:, :]),
                     reads=[psr(ba)], writes=[("YOUT", rot, 0)])
                P.op("dve", lambda e, rot=rot, bb=bb: e.tensor_copy(out=YOUT[:, rot, 512:1024], in_=ps(bb)[:, :]),
                     reads=[psr(bb)], writes=[("YOUT", rot, 1)])
                PS.free(ba)
                PS.free(bb)
                P.op("sp", lambda e, rot=rot, s=s, t0=t0, blk=blk: e.dma_start(
                    out=y[s, t0 + blk * 128:t0 + (blk + 1) * 128, :], in_=YOUT[:, rot, :]),
                    reads=[("YOUT", rot, 0), ("YOUT", rot, 1)], dma_key=("yout", rot))

        def proj_fm(bank, slot, c0, T, extra=None):
            W = wv(slot)

            def fn(e):
                ins = None
                for k in range(KT):
                    ins = e.matmul(ps(bank)[:, 0:T], W[:, k, c0:c0 + 128], Xb[:, k, 0:T],
                                   start=(k == 0), stop=(k == KT - 1 and extra is None))
                if extra is not None:
                    ins = extra(e)
                return ins
            P.op("pe", fn, reads=[("w", slot)] + [("Xb", k) for k in range(KT)], writes=[psr(bank)])

        def stats_mm(src_ap, sq_ap, first, last, reads):
            def fn(e):
                e.matmul(ps(ST1)[:, 0:src_ap.shape[-1]], ones32[:, :], src_ap, start=first, stop=last)
                return e.matmul(ps(ST2)[:, 0:src_ap.shape[-1]], onesb[:, 0:128], sq_ap, start=first, stop=last)
            P.op("pe", fn, reads=reads + ["ones32", "onesb"], writes=[psr(ST1), psr(ST2)])

        def stats_finish(T, eps):
            P.op("dve", lambda e: e.tensor_scalar(out=MEAN[:, 0:T], in0=ps(ST1)[:, 0:T], scalar1=1.0 / D, scalar2=None,
                                                  op0=ALU.mult), reads=[psr(ST1)], writes=["MEAN"])
            P.op("dve", lambda e: e.tensor_tensor(out=MSQ[:, 0:T], in0=MEAN[:, 0:T], in1=MEAN[:, 0:T], op=ALU.mult),
                 reads=["MEAN"], writes=["MSQ"])
            P.op("dve", lambda e: e.scalar_tensor_tensor(out=MSQ[:, 0:T], in0=ps(ST2)[:, 0:T], scalar=1.0 / D,
                                                         in1=MSQ[:, 0:T], op0=ALU.mult, op1=ALU.subtract),
                 reads=[psr(ST2), "MSQ"], writes=["MSQ"])
            P.op("act", lambda e: e.activation(out=MSQ[:, 0:T], in_=MSQ[:, 0:T], func=AF.Sqrt, bias=EPS_LN[:, 0:1]),
                 reads=["MSQ", "EPS"], writes=["MSQ"])
            P.op("dve", lambda e: e.reciprocal(out=RSTD[:, 0:T], in_=MSQ[:, 0:T]), reads=["MSQ"], writes=["RSTD"])

        def out_proj_and_post_ln(l, T, slots, bias_row):
            pending = []
            for i in range(KT):
                slot = slots[i // 4]
                c0 = (i % 4) * 128
                W = wv(slot)
                b = PS.alloc()

                def fn(e, W=W, c0=c0, b=b, i=i):
                    ins = None
                    for j in range(KT):
                        ins = e.matmul(ps(b)[:, 0:T], W[:, j, c0:c0 + 128], M[:, j, 0:T], start=(j == 0),
                                       stop=(j == KT - 1 and bias_row is None))
                    if bias_row is not None:
                        ins = e.matmul(ps(b)[:, 0:T], bias_row[0:1, i * 128:(i + 1) * 128], onesb[0:1, 0:T],
                                       start=False, stop=True)
                    return ins
                P.op("pe", fn, reads=[("w", slot), "BOROW", "onesb"] + [("M", j) for j in range(KT)], writes=[psr(b)])
                if i % 4 == 3:
                    RG.release(slot)
                P.op("dve", lambda e, i=i, b=b: e.scalar_tensor_tensor(
                    out=X[:, i, 0:T], in0=X[:, i, 0:T], scalar=float(ALPHA), in1=ps(b)[:, 0:T],
                    op0=ALU.mult, op1=ALU.add), reads=[psr(b), ("X", i)], writes=[("X", i)])
                PS.free(b)
                rot = i % 2
                P.op("act", lambda e, i=i, rot=rot: e.activation(out=SQ[:, rot, 0:T], in_=X[:, i, 0:T], func=AF.Square),
                     reads=[("X", i)], writes=[("SQ", rot)])
                pending.append((i, rot))
                if len(pending) > 1:
                    pi, prot = pending.pop(0)
                    stats_mm(X[:, pi, 0:T], SQ[:, prot, 0:T], pi == 0, False, [("X", pi), ("SQ", prot)])
            pi, prot = pending.pop(0)
            stats_mm(X[:, pi, 0:T], SQ[:, prot, 0:T], False, True, [("X", pi), ("SQ", prot)])
            stats_finish(T, LN_EPS)
            gcol = LNB + 2 * l
            last = (l == depth - 1)

            def xfin(i):
                if last:
                    P.op("act", lambda e: e.activation(
                        out=X[:, i, 0:T], in_=X[:, i, 0:T], func=AF.Identity,
                        scale=PT[:, i, gcol:gcol + 1], bias=PT[:, i, gcol + 1:gcol + 2]),
                        reads=[("X", i), "PT"], writes=[("X", i)])
                else:
                    P.op("pool", lambda e: e.tensor_scalar(
                        out=X[:, i, 0:T], in0=X[:, i, 0:T], scalar1=PT[:, i, gcol:gcol + 1],
                        scalar2=PT[:, i, gcol + 1:gcol + 2], op0=ALU.mult, op1=ALU.add),
                        reads=[("X", i), "PT"], writes=[("X", i)])

            for i in range(KT + 1):
                if i < KT:
                    P.op("dve", lambda e, i=i: e.tensor_tensor(out=X[:, i, 0:T], in0=X[:, i, 0:T],
                                                                in1=MEAN[:, 0:T], op=ALU.subtract),
                         reads=[("X", i), "MEAN"], writes=[("X", i)])
                if i >= 1:
                    k = i - 1
                    P.op("dve", lambda e, k=k: e.tensor_tensor(out=X[:, k, 0:T], in0=X[:, k, 0:T],
                                                                in1=RSTD[:, 0:T], op=ALU.mult),
                         reads=[("X", k), "RSTD"], writes=[("X", k)])
                    P.op("act", lambda e, k=k: e.activation(
                        out=Xb[:, k, 0:T], in_=X[:, k, 0:T], func=AF.Identity,
                        scale=PT[:, k, gcol:gcol + 1], bias=PT[:, k, gcol + 1:gcol + 2]),
                        reads=[("X", k), "PT"], writes=[("Xb", k)])
                    if last:
                        xfin(k)
                    else:
                        DEFER.append(lambda k=k: xfin(k))

        def conv_layer(l, T, first):
            lc = l // 2
            base = lc * CONV_ROWS
            AR.reset()
            GLU = AR.take(KT * (TT + HALO), BF16, "p (k t) -> p k t", k=KT)
            ZS = AR.take(KT * TT, BF16, "p (k t) -> p k t", k=KT)
            C = AR.take(KT * TT, F32, "p (k t) -> p k t", k=KT)
            SIG = AR.take(2 * TT, BF16, "p (k t) -> p k t", k=2)
            CS = AR.take(2 * TT, BF16, "p (k t) -> p k t", k=2)
            DG = AR.take(2 * TAPS * 128, BF16, "p (r t c) -> p r t c", r=2, t=TAPS)
            if first:
                P.op("dve", lambda e: e.memset(HAL[:, lc, :, :], 0.0), writes=[("HAL", lc)])
            P.op("dve", lambda e: e.tensor_copy(out=GLU[:, :, 0:HALO], in_=HAL[:, lc, :, :]),
                 reads=[("HAL", lc)], writes=[("GLU", j) for j in range(KT)])
            slots = {}

            def proj(j):
                if j % 4 == 0:
                    slots["a"], slots["g"] = RG.get(), RG.get()
                c0 = (j % 4) * 128
                pa, pg = PS.alloc(), PS.alloc()
                proj_fm(pg, slots["g"], c0, T)
                proj_fm(pa, slots["a"], c0, T)
                if j % 4 == 3:
                    RG.release(slots["a"]); RG.release(slots["g"])
                rot = j % 2
                P.op("act", lambda e: e.activation(out=SIG[:, rot, 0:T], in_=ps(pg)[:, 0:T], func=AF.Sigmoid,
                                                   bias=PT[:, j, base + 1:base + 2]),
                     reads=[psr(pg), "PT"], writes=[("SIG", rot)])
                P.op("dve", lambda e: e.scalar_tensor_tensor(
                    out=GLU[:, j, HALO:HALO + T], in0=ps(pa)[:, 0:T], scalar=PT[:, j, base:base + 1],
                    in1=SIG[:, rot, 0:T], op0=ALU.add, op1=ALU.mult),
                    reads=[psr(pa), ("SIG", rot), "PT"], writes=[("GLU", j)])
                PS.free(pg); PS.free(pa)

                def dg_dve(e):
                    ins = None
                    for tap in range(TAPS):
                        ins = e.tensor_scalar(out=DG[:, rot, tap, :], in0=identb[:, :],
                                              scalar1=PT[:, j, base + 3 + tap:base + 4 + tap], scalar2=None,
                                              op0=ALU.mult)
                    return ins

                def dg_act(e):
                    ins = None
                    for tap in range(TAPS):
                        ins = e.activation(out=DG[:, rot, tap, :], in_=identb[:, :], func=AF.Copy,
                                           scale=PT[:, j, base + 3 + tap:base + 4 + tap])
                    return ins
                if j % 2 == 0:
                    P.op("dve", dg_dve, reads=["identb", "PT"], writes=[("DG", rot)])
                else:
                    P.op("act", dg_act, reads=["identb", "PT"], writes=[("DG", rot)])

            def conv(j):
                rot = j % 2
                pc = PS.alloc()

                def fn(e):
                    ins = None
                    for tap in range(TAPS):
                        ins = e.matmul(ps(pc)[:, 0:T], DG[:, rot, tap, :], GLU[:, j, tap:tap + T],
                                       start=(tap == 0), stop=(tap == TAPS - 1))
                    return ins
                P.op("pe", fn, reads=[("DG", rot), ("GLU", j)], writes=[psr(pc)])
                P.op("act", lambda e: e.activation(out=C[:, j, 0:T], in_=ps(pc)[:, 0:T], func=AF.Identity,
                                                   bias=PT[:, j, base + 34:base + 35]),
                     reads=[psr(pc), "PT"], writes=[("C", j)])
                P.op("act", lambda e: e.activation(out=SQ[:, rot, 0:T], in_=ps(pc)[:, 0:T], func=AF.Square,
                                                   bias=PT[:, j, base + 34:base + 35]),
                     reads=[psr(pc), "PT"], writes=[("SQ", rot)])
                PS.free(pc)

            def stats(j):
                stats_mm(C[:, j, 0:T], SQ[:, j % 2, 0:T], j == 0, j == KT - 1, [("C", j), ("SQ", j % 2)])

            for j in range(KT + 2):
                if j < KT:
                    proj(j)
                if 1 <= j <= KT:
                    conv(j - 1)
                if j >= 2:
                    stats(j - 2)
            P.op("dve", lambda e: e.tensor_copy(out=HAL[:, lc, :, :], in_=GLU[:, :, T:T + HALO]),
                 reads=[("GLU", j) for j in range(KT)], writes=[("HAL", lc)])
            def zproj(j):
                if j % 4 == 0:
                    slots["z"] = RG.get()
                pz = PS.alloc()
                proj_fm(pz, slots["z"], (j % 4) * 128, T)
                if j % 4 == 3:
                    RG.release(slots["z"])
                P.op("act", lambda e: e.activation(out=ZS[:, j, 0:T], in_=ps(pz)[:, 0:T], func=AF.Silu,
                                                   bias=PT[:, j, base + 2:base + 3]),
                     reads=[psr(pz), "PT"], writes=[("ZS", j)])
                PS.free(pz)

            for j in range(KT):
                zproj(j)
                if j == 1:
                    stats_finish(T, LN_EPS)
            for j in range(KT):
                rot = j % 2
                P.op("dve", lambda e, j=j, rot=rot: e.tensor_tensor(out=T1[:, rot, 0:T], in0=C[:, j, 0:T],
                                                                     in1=MEAN[:, 0:T], op=ALU.subtract),
                     reads=[("C", j), "MEAN"], writes=[("T1", rot)])
                P.op("dve", lambda e, rot=rot: e.tensor_tensor(out=T1[:, rot, 0:T], in0=T1[:, rot, 0:T],
                                                               in1=RSTD[:, 0:T], op=ALU.mult),
                     reads=[("T1", rot), "RSTD"], writes=[("T1", rot)])
                P.op("act", lambda e, j=j, rot=rot: e.activation(
                    out=CS[:, rot, 0:T], in_=T1[:, rot, 0:T], func=AF.Silu,
                    scale=PT[:, j, base + 35:base + 36], bias=PT[:, j, base + 36:base + 37]),
                    reads=[("T1", rot), "PT"], writes=[("CS", rot)])
                P.op("dve", lambda e, j=j, rot=rot: e.tensor_tensor(out=M[:, j, 0:T], in0=CS[:, rot, 0:T],
                                                                     in1=ZS[:, j, 0:T], op=ALU.mult),
                     reads=[("CS", rot), ("ZS", j)], writes=[("M", j)])
            so = [RG.get(), RG.get()]
            out_proj_and_post_ln(l, T, so, BOROW[:, lc, :])

        def gla_layer(l, T, first):
            lg = l // 2
            AR.reset()
            A1 = AR.take(TT, BF16)
            G = AR.take(4 * 512, F32, "p (b n) -> p b n", b=4)
            E = AR.take(1 * 512, F32, "p (r n) -> p r n", r=1)
            EB = AR.take(4 * TT, F32, "p (h t) -> p h t", h=4)
            ENB = AR.take(1 * TT, F32, "p (r t) -> p r t", r=1)
            Q = AR.take(4 * TT, BF16, "p (h t) -> p h t", h=4)
            Kt = AR.take(4 * TT, BF16, "p (h t) -> p h t", h=4)
            KTm = AR.take(4 * 512, BF16, "p (b n) -> p b n", b=4)
            V = AR.take(4 * D, BF16, "p (b n) -> p b n", b=4)
            RS = AR.take(KT * TT, BF16, "p (k t) -> p k t", k=KT)
            AT = AR.take(4 * 128, BF16, "p (r t) -> p r t", r=4)
            TMP = AR.take(4 * 256, F32, "p (r t) -> p r t", r=4)
            ON = AR.take(2 * D, BF16, "p (r t) -> p r t", r=2)
            SS = AR.take(16, F32)
            RH = AR.take(16, F32)
            JUNK = AR.take(256, BF16)
            blks = blocks(T)
            if first:
                P.op("dve", lambda e: e.memset(S32[:, lg, :, :], 0.0), writes=[("S32", lg, h) for h in range(4)])
                P.op("dve", lambda e: e.memset(Sb[:, lg, :, :], 0.0), writes=[("Sb", lg, h) for h in range(4)])
            sq, sk = RG.get(), RG.get()
            b = PS.alloc()

            def fa1(e, b=b):
                ins = None
                for k in range(KT):
                    ins = e.matmul(ps(b)[0:16, 0:T], WSM[:, lg, k, :], Xb[:, k, 0:T], start=(k == 0), stop=(k == KT - 1))
                return ins
            P.op("pe", fa1, reads=["WSM"] + [("Xb", k) for k in range(KT)], writes=[psr(b)])
            P.op("dve", lambda e, b=b: e.tensor_copy(out=A1[0:16, 0:T], in_=ps(b)[0:16, 0:T]), reads=[psr(b)], writes=["A1"])
            PS.free(b)
            for (blk, tb) in blks:
                b = PS.alloc()
                rot = 0

                def fg(e, b=b, blk=blk, tb=tb):
                    e.matmul(ps(b)[0:tb, :], A1[0:16, blk * 128:blk * 128 + tb], WA2[0:16, lg, :], start=True, stop=False)
                    return e.matmul(ps(b)[0:tb, :], onesb[0:1, 0:tb], BAROW[0:1, lg, :], start=False, stop=True)
                P.op("pe", fg, reads=["A1", "WA2", "BAROW", "onesb"], writes=[psr(b)])
                P.op("act", lambda e, b=b, tb=tb, rot=rot: e.activation(out=E[0:tb, rot, :], in_=ps(b)[0:tb, :],
                                                                        func=AF.Exp, scale=-1.0),
                     reads=[psr(b)], writes=[("E", rot)])
                PS.free(b)
                P.op("act", lambda e, blk=blk, tb=tb, rot=rot: e.activation(out=G[0:tb, blk, :], in_=E[0:tb, rot, :],
                                                                            func=AF.Ln, bias=1.0),
                     reads=[("E", rot)], writes=[("G", blk)])
            for h in range(4):
                bbc, bq, bk = PS.alloc(), PS.alloc(), PS.alloc()

                def fbc(e, h=h, bbc=bbc):
                    ins = None
                    for (blk, tb) in blks:
                        ins = e.matmul(ps(bbc)[:, blk * 128:blk * 128 + tb], G[0:tb, blk, h * 128:(h + 1) * 128],
                                       tri32[0:tb, 0:tb], start=True, stop=True)
                    return ins
                P.op("pe", fbc, reads=[("G", blk) for (blk, _) in blks] + ["tri32"], writes=[psr(bbc)])
                proj_fm(bq, sq, h * 128, T)
                proj_fm(bk, sk, h * 128, T)
                rot = 0
                P.op("act", lambda e, h=h, bbc=bbc: e.activation(out=EB[:, h, 0:T], in_=ps(bbc)[:, 0:T], func=AF.Exp,
                                                                 scale=-1.0 / 16.0),
                     reads=[psr(bbc)], writes=[("EB", h)])
                P.op("act", lambda e, rot=rot, bbc=bbc: e.activation(out=ENB[:, rot, 0:T], in_=ps(bbc)[:, 0:T],
                                                                     func=AF.Exp, scale=1.0 / 16.0),
                     reads=[psr(bbc)], writes=[("ENB", rot)])
                P.op("dve", lambda e, h=h, bq=bq: e.scalar_tensor_tensor(
                    out=Q[:, h, 0:T], in0=ps(bq)[:, 0:T], scalar=float(128 ** -0.5), in1=EB[:, h, 0:T],
                    op0=ALU.mult, op1=ALU.mult), reads=[psr(bq), ("EB", h)], writes=[("Q", h)])
                P.op("dve", lambda e, h=h, bk=bk, rot=rot: e.tensor_tensor(
                    out=Kt[:, h, 0:T], in0=ps(bk)[:, 0:T], in1=ENB[:, rot, 0:T], op=ALU.mult),
                    reads=[psr(bk), ("ENB", rot)], writes=[("Kt", h)])
                PS.free(bbc); PS.free(bq); PS.free(bk)
            RG.release(sq); RG.release(sk)
            sv = [RG.get(), RG.get()]
            for half in range(2):
                W = wv(sv[half])
                for (blk, tb) in blks:
                    b = PS.alloc()

                    def fv(e, W=W, b=b, blk=blk, tb=tb):
                        ins = None
                        for k in range(KT):
                            ins = e.matmul(ps(b)[0:tb, :], Xb[:, k, blk * 128:blk * 128 + tb], W[:, k, :],
                                           start=(k == 0), stop=(k == KT - 1))
                        return ins
                    P.op("pe", fv, reads=[("w", sv[half])] + [("Xb", k) for k in range(KT)], writes=[psr(b)])
                    eng = "act" if (blk + half) % 2 == 0 else "dve"
                    if eng == "act":
                        P.op("act", lambda e, b=b, blk=blk, tb=tb, half=half: e.copy(
                            out=V[0:tb, blk, half * 512:(half + 1) * 512], in_=ps(b)[0:tb, :]),
                            reads=[psr(b)], writes=[("V", blk, half)])
                    else:
                        P.op("dve", lambda e, b=b, blk=blk, tb=tb, half=half: e.tensor_copy(
                            out=V[0:tb, blk, half * 512:(half + 1) * 512], in_=ps(b)[0:tb, :]),
                            reads=[psr(b)], writes=[("V", blk, half)])
                    PS.free(b)
                RG.release(sv[half])
            for (blk, tb) in blks:
                b = PS.alloc()
                pb = ps(b)[:, :].bitcast(BF16)

                def fkt(e, pb=pb, blk=blk, tb=tb):
                    ins = None
                    for h in range(4):
                        ins = e.transpose(out=pb[0:tb, h * 128:(h + 1) * 128], in_=Kt[:, h, blk * 128:blk * 128 + tb],
                                          identity=identb[:, :])
                    return ins
                P.op("pe", fkt, reads=[("Kt", h) for h in range(4)] + ["identb"], writes=[psr(b)])
                P.op("dve", lambda e, pb=pb, blk=blk, tb=tb: e.tensor_copy(out=KTm[0:tb, blk, :], in_=pb[0:tb, 0:512]),
                     reads=[psr(b)], writes=[("KTm", blk)])
                PS.free(b)
            def chunk(blk, tb):
                t0 = blk * 128
                orot = blk % 2
                pat = PS.alloc()

                def fat(e):
                    ins = None
                    for h in range(4):
                        ins = e.matmul(ps(pat)[0:tb, h * 128:h * 128 + tb], Kt[:, h, t0:t0 + tb], Q[:, h, t0:t0 + tb],
                                       start=True, stop=True)
                    return ins
                P.op("pe", fat, reads=[("Kt", h) for h in range(4)] + [("Q", h) for h in range(4)], writes=[psr(pat)])
                for h in range(4):
                    P.op("dve", lambda e, h=h: e.tensor_tensor(out=AT[0:tb, h, 0:tb], in0=ps(pat)[0:tb, h * 128:h * 128 + tb],
                                                                in1=trib[0:tb, 0:tb], op=ALU.mult),
                         reads=[psr(pat), "trib"], writes=[("AT", h)])
                PS.free(pat)
                pkv = [PS.alloc(), PS.alloc()]
                for h in range(4):
                    P.op("pe", lambda e, h=h: e.matmul(ps(pkv[h // 2])[:, (h % 2) * 256:(h % 2) * 256 + 256],
                                                       KTm[0:tb, blk, h * 128:(h + 1) * 128],
                                                       V[0:tb, blk, h * 256:(h + 1) * 256], start=True, stop=True),
                         reads=[("KTm", blk), ("V", blk, h // 2)], writes=[psr(pkv[h // 2])])
                po = [PS.alloc(), PS.alloc()]
                for h in range(4):
                    pob = po[h // 2]
                    oc = (h % 2) * 256
                    P.op("pe", lambda e, h=h, pob=pob, oc=oc: e.matmul(
                        ps(pob)[0:tb, oc:oc + 256], AT[0:tb, h, 0:tb], V[0:tb, blk, h * 256:(h + 1) * 256],
                        start=True, stop=False), reads=[("AT", h), ("V", blk, h // 2)], writes=[psr(pob)])
                    P.op("pe", lambda e, h=h, pob=pob, oc=oc: e.matmul(
                        ps(pob)[0:tb, oc:oc + 256], Q[:, h, t0:t0 + tb], Sb[:, lg, h, :],
                        start=False, stop=True), reads=[("Q", h), ("Sb", lg, h)], writes=[psr(pob)])
                for h in range(4):
                    el = EB[:, h, t0 + tb - 1:t0 + tb]
                    kc = (h % 2) * 256
                    P.op("act", lambda e, h=h, el=el, kc=kc: e.activation(out=TMP[:, h, :], in_=ps(pkv[h // 2])[:, kc:kc + 256],
                                                                          func=AF.Copy, scale=el),
                         reads=[psr(pkv[h // 2]), ("EB", h)], writes=[("TMP", h)])
                    P.op("dve", lambda e, h=h, el=el: e.scalar_tensor_tensor(
                        out=S32[:, lg, h, :], in0=S32[:, lg, h, :], scalar=el, in1=TMP[:, h, :],
                        op0=ALU.mult, op1=ALU.add), reads=[("S32", lg, h), ("TMP", h), ("EB", h)],
                        writes=[("S32", lg, h)])
                    P.op("dve", lambda e, h=h: e.tensor_copy(out=Sb[:, lg, h, :], in_=S32[:, lg, h, :]),
                         reads=[("S32", lg, h)], writes=[("Sb", lg, h)])
                PS.free(pkv[0]); PS.free(pkv[1])
                rs_js = list(range(blk * KT // len(blks), (blk + 1) * KT // len(blks)))
                rs_banks = []

                def rs_evac(j, pr):
                    gc = lg * 2 + (j % 2)
                    P.op("act", lambda e: e.activation(out=RS[:, j, 0:T], in_=ps(pr)[:, 0:T], func=AF.Silu),
                         reads=[psr(pr)], writes=[("RS", j)])
                    PS.free(pr)
                    P.op("dve", lambda e: e.tensor_scalar(out=RS[:, j, 0:T], in0=RS[:, j, 0:T],
                                                          scalar1=GN[:, gc:gc + 1], scalar2=None, op0=ALU.mult),
                         reads=[("RS", j), "GN"], writes=[("RS", j)])

                for j in rs_js:
                    if j % 4 == 0:
                        srs[j // 4] = RG.get()
                    pr = PS.alloc()
                    proj_fm(pr, srs[j // 4], (j % 4) * 128, T)
                    if j % 4 == 3:
                        RG.release(srs[j // 4])
                    if len(rs_js) <= 2:
                        rs_banks.append((j, pr))
                    else:
                        rs_evac(j, pr)
                for h in range(4):
                    pob = po[h // 2]
                    oc = (h % 2) * 256
                    P.op("act", lambda e, h=h, pob=pob, oc=oc: e.activation(
                        out=JUNK[0:tb, :], in_=ps(pob)[0:tb, oc:oc + 256], func=AF.Square,
                        accum_out=SS[0:tb, blk * 4 + h:blk * 4 + h + 1]),
                        reads=[psr(pob)], writes=[("SS", blk), "JUNK"])
                P.op("act", lambda e: e.activation(out=RH[0:tb, blk * 4:blk * 4 + 4], in_=SS[0:tb, blk * 4:blk * 4 + 4],
                                                   func=AF.Sqrt, scale=1.0 / 256.0, bias=EPS_RMS[0:tb, 0:1]),
                     reads=[("SS", blk), "EPS"], writes=[("RH", blk)])
                P.op("dve", lambda e: e.reciprocal(out=RH[0:tb, blk * 4:blk * 4 + 4], in_=RH[0:tb, blk * 4:blk * 4 + 4]),
                     reads=[("RH", blk)], writes=[("RH", blk)])
                for h in range(4):
                    pob = po[h // 2]
                    oc = (h % 2) * 256
                    P.op("act", lambda e, h=h, pob=pob, oc=oc: e.activation(
                        out=ON[0:tb, orot, h * 256:(h + 1) * 256], in_=ps(pob)[0:tb, oc:oc + 256], func=AF.Copy,
                        scale=RH[0:tb, blk * 4 + h:blk * 4 + h + 1]),
                        reads=[psr(pob), ("RH", blk)], writes=[("ON", orot)])
                PS.free(po[0]); PS.free(po[1])
                for (j, pr) in rs_banks:
                    rs_evac(j, pr)
                pt_ = PS.alloc()
                ptb = ps(pt_)[:, :].bitcast(BF16).rearrange("p (j t) -> p j t", j=KT)

                def ftr(e):
                    ins = None
                    for j in range(KT):
                        ins = e.transpose(out=ptb[:, j, 0:tb], in_=ON[0:tb, orot, j * 128:(j + 1) * 128],
                                          identity=identb[0:tb, 0:tb])
                    return ins
                P.op("pe", ftr, reads=[("ON", orot), "identb"], writes=[psr(pt_)])
                P.op("dve", lambda e: e.tensor_copy(out=M[:, :, t0:t0 + tb], in_=ptb[:, :, 0:tb]),
                     reads=[psr(pt_)], writes=[("M", j) for j in range(KT)])
                PS.free(pt_)

            srs = [None, None]
            for (blk, tb) in blks:
                chunk(blk, tb)
            for j in range(KT):
                P.op("dve", lambda e, j=j: e.tensor_tensor(out=M[:, j, 0:T], in0=M[:, j, 0:T], in1=RS[:, j, 0:T],
                                                            op=ALU.mult),
                     reads=[("M", j), ("RS", j)], writes=[("M", j)])
            so = [RG.get(), RG.get()]
            out_proj_and_post_ln(l, T, so, None)

        ycnt = [0]
        load_x(tiles[0])
        for ti, tile in enumerate(tiles):
            s, t0, T, is_meta = tile
            if "notile" in DBG:
                break
            if "nometa" in DBG and is_meta:
                if ti + 1 < len(tiles):
                    load_x(tiles[ti + 1])
                continue
            transpose_in(T)
            if ti + 1 < len(tiles):
                load_x(tiles[ti + 1])
            for l in range(depth):
                P.barrier()
                run_deferred()
                if l % 2 == 0:
                    conv_layer(l, T, is_meta)
                else:
                    gla_layer(l, T, is_meta)
            P.barrier()
            run_deferred()
            if not is_meta and "noout" not in DBG:
                transpose_out(tile, ycnt)
        P.emit(None, sems, dma_sems, [("yout", 0), ("yout", 1)])
    return nc


_CACHE = {}


def _consts():
    ident = np.eye(128, dtype=np.float32)
    tri = np.triu(np.ones((128, 128), dtype=np.float32))
    return ident, tri


def run(inputs, nseq_per_core, ncores, depth=DEPTH_FULL, trace=False):
    x = np.ascontiguousarray(inputs["x"], dtype=np.float32)
    seq = x.shape[1]
    key = (nseq_per_core, seq, depth)
    if key not in _CACHE:
        _CACHE[key] = build(nseq_per_core, seq, depth)
    nc = _CACHE[key]
    ident, tri = _consts()
    in_maps = []
    for c in range(ncores):
        m = {k: np.ascontiguousarray(v, dtype=np.float32) for k, v in inputs.items() if k != "x"}
        m["x"] = x[c * nseq_per_core:(c + 1) * nseq_per_core]
        m["c_ident"] = ident
        m["c_tri"] = tri
        in_maps.append(m)
    res = run_bass_kernel_spmd(nc, in_maps, core_ids=list(range(ncores)), trace=trace)
    out = np.concatenate([r["y"] for r in res.results], axis=0)
    return out, res


def kernel(**inputs):
    out, _ = run(inputs, inputs["x"].shape[0] // NCORES, NCORES)
    return out
```

```python
import numpy as np
import concourse.bass as bass
import concourse.mybir as mybir
from concourse.bass_utils import run_bass_kernel_spmd

F32 = mybir.dt.float32
BF16 = mybir.dt.bfloat16
AF = mybir.ActivationFunctionType
ALU = mybir.AluOpType

D = 1024
KT = 8
NMETA = 16
TAPS = 31
HALO = 30
DEPTH_FULL = 4
ALPHA = (2 * DEPTH_FULL) ** 0.25
LN_EPS = 1e-5
RMS_EPS = 1e-6
TT = 512
NCORES = 8
CONV_ROWS = 37
NRING = 6

COMPUTE = ("pe", "act", "dve", "pool")


class Prog:
    def __init__(self, nc):
        self.nc = nc
        self.ops = {e: [] for e in ("pe", "act", "dve", "pool", "sp")}
        self.last_w = {}
        self.readers = {}
        self.dma_cnt = {}
        self.bar = {e: set() for e in self.ops}

    def op(self, eng, fn, reads=(), writes=(), dma_key=None):
        lst = self.ops[eng]
        idx = len(lst)
        deps = {}

        def add(d, kind):
            if d is None:
                return
            if d in deps and deps[d] == "raw":
                return
            deps[d] = kind

        for r in reads:
            add(self.last_w.get(r), "raw")
        for r in writes:
            add(self.last_w.get(r), "waw")
            for rd in self.readers.get(r, ()):
                add(rd, "war")
        for d in self.bar[eng]:
            add(d, "raw")
        self.bar[eng] = set()
        rec = dict(fn=fn, deps=deps, dma_key=dma_key, dma_n=None, signal=False, ordinal=None)
        if dma_key is not None:
            n = self.dma_cnt.get(dma_key, 0) + 1
            self.dma_cnt[dma_key] = n
            rec["dma_n"] = n
        lst.append(rec)
        me = (eng, idx)
        for r in reads:
            s = self.readers.setdefault(r, set())
            if dma_key is None:
                for o in [o for o in s if o[0] == eng and self.ops[eng][o[1]]["dma_key"] is None]:
                    s.discard(o)
            s.add(me)
        for r in writes:
            self.last_w[r] = me
            self.readers[r] = set()
        return me

    def barrier(self):
        lasts = set()
        for e in COMPUTE:
            for i in range(len(self.ops[e]) - 1, -1, -1):
                if self.ops[e][i]["dma_key"] is None:
                    lasts.add((e, i))
                    break
        for e in COMPUTE:
            self.bar[e] = set(d for d in lasts if d[0] != e)

    def resolve(self):
        for eng, lst in self.ops.items():
            waited = {e: -1 for e in self.ops}
            waited_dma = {}
            for idx, rec in enumerate(lst):
                waits = []
                for (pe_, pi), kind in rec["deps"].items():
                    prod = self.ops[pe_][pi]
                    if prod["dma_key"] is not None:
                        k = prod["dma_key"]
                        if waited_dma.get(k, 0) >= prod["dma_n"]:
                            continue
                        waited_dma[k] = prod["dma_n"]
                        waits.append(("dma", k, prod["dma_n"]))
                        continue
                    if pe_ == eng:
                        if eng in ("pe", "sp"):
                            continue
                    if waited[pe_] >= pi:
                        continue
                    waited[pe_] = pi
                    prod["signal"] = True
                    waits.append(("eng", pe_, pi))
                rec["waits"] = waits
        for eng, lst in self.ops.items():
            n = 0
            for rec in lst:
                if rec["signal"]:
                    n += 1
                    rec["ordinal"] = n

    def emit(self, block_ctx_factory, sems, dma_sems, final_waits):
        self.resolve()
        nc = self.nc
        P = self

        def run(eng_name):
            def body(e):
                for rec in P.ops[eng_name]:
                    for w in rec["waits"]:
                        if w[0] == "dma":
                            e.wait_ge(dma_sems[w[1]], 16 * w[2])
                        else:
                            e.wait_ge(sems[w[1]], P.ops[w[1]][w[2]]["ordinal"])
                    ins = rec["fn"](e)
                    if rec["dma_key"] is not None:
                        ins.then_inc(dma_sems[rec["dma_key"]], 16)
                    elif rec["signal"]:
                        ins.then_inc(sems[eng_name], 1)
                if eng_name == "sp":
                    for k in final_waits:
                        if P.dma_cnt.get(k, 0):
                            e.wait_ge(dma_sems[k], 16 * P.dma_cnt[k])
            return body

        with nc.Block() as block:
            block.tensor(run("pe"))
            block.scalar(run("act"))
            block.vector(run("dve"))
            block.gpsimd(run("pool"))
            block.sync(run("sp"))


def layer_groups(depth):
    groups = []
    for l in range(depth):
        j = l // 2
        if l % 2 == 0:
            for c0 in (0, 1024, 512, 1536, 2048, 2560):
                groups.append((l, "conv_w_in", j, c0))
            for c0 in (0, 512):
                groups.append((l, "conv_w_out", j, c0))
        else:
            for c0 in (1024, 1536, 0, 512, 2048, 2560):
                groups.append((l, "gla_w_in", j, c0))
            for c0 in (0, 512):
                groups.append((l, "gla_w_out", j, c0))
    return groups


def build(nseq, seq, depth):
    assert seq % TT == 0
    nc = bass.Bass("TRN2", target_bir_lowering=False)
    n_conv = (depth + 1) // 2
    n_gla = depth // 2
    dr = {}

    def din(name, shape):
        dr[name] = nc.dram_tensor(name, list(shape), F32, kind="ExternalInput").ap()
        return dr[name]

    x = din("x", (nseq, seq, D))
    meta = din("meta", (NMETA, D))
    din("conv_w_in", (2, D, 3072)); din("conv_b_in", (2, 3072)); din("conv_w_dw", (2, TAPS, D))
    din("conv_b_dw", (2, D)); din("conv_norm_g", (2, D)); din("conv_norm_b", (2, D))
    din("conv_w_out", (2, D, D)); din("conv_b_out", (2, D))
    din("gla_w_in", (2, D, 3088)); din("gla_w_a2", (2, 16, 512)); din("gla_b_a", (2, 512))
    din("gla_norm_g", (2, 256)); din("gla_w_out", (2, D, D))
    din("post_ln_g", (4, D)); din("post_ln_b", (4, D))
    ident_d = din("c_ident", (128, 128))
    tri_d = din("c_tri", (128, 128))
    y = nc.dram_tensor("y", [nseq, seq, D], F32, kind="ExternalOutput").ap()

    groups = layer_groups(depth)
    NG = len(groups)
    wscr = nc.dram_tensor("wscr", [max(NG, 1), 128, KT * 512], BF16, kind="Internal").ap()

    P = Prog(nc)
    import contextlib
    es = contextlib.ExitStack()
    with es:
        def sb(name, shape, dt):
            return es.enter_context(nc.sbuf_tensor(name, list(shape), dt))

        XIN = sb("XIN", [128, 4, D], F32)
        YOUT = sb("YOUT", [128, 2, D], F32)
        X = sb("X", [128, KT, TT], F32)
        Xb = sb("Xb", [128, KT, TT], BF16)
        WR = sb("WR", [128, NRING, KT * 512], BF16)
        WSM = sb("WSM", [128, 2, KT, 16], BF16)
        WA2 = sb("WA2", [16, 2, 512], BF16)
        BAROW = sb("BAROW", [1, 2, 512], BF16)
        BOROW = sb("BOROW", [1, 2, D], BF16)
        HAL = sb("HAL", [128, 2, KT, HALO], BF16)
        S32 = sb("S32", [128, 2, 4, 256], F32)
        Sb = sb("Sb", [128, 2, 4, 256], BF16)
        PROW2 = sb("PROW2", [4, 128], F32)
        PT = sb("PT", [128, KT, 88], F32)
        GN = sb("GN", [128, 4], F32)
        ident32 = sb("ident32", [128, 128], F32)
        tri32 = sb("tri32", [128, 128], F32)
        identb = sb("identb", [128, 128], BF16)
        trib = sb("trib", [128, 128], BF16)
        ones32 = sb("ones32", [128, 128], F32)
        onesb = sb("onesb", [128, TT], BF16)
        MEAN = sb("MEAN", [128, TT], F32)
        EPS_LN = sb("EPS_LN", [128, 1], F32)
        EPS_RMS = sb("EPS_RMS", [128, 1], F32)
        MSQ = sb("MSQ", [128, TT], F32)
        RSTD = sb("RSTD", [128, TT], F32)
        T1 = sb("T1", [128, 2, TT], F32)
        SQ = sb("SQ", [128, 2, TT], BF16)
        M = sb("M", [128, KT, TT], BF16)
        ARENA = sb("ARENA", [128, 15 * 1024], F32)
        PSt = [es.enter_context(nc.psum_tensor(f"ps{i}", [128, 512], F32)) for i in range(8)]

        class Arena:
            def __init__(self):
                self.off = 0

            def reset(self):
                self.off = 0

            def take(self, n_elems, dt, shape_str=None, **kw):
                nb = n_elems * (4 if dt == F32 else 2)
                nw = (nb + 3) // 4
                nw = (nw + 7) // 8 * 8
                v = ARENA[:, self.off:self.off + nw]
                self.off += nw
                assert self.off <= 15 * 1024, self.off
                if dt != F32:
                    v = v.bitcast(dt)
                v = v[:, 0:n_elems]
                if shape_str:
                    v = v.rearrange(shape_str, **kw)
                return v

        AR = Arena()
        PROW = AR.take(D, F32)

        sem_names = ["pe", "act", "dve", "pool", "sp"]
        sems = {n: es.enter_context(nc.semaphore("s_" + n)) for n in sem_names}
        dma_keys = [("w", s) for s in range(NRING)] + [("cv", i) for i in range(8)] + \
                   [("xin",), ("yout", 0), ("yout", 1)] + [("par", i) for i in range(4)]
        dma_sems = {k: es.enter_context(nc.semaphore("d_" + "_".join(str(a) for a in k))) for k in dma_keys}

        class Banks:
            def __init__(self):
                self.free_list = list(range(6))

            def alloc(self):
                assert self.free_list, "PSUM pool exhausted"
                return self.free_list.pop(0)

            def free(self, b):
                self.free_list.append(b)

        PS = Banks()
        ST1, ST2 = 6, 7

        def ps(b):
            return PSt[b]

        def psr(b):
            return ("ps", b)

        import os
        DBG = os.environ.get("KDBG", "")
        P.op("sp", lambda e: e.dma_start(out=ident32[:], in_=ident_d), writes=["ident32"], dma_key=("par", 0))
        P.op("sp", lambda e: e.dma_start(out=tri32[:], in_=tri_d), writes=["tri32"], dma_key=("par", 1))
        P.op("dve", lambda e: e.tensor_copy(out=identb[:], in_=ident32[:]), reads=["ident32"], writes=["identb"])
        P.op("dve", lambda e: e.tensor_copy(out=trib[:], in_=tri32[:]), reads=["tri32"], writes=["trib"])
        P.op("dve", lambda e: e.memset(ones32[:], 1.0), writes=["ones32"])
        P.op("dve", lambda e: e.memset(EPS_LN[:], LN_EPS), writes=["EPS"])
        P.op("dve", lambda e: e.memset(EPS_RMS[:], RMS_EPS), writes=["EPS"])
        P.op("dve", lambda e: e.memset(onesb[:], 1.0), writes=["onesb"])
        P.op("dve", lambda e: e.memset(PROW[:], 0.0), writes=["PROW"])

        def prow_dma(dst_rows, src, key_i):
            P.op("sp", lambda e: e.dma_start(out=dst_rows, in_=src), reads=[], writes=["PROW"],
                 dma_key=("par", key_i))

        r = 0
        for j in range(2 if "noparam" not in DBG else 0):
            base = j * CONV_ROWS
            prow_dma(PROW[base:base + 3, :], dr["conv_b_in"][j].rearrange("(a n) -> a n", a=3), 2)
            prow_dma(PROW[base + 3:base + 34, :], dr["conv_w_dw"][j], 2)
            prow_dma(PROW[base + 34:base + 35, :], dr["conv_b_dw"][j:j + 1, :], 2)
            prow_dma(PROW[base + 35:base + 36, :], dr["conv_norm_g"][j:j + 1, :], 2)
            prow_dma(PROW[base + 36:base + 37, :], dr["conv_norm_b"][j:j + 1, :], 2)
        LNB = 2 * CONV_ROWS
        for l in range(4 if "noparam" not in DBG else 0):
            prow_dma(PROW[LNB + 2 * l:LNB + 2 * l + 1, :], dr["post_ln_g"][l:l + 1, :], 2)
            prow_dma(PROW[LNB + 2 * l + 1:LNB + 2 * l + 2, :], dr["post_ln_b"][l:l + 1, :], 2)
        NROWS = LNB + 8
        P.op("sp", lambda e: e.dma_start(out=PROW2[:], in_=dr["gla_norm_g"].rearrange("l (a n) -> (l a) n", a=2)),
             writes=["PROW2"], dma_key=("par", 3))
        for i in range(KT if "nopt" not in DBG else 0):
            b = PS.alloc()
            P.op("pe", lambda e, i=i, b=b: e.transpose(out=ps(b)[:, 0:NROWS], in_=PROW[0:NROWS, i * 128:(i + 1) * 128],
                                                       identity=ident32[0:NROWS, 0:NROWS]),
                 reads=["PROW", "ident32"], writes=[psr(b)])
            P.op("act", lambda e, i=i, b=b: e.copy(out=PT[:, i, 0:NROWS], in_=ps(b)[:, 0:NROWS]),
                 reads=[psr(b)], writes=["PT"])
            PS.free(b)
        b = PS.alloc()
        P.op("pe", lambda e, b=b: e.transpose(out=ps(b)[:, 0:4], in_=PROW2[0:4, :], identity=ident32[0:4, 0:4]),
             reads=["PROW2", "ident32"], writes=[psr(b)])
        P.op("act", lambda e, b=b: e.copy(out=GN[:], in_=ps(b)[:, 0:4]), reads=[psr(b)], writes=["GN"])
        PS.free(b)
        if "nosmall" in DBG:
            class _N:
                def op(self, *a, **k): pass
            P_ = P; P = _N()
        P.op("pool", lambda e: e.dma_start(out=WA2[:], in_=dr["gla_w_a2"].rearrange("l r n -> r l n")),
             writes=["WA2", ("cvslot", 0)], dma_key=("cv", 0))
        P.op("pool", lambda e: e.dma_start(out=BAROW[:], in_=dr["gla_b_a"].rearrange("(o l) n -> o l n", o=1)),
             writes=["BAROW", ("cvslot", 1)], dma_key=("cv", 1))
        P.op("pool", lambda e: e.dma_start(out=BOROW[:], in_=dr["conv_b_out"].rearrange("(o l) n -> o l n", o=1)),
             writes=["BOROW", ("cvslot", 2)], dma_key=("cv", 2))
        P.op("pool", lambda e: e.dma_start(
            out=WSM[:], in_=dr["gla_w_in"][:, :, 3072:3088].rearrange("l (kt p) n -> p l kt n", p=128)),
            writes=["WSM", ("cvslot", 3)], dma_key=("cv", 3))
        if "nosmall" in DBG:
            P = P_
        for g, (l, nm, j, c0) in enumerate(groups):
            src = dr[nm][j][:, c0:c0 + 512].rearrange("(kt p) n -> p kt n", p=128)
            dst = wscr[g].rearrange("p (kt n) -> p kt n", kt=KT)
            P.op("pool", lambda e, src=src, dst=dst: e.dma_start(out=dst, in_=src),
                 writes=[("wscr", g), ("cvslot", g % 8)], dma_key=("cv", g % 8))

        tiles = []
        for s in range(nseq):
            tiles.append((s, 0, NMETA, True))
            for t in range(seq // TT):
                tiles.append((s, t * TT, TT, False))
        wseq = []
        for _ in tiles:
            for g in range(NG):
                wseq.append(g)

        class Ring:
            def __init__(self):
                self.next_load = 0
                self.next_use = 0

            def issue(self):
                if self.next_load >= len(wseq):
                    return
                n = self.next_load
                g = wseq[n]
                slot = n % NRING
                self.next_load += 1
                P.op("sp", lambda e, g=g, slot=slot: e.dma_start(out=WR[:, slot, :], in_=wscr[g]),
                     reads=[("wscr", g)], writes=[("w", slot)], dma_key=("w", slot))

            def get(self):
                n = self.next_use
                self.next_use += 1
                assert n < self.next_load
                return n % NRING

            def release(self, slot):
                self.issue()

        RG = Ring()
        for _ in range(NRING):
            RG.issue()

        def wv(slot):
            return WR[:, slot, :].rearrange("p (kt n) -> p kt n", kt=KT)

        DEFER = []

        def run_deferred():
            while DEFER:
                DEFER.pop(0)()

        def load_x(tile):
            s, t0, T, is_meta = tile
            if is_meta:
                P.op("sp", lambda e: e.dma_start(out=XIN[0:NMETA, 0, :], in_=meta), writes=["XIN"], dma_key=("xin",))
            else:
                P.op("sp", lambda e, s=s, t0=t0: e.dma_start(
                    out=XIN[:, :, :], in_=x[s, t0:t0 + TT, :].rearrange("(nb p) d -> p nb d", p=128)),
                    writes=["XIN"], dma_key=("xin",))

        def blocks(T):
            return [(b, min(128, T - b * 128)) for b in range((T + 127) // 128)]

        def transpose_in(T):
            for i in range(KT):
                b = PS.alloc()
                for (blk, tb) in blocks(T):
                    if "onetr" in DBG and blk > 0:
                        continue
                    if "nope" in DBG:
                        continue
                    P.op("pe", lambda e, i=i, b=b, blk=blk, tb=tb: e.transpose(
                        out=ps(b)[:, blk * 128:blk * 128 + tb], in_=XIN[0:tb, blk, i * 128:(i + 1) * 128],
                        identity=ident32[0:tb, 0:tb]), reads=["XIN", "ident32"], writes=[psr(b)])
                if "noact" not in DBG:
                    P.op("act", lambda e, i=i, b=b: e.copy(out=X[:, i, 0:T], in_=ps(b)[:, 0:T]),
                         reads=[psr(b)], writes=[("X", i)])
                if "nodve" not in DBG:
                    P.op("dve", lambda e, i=i, b=b: e.tensor_copy(out=Xb[:, i, 0:T], in_=X[:, i, 0:T]),
                         reads=[("X", i)], writes=[("Xb", i)])
                PS.free(b)

        def transpose_out(tile, cnt):
            s, t0, T, _ = tile
            for (blk, tb) in blocks(T):
                rot = cnt[0] % 2
                cnt[0] += 1
                ba, bb = PS.alloc(), PS.alloc()
                for i in range(KT):
                    bk = ba if i < 4 else bb
                    P.op("pe", lambda e, i=i, bk=bk, blk=blk: e.transpose(
                        out=ps(bk)[:, (i % 4) * 128:(i % 4 + 1) * 128], in_=X[:, i, blk * 128:(blk + 1) * 128],
                        identity=ident32[:, :]), reads=[("X", i), "ident32"], writes=[psr(bk)])
                P.op("act", lambda e, rot=rot, ba=ba: e.copy(out=YOUT[:, rot, 0:512], in_=ps(ba)[:, :]),
                     reads=[psr(ba)], writes=[("YOUT", rot, 0)])
                P.op("dve", lambda e, rot=rot, bb=bb: e.tensor_copy(out=YOUT[:, rot, 512:1024], in_=ps(bb)[:, :]),
                     reads=[psr(bb)], writes=[("YOUT", rot, 1)])
                PS.free(ba)
                PS.free(bb)
                P.op("sp", lambda e, rot=rot, s=s, t0=t0, blk=blk: e.dma_start(
                    out=y[s, t0 + blk * 128:t0 + (blk + 1) * 128, :], in_=YOUT[:, rot, :]),
                    reads=[("YOUT", rot, 0), ("YOUT", rot, 1)], dma_key=("yout", rot))

        def proj_fm(bank, slot, c0, T, extra=None):
            W = wv(slot)

            def fn(e):
                ins = None
                for k in range(KT):
                    ins = e.matmul(ps(bank)[:, 0:T], W[:, k, c0:c0 + 128], Xb[:, k, 0:T],
                                   start=(k == 0), stop=(k == KT - 1 and extra is None))
                if extra is not None:
                    ins = extra(e)
                return ins
            P.op("pe", fn, reads=[("w", slot)] + [("Xb", k) for k in range(KT)], writes=[psr(bank)])

        def stats_mm(src_ap, sq_ap, first, last, reads):
            def fn(e):
                e.matmul(ps(ST1)[:, 0:src_ap.shape[-1]], ones32[:, :], src_ap, start=first, stop=last)
                return e.matmul(ps(ST2)[:, 0:src_ap.shape[-1]], onesb[:, 0:128], sq_ap, start=first, stop=last)
            P.op("pe", fn, reads=reads + ["ones32", "onesb"], writes=[psr(ST1), psr(ST2)])

        def stats_finish(T, eps):
            P.op("dve", lambda e: e.tensor_scalar(out=MEAN[:, 0:T], in0=ps(ST1)[:, 0:T], scalar1=1.0 / D, scalar2=None,
                                                  op0=ALU.mult), reads=[psr(ST1)], writes=["MEAN"])
            P.op("dve", lambda e: e.tensor_tensor(out=MSQ[:, 0:T], in0=MEAN[:, 0:T], in1=MEAN[:, 0:T], op=ALU.mult),
                 reads=["MEAN"], writes=["MSQ"])
            P.op("dve", lambda e: e.scalar_tensor_tensor(out=MSQ[:, 0:T], in0=ps(ST2)[:, 0:T], scalar=1.0 / D,
                                                         in1=MSQ[:, 0:T], op0=ALU.mult, op1=ALU.subtract),
                 reads=[psr(ST2), "MSQ"], writes=["MSQ"])
            P.op("act", lambda e: e.activation(out=MSQ[:, 0:T], in_=MSQ[:, 0:T], func=AF.Sqrt, bias=EPS_LN[:, 0:1]),
                 reads=["MSQ", "EPS"], writes=["MSQ"])
            P.op("dve", lambda e: e.reciprocal(out=RSTD[:, 0:T], in_=MSQ[:, 0:T]), reads=["MSQ"], writes=["RSTD"])

        def out_proj_and_post_ln(l, T, slots, bias_row):
            pending = []
            for i in range(KT):
                slot = slots[i // 4]
                c0 = (i % 4) * 128
                W = wv(slot)
                b = PS.alloc()

                def fn(e, W=W, c0=c0, b=b, i=i):
                    ins = None
                    for j in range(KT):
                        ins = e.matmul(ps(b)[:, 0:T], W[:, j, c0:c0 + 128], M[:, j, 0:T], start=(j == 0),
                                       stop=(j == KT - 1 and bias_row is None))
                    if bias_row is not None:
                        ins = e.matmul(ps(b)[:, 0:T], bias_row[0:1, i * 128:(i + 1) * 128], onesb[0:1, 0:T],
                                       start=False, stop=True)
                    return ins
                P.op("pe", fn, reads=[("w", slot), "BOROW", "onesb"] + [("M", j) for j in range(KT)], writes=[psr(b)])
                if i % 4 == 3:
                    RG.release(slot)
                P.op("dve", lambda e, i=i, b=b: e.scalar_tensor_tensor(
                    out=X[:, i, 0:T], in0=X[:, i, 0:T], scalar=float(ALPHA), in1=ps(b)[:, 0:T],
                    op0=ALU.mult, op1=ALU.add), reads=[psr(b), ("X", i)], writes=[("X", i)])
                PS.free(b)
                rot = i % 2
                P.op("act", lambda e, i=i, rot=rot: e.activation(out=SQ[:, rot, 0:T], in_=X[:, i, 0:T], func=AF.Square),
                     reads=[("X", i)], writes=[("SQ", rot)])
                pending.append((i, rot))
                if len(pending) > 1:
                    pi, prot = pending.pop(0)
                    stats_mm(X[:, pi, 0:T], SQ[:, prot, 0:T], pi == 0, False, [("X", pi), ("SQ", prot)])
            pi, prot = pending.pop(0)
            stats_mm(X[:, pi, 0:T], SQ[:, prot, 0:T], False, True, [("X", pi), ("SQ", prot)])
            stats_finish(T, LN_EPS)
            gcol = LNB + 2 * l
            last = (l == depth - 1)

            def xfin(i):
                if last:
                    P.op("act", lambda e: e.activation(
                        out=X[:, i, 0:T], in_=X[:, i, 0:T], func=AF.Identity,
                        scale=PT[:, i, gcol:gcol + 1], bias=PT[:, i, gcol + 1:gcol + 2]),
                        reads=[("X", i), "PT"], writes=[("X", i)])
                else:
                    P.op("pool", lambda e: e.tensor_scalar(
                        out=X[:, i, 0:T], in0=X[:, i, 0:T], scalar1=PT[:, i, gcol:gcol + 1],
                        scalar2=PT[:, i, gcol + 1:gcol + 2], op0=ALU.mult, op1=ALU.add),
                        reads=[("X", i), "PT"], writes=[("X", i)])

            for i in range(KT + 1):
                if i < KT:
                    P.op("dve", lambda e, i=i: e.tensor_tensor(out=X[:, i, 0:T], in0=X[:, i, 0:T],
                                                                in1=MEAN[:, 0:T], op=ALU.subtract),
                         reads=[("X", i), "MEAN"], writes=[("X", i)])
                if i >= 1:
                    k = i - 1
                    P.op("dve", lambda e, k=k: e.tensor_tensor(out=X[:, k, 0:T], in0=X[:, k, 0:T],
                                                                in1=RSTD[:, 0:T], op=ALU.mult),
                         reads=[("X", k), "RSTD"], writes=[("X", k)])
                    P.op("act", lambda e, k=k: e.activation(
                        out=Xb[:, k, 0:T], in_=X[:, k, 0:T], func=AF.Identity,
                        scale=PT[:, k, gcol:gcol + 1], bias=PT[:, k, gcol + 1:gcol + 2]),
                        reads=[("X", k), "PT"], writes=[("Xb", k)])
                    if last:
                        xfin(k)
                    else:
                        DEFER.append(lambda k=k: xfin(k))

        def conv_layer(l, T, first):
            lc = l // 2
            base = lc * CONV_ROWS
            AR.reset()
            GLU = AR.take(KT * (TT + HALO), BF16, "p (k t) -> p k t", k=KT)
            ZS = AR.take(KT * TT, BF16, "p (k t) -> p k t", k=KT)
            C = AR.take(KT * TT, F32, "p (k t) -> p k t", k=KT)
            SIG = AR.take(2 * TT, BF16, "p (k t) -> p k t", k=2)
            CS = AR.take(2 * TT, BF16, "p (k t) -> p k t", k=2)
            DG = AR.take(2 * TAPS * 128, BF16, "p (r t c) -> p r t c", r=2, t=TAPS)
            if first:
                P.op("dve", lambda e: e.memset(HAL[:, lc, :, :], 0.0), writes=[("HAL", lc)])
            P.op("dve", lambda e: e.tensor_copy(out=GLU[:, :, 0:HALO], in_=HAL[:, lc, :, :]),
                 reads=[("HAL", lc)], writes=[("GLU", j) for j in range(KT)])
            slots = {}

            def proj(j):
                if j % 4 == 0:
                    slots["a"], slots["g"] = RG.get(), RG.get()
                c0 = (j % 4) * 128
                pa, pg = PS.alloc(), PS.alloc()
                proj_fm(pg, slots["g"], c0, T)
                proj_fm(pa, slots["a"], c0, T)
                if j % 4 == 3:
                    RG.release(slots["a"]); RG.release(slots["g"])
                rot = j % 2
                P.op("act", lambda e: e.activation(out=SIG[:, rot, 0:T], in_=ps(pg)[:, 0:T], func=AF.Sigmoid,
                                                   bias=PT[:, j, base + 1:base + 2]),
                     reads=[psr(pg), "PT"], writes=[("SIG", rot)])
                P.op("dve", lambda e: e.scalar_tensor_tensor(
                    out=GLU[:, j, HALO:HALO + T], in0=ps(pa)[:, 0:T], scalar=PT[:, j, base:base + 1],
                    in1=SIG[:, rot, 0:T], op0=ALU.add, op1=ALU.mult),
                    reads=[psr(pa), ("SIG", rot), "PT"], writes=[("GLU", j)])
                PS.free(pg); PS.free(pa)

                def dg_dve(e):
                    ins = None
                    for tap in range(TAPS):
                        ins = e.tensor_scalar(out=DG[:, rot, tap, :], in0=identb[:, :],
                                              scalar1=PT[:, j, base + 3 + tap:base + 4 + tap], scalar2=None,
                                              op0=ALU.mult)
                    return ins

                def dg_act(e):
                    ins = None
                    for tap in range(TAPS):
                        ins = e.activation(out=DG[:, rot, tap, :], in_=identb[:, :], func=AF.Copy,
                                           scale=PT[:, j, base + 3 + tap:base + 4 + tap])
                    return ins
                if j % 2 == 0:
                    P.op("dve", dg_dve, reads=["identb", "PT"], writes=[("DG", rot)])
                else:
                    P.op("act", dg_act, reads=["identb", "PT"], writes=[("DG", rot)])

            def conv(j):
                rot = j % 2
                pc = PS.alloc()

                def fn(e):
                    ins = None
                    for tap in range(TAPS):
                        ins = e.matmul(ps(pc)[:, 0:T], DG[:, rot, tap, :], GLU[:, j, tap:tap + T],
                                       start=(tap == 0), stop=(tap == TAPS - 1))
                    return ins
                P.op("pe", fn, reads=[("DG", rot), ("GLU", j)], writes=[psr(pc)])
                P.op("act", lambda e: e.activation(out=C[:, j, 0:T], in_=ps(pc)[:, 0:T], func=AF.Identity,
                                                   bias=PT[:, j, base + 34:base + 35]),
                     reads=[psr(pc), "PT"], writes=[("C", j)])
                P.op("act", lambda e: e.activation(out=SQ[:, rot, 0:T], in_=ps(pc)[:, 0:T], func=AF.Square,
                                                   bias=PT[:, j, base + 34:base + 35]),
                     reads=[psr(pc), "PT"], writes=[("SQ", rot)])
                PS.free(pc)

            def stats(j):
                stats_mm(C[:, j, 0:T], SQ[:, j % 2, 0:T], j == 0, j == KT - 1, [("C", j), ("SQ", j % 2)])

            for j in range(KT + 2):
                if j < KT:
                    proj(j)
                if 1 <= j <= KT:
                    conv(j - 1)
                if j >= 2:
                    stats(j - 2)
            P.op("dve", lambda e: e.tensor_copy(out=HAL[:, lc, :, :], in_=GLU[:, :, T:T + HALO]),
                 reads=[("GLU", j) for j in range(KT)], writes=[("HAL", lc)])
            def zproj(j):
                if j % 4 == 0:
                    slots["z"] = RG.get()
                pz = PS.alloc()
                proj_fm(pz, slots["z"], (j % 4) * 128, T)
                if j % 4 == 3:
                    RG.release(slots["z"])
                P.op("act", lambda e: e.activation(out=ZS[:, j, 0:T], in_=ps(pz)[:, 0:T], func=AF.Silu,
                                                   bias=PT[:, j, base + 2:base + 3]),
                     reads=[psr(pz), "PT"], writes=[("ZS", j)])
                PS.free(pz)

            for j in range(KT):
                zproj(j)
                if j == 1:
                    stats_finish(T, LN_EPS)
            for j in range(KT):
                rot = j % 2
                P.op("dve", lambda e, j=j, rot=rot: e.tensor_tensor(out=T1[:, rot, 0:T], in0=C[:, j, 0:T],
                                                                     in1=MEAN[:, 0:T], op=ALU.subtract),
                     reads=[("C", j), "MEAN"], writes=[("T1", rot)])
                P.op("dve", lambda e, rot=rot: e.tensor_tensor(out=T1[:, rot, 0:T], in0=T1[:, rot, 0:T],
                                                               in1=RSTD[:, 0:T], op=ALU.mult),
                     reads=[("T1", rot), "RSTD"], writes=[("T1", rot)])
                P.op("act", lambda e, j=j, rot=rot: e.activation(
                    out=CS[:, rot, 0:T], in_=T1[:, rot, 0:T], func=AF.Silu,
                    scale=PT[:, j, base + 35:base + 36], bias=PT[:, j, base + 36:base + 37]),
                    reads=[("T1", rot), "PT"], writes=[("CS", rot)])
                P.op("dve", lambda e, j=j, rot=rot: e.tensor_tensor(out=M[:, j, 0:T], in0=CS[:, rot, 0:T],
                                                                     in1=ZS[:, j, 0:T], op=ALU.mult),
                     reads=[("CS", rot), ("ZS", j)], writes=[("M", j)])
            so = [RG.get(), RG.get()]
            out_proj_and_post_ln(l, T, so, BOROW[:, lc, :])

        def gla_layer(l, T, first):
            lg = l // 2
            AR.reset()
            A1 = AR.take(TT, BF16)
            G = AR.take(4 * 512, F32, "p (b n) -> p b n", b=4)
            E = AR.take(1 * 512, F32, "p (r n) -> p r n", r=1)
            EB = AR.take(4 * TT, F32, "p (h t) -> p h t", h=4)
            ENB = AR.take(1 * TT, F32, "p (r t) -> p r t", r=1)
            Q = AR.take(4 * TT, BF16, "p (h t) -> p h t", h=4)
            Kt = AR.take(4 * TT, BF16, "p (h t) -> p h t", h=4)
            KTm = AR.take(4 * 512, BF16, "p (b n) -> p b n", b=4)
            V = AR.take(4 * D, BF16, "p (b n) -> p b n", b=4)
            RS = AR.take(KT * TT, BF16, "p (k t) -> p k t", k=KT)
            AT = AR.take(4 * 128, BF16, "p (r t) -> p r t", r=4)
            TMP = AR.take(4 * 256, F32, "p (r t) -> p r t", r=4)
            ON = AR.take(2 * D, BF16, "p (r t) -> p r t", r=2)
            SS = AR.take(16, F32)
            RH = AR.take(16, F32)
            JUNK = AR.take(256, BF16)
            blks = blocks(T)
            if first:
                P.op("dve", lambda e: e.memset(S32[:, lg, :, :], 0.0), writes=[("S32", lg, h) for h in range(4)])
                P.op("dve", lambda e: e.memset(Sb[:, lg, :, :], 0.0), writes=[("Sb", lg, h) for h in range(4)])
            b = PS.alloc()

            def fa1(e, b=b):
                ins = None
                for k in range(KT):
                    ins = e.matmul(ps(b)[0:16, 0:T], WSM[:, lg, k, :], Xb[:, k, 0:T], start=(k == 0), stop=(k == KT - 1))
                return ins
            P.op("pe", fa1, reads=["WSM"] + [("Xb", k) for k in range(KT)], writes=[psr(b)])
            P.op("dve", lambda e, b=b: e.tensor_copy(out=A1[0:16, 0:T], in_=ps(b)[0:16, 0:T]), reads=[psr(b)], writes=["A1"])
            PS.free(b)
            sv = [RG.get(), RG.get()]

            def vhalf(half):
                W = wv(sv[half])
                for (blk, tb) in blks:
                    b = PS.alloc()

                    def fv(e, W=W, b=b, blk=blk, tb=tb):
                        ins = None
                        for k in range(KT):
                            ins = e.matmul(ps(b)[0:tb, :], Xb[:, k, blk * 128:blk * 128 + tb], W[:, k, :],
                                           start=(k == 0), stop=(k == KT - 1))
                        return ins
                    P.op("pe", fv, reads=[("w", sv[half])] + [("Xb", k) for k in range(KT)], writes=[psr(b)])
                    eng = "act" if (blk + half) % 2 == 0 else "dve"
                    if eng == "act":
                        P.op("act", lambda e, b=b, blk=blk, tb=tb, half=half: e.copy(
                            out=V[0:tb, blk, half * 512:(half + 1) * 512], in_=ps(b)[0:tb, :]),
                            reads=[psr(b)], writes=[("V", blk, half)])
                    else:
                        P.op("dve", lambda e, b=b, blk=blk, tb=tb, half=half: e.tensor_copy(
                            out=V[0:tb, blk, half * 512:(half + 1) * 512], in_=ps(b)[0:tb, :]),
                            reads=[psr(b)], writes=[("V", blk, half)])
                    PS.free(b)
                RG.release(sv[half])

            vhalf(0)
            for (blk, tb) in blks:
                b = PS.alloc()
                rot = 0

                def fg(e, b=b, blk=blk, tb=tb):
                    e.matmul(ps(b)[0:tb, :], A1[0:16, blk * 128:blk * 128 + tb], WA2[0:16, lg, :], start=True, stop=False)
                    return e.matmul(ps(b)[0:tb, :], onesb[0:1, 0:tb], BAROW[0:1, lg, :], start=False, stop=True)
                P.op("pe", fg, reads=["A1", "WA2", "BAROW", "onesb"], writes=[psr(b)])
                P.op("act", lambda e, b=b, tb=tb, rot=rot: e.activation(out=E[0:tb, rot, :], in_=ps(b)[0:tb, :],
                                                                        func=AF.Exp, scale=-1.0),
                     reads=[psr(b)], writes=[("E", rot)])
                PS.free(b)
                P.op("act", lambda e, blk=blk, tb=tb, rot=rot: e.activation(out=G[0:tb, blk, :], in_=E[0:tb, rot, :],
                                                                            func=AF.Ln, bias=1.0),
                     reads=[("E", rot)], writes=[("G", blk)])
            vhalf(1)
            sq, sk = RG.get(), RG.get()
            for h in range(4):
                bbc, bq, bk = PS.alloc(), PS.alloc(), PS.alloc()

                def fbc(e, h=h, bbc=bbc):
                    ins = None
                    for (blk, tb) in blks:
                        ins = e.matmul(ps(bbc)[:, blk * 128:blk * 128 + tb], G[0:tb, blk, h * 128:(h + 1) * 128],
                                       tri32[0:tb, 0:tb], start=True, stop=True)
                    return ins
                P.op("pe", fbc, reads=[("G", blk) for (blk, _) in blks] + ["tri32"], writes=[psr(bbc)])
                proj_fm(bq, sq, h * 128, T)
                proj_fm(bk, sk, h * 128, T)
                rot = 0
                P.op("act", lambda e, h=h, bbc=bbc: e.activation(out=EB[:, h, 0:T], in_=ps(bbc)[:, 0:T], func=AF.Exp,
                                                                 scale=-1.0 / 16.0),
                     reads=[psr(bbc)], writes=[("EB", h)])
                P.op("act", lambda e, rot=rot, bbc=bbc: e.activation(out=ENB[:, rot, 0:T], in_=ps(bbc)[:, 0:T],
                                                                     func=AF.Exp, scale=1.0 / 16.0),
                     reads=[psr(bbc)], writes=[("ENB", rot)])
                P.op("dve", lambda e, h=h, bq=bq: e.scalar_tensor_tensor(
                    out=Q[:, h, 0:T], in0=ps(bq)[:, 0:T], scalar=float(128 ** -0.5), in1=EB[:, h, 0:T],
                    op0=ALU.mult, op1=ALU.mult), reads=[psr(bq), ("EB", h)], writes=[("Q", h)])
                P.op("dve", lambda e, h=h, bk=bk, rot=rot: e.tensor_tensor(
                    out=Kt[:, h, 0:T], in0=ps(bk)[:, 0:T], in1=ENB[:, rot, 0:T], op=ALU.mult),
                    reads=[psr(bk), ("ENB", rot)], writes=[("Kt", h)])
                PS.free(bbc); PS.free(bq); PS.free(bk)
            RG.release(sq); RG.release(sk)
            for (blk, tb) in blks:
                b = PS.alloc()
                pb = ps(b)[:, :].bitcast(BF16)

                def fkt(e, pb=pb, blk=blk, tb=tb):
                    ins = None
                    for h in range(4):
                        ins = e.transpose(out=pb[0:tb, h * 128:(h + 1) * 128], in_=Kt[:, h, blk * 128:blk * 128 + tb],
                                          identity=identb[:, :])
                    return ins
                P.op("pe", fkt, reads=[("Kt", h) for h in range(4)] + ["identb"], writes=[psr(b)])
                P.op("dve", lambda e, pb=pb, blk=blk, tb=tb: e.tensor_copy(out=KTm[0:tb, blk, :], in_=pb[0:tb, 0:512]),
                     reads=[psr(b)], writes=[("KTm", blk)])
                PS.free(b)
            def chunk(blk, tb):
                t0 = blk * 128
                orot = blk % 2
                pat = PS.alloc()

                def fat(e):
                    ins = None
                    for h in range(4):
                        ins = e.matmul(ps(pat)[0:tb, h * 128:h * 128 + tb], Kt[:, h, t0:t0 + tb], Q[:, h, t0:t0 + tb],
                                       start=True, stop=True)
                    return ins
                P.op("pe", fat, reads=[("Kt", h) for h in range(4)] + [("Q", h) for h in range(4)], writes=[psr(pat)])
                for h in range(4):
                    P.op("dve", lambda e, h=h: e.tensor_tensor(out=AT[0:tb, h, 0:tb], in0=ps(pat)[0:tb, h * 128:h * 128 + tb],
                                                                in1=trib[0:tb, 0:tb], op=ALU.mult),
                         reads=[psr(pat), "trib"], writes=[("AT", h)])
                PS.free(pat)
                pkv = [PS.alloc(), PS.alloc()]
                for h in range(4):
                    P.op("pe", lambda e, h=h: e.matmul(ps(pkv[h // 2])[:, (h % 2) * 256:(h % 2) * 256 + 256],
                                                       KTm[0:tb, blk, h * 128:(h + 1) * 128],
                                                       V[0:tb, blk, h * 256:(h + 1) * 256], start=True, stop=True),
                         reads=[("KTm", blk), ("V", blk, h // 2)], writes=[psr(pkv[h // 2])])
                po = [PS.alloc(), PS.alloc()]
                for h in range(4):
                    pob = po[h // 2]
                    oc = (h % 2) * 256
                    P.op("pe", lambda e, h=h, pob=pob, oc=oc: e.matmul(
                        ps(pob)[0:tb, oc:oc + 256], AT[0:tb, h, 0:tb], V[0:tb, blk, h * 256:(h + 1) * 256],
                        start=True, stop=False), reads=[("AT", h), ("V", blk, h // 2)], writes=[psr(pob)])
                    P.op("pe", lambda e, h=h, pob=pob, oc=oc: e.matmul(
                        ps(pob)[0:tb, oc:oc + 256], Q[:, h, t0:t0 + tb], Sb[:, lg, h, :],
                        start=False, stop=True), reads=[("Q", h), ("Sb", lg, h)], writes=[psr(pob)])
                for h in range(4):
                    el = EB[:, h, t0 + tb - 1:t0 + tb]
                    kc = (h % 2) * 256
                    P.op("act", lambda e, h=h, el=el, kc=kc: e.activation(out=TMP[:, h, :], in_=ps(pkv[h // 2])[:, kc:kc + 256],
                                                                          func=AF.Copy, scale=el),
                         reads=[psr(pkv[h // 2]), ("EB", h)], writes=[("TMP", h)])
                    P.op("dve", lambda e, h=h, el=el: e.scalar_tensor_tensor(
                        out=S32[:, lg, h, :], in0=S32[:, lg, h, :], scalar=el, in1=TMP[:, h, :],
                        op0=ALU.mult, op1=ALU.add), reads=[("S32", lg, h), ("TMP", h), ("EB", h)],
                        writes=[("S32", lg, h)])
                    P.op("dve", lambda e, h=h: e.tensor_copy(out=Sb[:, lg, h, :], in_=S32[:, lg, h, :]),
                         reads=[("S32", lg, h)], writes=[("Sb", lg, h)])
                PS.free(pkv[0]); PS.free(pkv[1])
                rs_js = list(range(blk * KT // len(blks), (blk + 1) * KT // len(blks)))
                rs_banks = []

                def rs_evac(j, pr):
                    gc = lg * 2 + (j % 2)
                    P.op("act", lambda e: e.activation(out=RS[:, j, 0:T], in_=ps(pr)[:, 0:T], func=AF.Silu),
                         reads=[psr(pr)], writes=[("RS", j)])
                    PS.free(pr)
                    P.op("dve", lambda e: e.tensor_scalar(out=RS[:, j, 0:T], in0=RS[:, j, 0:T],
                                                          scalar1=GN[:, gc:gc + 1], scalar2=None, op0=ALU.mult),
                         reads=[("RS", j), "GN"], writes=[("RS", j)])

                for j in rs_js:
                    if j % 4 == 0:
                        srs[j // 4] = RG.get()
                    pr = PS.alloc()
                    proj_fm(pr, srs[j // 4], (j % 4) * 128, T)
                    if j % 4 == 3:
                        RG.release(srs[j // 4])
                    if len(rs_js) <= 2:
                        rs_banks.append((j, pr))
                    else:
                        rs_evac(j, pr)
                for h in range(4):
                    pob = po[h // 2]
                    oc = (h % 2) * 256
                    P.op("act", lambda e, h=h, pob=pob, oc=oc: e.activation(
                        out=JUNK[0:tb, :], in_=ps(pob)[0:tb, oc:oc + 256], func=AF.Square,
                        accum_out=SS[0:tb, blk * 4 + h:blk * 4 + h + 1]),
                        reads=[psr(pob)], writes=[("SS", blk), "JUNK"])
                P.op("act", lambda e: e.activation(out=RH[0:tb, blk * 4:blk * 4 + 4], in_=SS[0:tb, blk * 4:blk * 4 + 4],
                                                   func=AF.Sqrt, scale=1.0 / 256.0, bias=EPS_RMS[0:tb, 0:1]),
                     reads=[("SS", blk), "EPS"], writes=[("RH", blk)])
                P.op("dve", lambda e: e.reciprocal(out=RH[0:tb, blk * 4:blk * 4 + 4], in_=RH[0:tb, blk * 4:blk * 4 + 4]),
                     reads=[("RH", blk)], writes=[("RH", blk)])
                for h in range(4):
                    pob = po[h // 2]
                    oc = (h % 2) * 256
                    P.op("act", lambda e, h=h, pob=pob, oc=oc: e.activation(
                        out=ON[0:tb, orot, h * 256:(h + 1) * 256], in_=ps(pob)[0:tb, oc:oc + 256], func=AF.Copy,
                        scale=RH[0:tb, blk * 4 + h:blk * 4 + h + 1]),
                        reads=[psr(pob), ("RH", blk)], writes=[("ON", orot)])
                PS.free(po[0]); PS.free(po[1])
                for (j, pr) in rs_banks:
                    rs_evac(j, pr)
                pt_ = PS.alloc()
                ptb = ps(pt_)[:, :].bitcast(BF16).rearrange("p (j t) -> p j t", j=KT)

                def ftr(e):
                    ins = None
                    for j in range(KT):
                        ins = e.transpose(out=ptb[:, j, 0:tb], in_=ON[0:tb, orot, j * 128:(j + 1) * 128],
                                          identity=identb[0:tb, 0:tb])
                    return ins
                P.op("pe", ftr, reads=[("ON", orot), "identb"], writes=[psr(pt_)])
                P.op("dve", lambda e: e.tensor_copy(out=M[:, :, t0:t0 + tb], in_=ptb[:, :, 0:tb]),
                     reads=[psr(pt_)], writes=[("M", j) for j in range(KT)])
                PS.free(pt_)

            srs = [None, None]
            for (blk, tb) in blks:
                chunk(blk, tb)
            for j in range(KT):
                P.op("dve", lambda e, j=j: e.tensor_tensor(out=M[:, j, 0:T], in0=M[:, j, 0:T], in1=RS[:, j, 0:T],
                                                            op=ALU.mult),
                     reads=[("M", j), ("RS", j)], writes=[("M", j)])
            so = [RG.get(), RG.get()]
            out_proj_and_post_ln(l, T, so, None)

        ycnt = [0]
        load_x(tiles[0])
        for ti, tile in enumerate(tiles):
            s, t0, T, is_meta = tile
            if "notile" in DBG:
                break
            if "nometa" in DBG and is_meta:
                if ti + 1 < len(tiles):
                    load_x(tiles[ti + 1])
                continue
            transpose_in(T)
            if ti + 1 < len(tiles):
                load_x(tiles[ti + 1])
            for l in range(depth):
                P.barrier()
                run_deferred()
                if l % 2 == 0:
                    conv_layer(l, T, is_meta)
                else:
                    gla_layer(l, T, is_meta)
            P.barrier()
            run_deferred()
            if not is_meta and "noout" not in DBG:
                transpose_out(tile, ycnt)
        P.emit(None, sems, dma_sems, [("yout", 0), ("yout", 1)])
    return nc


_CACHE = {}


def _consts():
    ident = np.eye(128, dtype=np.float32)
    tri = np.triu(np.ones((128, 128), dtype=np.float32))
    return ident, tri


def run(inputs, nseq_per_core, ncores, depth=DEPTH_FULL, trace=False):
    x = np.ascontiguousarray(inputs["x"], dtype=np.float32)
    seq = x.shape[1]
    key = (nseq_per_core, seq, depth)
    if key not in _CACHE:
        _CACHE[key] = build(nseq_per_core, seq, depth)
    nc = _CACHE[key]
    ident, tri = _consts()
    in_maps = []
    for c in range(ncores):
        m = {k: np.ascontiguousarray(v, dtype=np.float32) for k, v in inputs.items() if k != "x"}
        m["x"] = x[c * nseq_per_core:(c + 1) * nseq_per_core]
        m["c_ident"] = ident
        m["c_tri"] = tri
        in_maps.append(m)
    res = run_bass_kernel_spmd(nc, in_maps, core_ids=list(range(ncores)), trace=trace)
    out = np.concatenate([r["y"] for r in res.results], axis=0)
    return out, res


def kernel(**inputs):
    out, _ = run(inputs, inputs["x"].shape[0] // NCORES, NCORES)
    return out
```
